# Optimizing a Trainium2 kernel written in Bass

```python
import jax, jax.numpy as jnp
from jax import lax
import numpy as np

D_MODEL = 1024
BATCH = 1
SEQ = 16384
DEPTH = 4
DEC_BATCH = 32
DEC_SEQ = 64
PAST_LEN = 4096

CHUNK = 64
N_PAIR = DEPTH // 2
POOL_GROUPS = 4
POOL_GC = 128
POOL_WIDTH = POOL_GROUPS * POOL_GC
POOL_WINDOWS = (2, 4, 8, 16)
POOL_HIST = max(POOL_WINDOWS) - 1
CCV_WIDTH = 512
CCV_K = 31
SGU_HEADS = 4
SGU_HC = 128
SGU_WIDTH = SGU_HEADS * SGU_HC
SGU_CHUNK = 128
SCONV_WIDTH = 512
SCONV_K = 3
D_FF = 2816
FFN_K = 3
EVEN_IN = POOL_WIDTH + 2 * CCV_WIDTH
ODD_IN = 2 * SGU_WIDTH + 3 * SCONV_WIDTH
MIX_OUT = 1024
EPS = 1e-6

kernel_name = 'hybrid_streaming_encoder_step'


def rmsnorm(x, g):
    xf = x.astype(jnp.float32)
    y = xf * lax.rsqrt(jnp.mean(xf * xf, axis=-1, keepdims=True) + EPS)
    return (y * g.astype(jnp.float32)).astype(x.dtype)


def layernorm(x, g, b):
    xf = x.astype(jnp.float32)
    mu = jnp.mean(xf, axis=-1, keepdims=True)
    xc = xf - mu
    var = jnp.mean(xc * xc, axis=-1, keepdims=True)
    y = xc * lax.rsqrt(var + EPS) * g.astype(jnp.float32) + b.astype(jnp.float32)
    return y.astype(x.dtype)


def causal_dwconv(x, hist, w):
    xx = jnp.concatenate([hist.astype(x.dtype), x], axis=1)
    y = lax.conv_general_dilated(xx, w.astype(x.dtype)[:, None, :], (1,), 'VALID',
                                 dimension_numbers=('NWC', 'WIO', 'NWC'),
                                 feature_group_count=x.shape[-1])
    return y, xx[:, xx.shape[1] - (w.shape[0] - 1):]


def multiscale_pool(z, hist, w_pool, scale, pos0):
    B, L, _ = z.shape
    zz = jnp.concatenate([hist.astype(z.dtype), z], axis=1)
    zf = zz.astype(jnp.float32)
    csum = jnp.cumsum(zf, axis=1)
    csum = jnp.concatenate([jnp.zeros_like(csum[:, :1]), csum], axis=1)
    pos = pos0 + jnp.arange(L)
    outs = []
    for g, w in enumerate(POOL_WINDOWS):
        sl = slice(g * POOL_GC, (g + 1) * POOL_GC)
        win_sum = csum[:, POOL_HIST + 1:, sl] - csum[:, POOL_HIST + 1 - w:POOL_HIST + 1 - w + L, sl]
        cnt = jnp.minimum(pos + 1, w).astype(jnp.float32)[None, :, None]
        outs.append(win_sum / cnt - zf[:, POOL_HIST:, sl])
    d = jnp.stack(outs, axis=2)
    y = jnp.einsum('blgc,gcd->blgd', d, w_pool.astype(jnp.float32)).reshape(B, L, POOL_WIDTH)
    y = y * scale.astype(jnp.float32)
    return y.astype(z.dtype), zz[:, L:]


def even_mixer(h, w_in, pool_w, pool_scale, ccv_w, ccv_b, ccv_ln_g, ccv_ln_b, w_out,
               hist_pool, hist_ccv, pos0):
    proj = h @ w_in
    za = proj[..., :POOL_WIDTH]
    a = proj[..., POOL_WIDTH:POOL_WIDTH + CCV_WIDTH]
    gt = proj[..., POOL_WIDTH + CCV_WIDTH:]
    zb = a * jax.nn.sigmoid(gt)
    ya, new_pool = multiscale_pool(za, hist_pool, pool_w, pool_scale, pos0)
    cb, new_ccv = causal_dwconv(zb, hist_ccv, ccv_w)
    yb = jax.nn.silu(layernorm(cb + ccv_b, ccv_ln_g, ccv_ln_b))
    return jnp.concatenate([ya, yb], axis=-1) @ w_out, new_pool, new_ccv


def odd_mixer(h, w_in, sgu_ln_g, sgu_ln_b, sgu_ws, sgu_b, sconv_w, w_out, hist_sconv):
    B, L, _ = h.shape
    proj = h @ w_in
    u = proj[..., :SGU_WIDTH]
    v = layernorm(proj[..., SGU_WIDTH:2 * SGU_WIDTH], sgu_ln_g, sgu_ln_b)
    o = 2 * SGU_WIDTH
    bg = proj[..., o:o + SCONV_WIDTH]
    cg = proj[..., o + SCONV_WIDTH:o + 2 * SCONV_WIDTH]
    xin = proj[..., o + 2 * SCONV_WIDTH:]
    n = -(-L // SGU_CHUNK)
    vp = jnp.pad(v, ((0, 0), (0, n * SGU_CHUNK - L), (0, 0)))
    vp = vp.reshape(B, n, SGU_CHUNK, SGU_HEADS, SGU_HC)
    mask = jnp.tril(jnp.ones((SGU_CHUNK, SGU_CHUNK), dtype=bool))
    ws = jnp.where(mask[None], sgu_ws, jnp.zeros_like(sgu_ws))
    mixed = jnp.einsum('hts,bnshc->bnthc', ws, vp) + sgu_b.T[:, :, None]
    mixed = mixed.reshape(B, n * SGU_CHUNK, SGU_WIDTH)[:, :L]
    yc = u * mixed
    cz, new_sconv = causal_dwconv(cg * xin, hist_sconv, sconv_w)
    yd = bg * cz
    return jnp.concatenate([yc, yd], axis=-1) @ w_out, v, new_sconv


def conv_ffn(h, w_up, conv_w, conv_b, w_down, hist):
    up = h @ w_up
    c, new_hist = causal_dwconv(up, hist, conv_w)
    c = c + conv_b
    return (jax.nn.silu(c[..., :D_FF]) * c[..., D_FF:]) @ w_down, new_hist


def trunk(x, st_pool, st_ccv, st_sconv, st_ffn, pos0,
          norm_mix_g, norm_ffn_g, norm_final_g,
          w_in_even, pool_w, pool_scale, ccv_w, ccv_b, ccv_ln_g, ccv_ln_b, w_out_even,
          w_in_odd, sgu_ln_g, sgu_ln_b, sgu_ws, sgu_b, sconv_w, w_out_odd,
          ffn_w_up, ffn_conv_w, ffn_conv_b, ffn_w_down):
    new_pool, new_ccv, new_sconv, new_v, new_ffn = [], [], [], [], []
    for i in range(DEPTH):
        j = i // 2
        h = rmsnorm(x, norm_mix_g[i])
        if i % 2 == 0:
            y, npool, nccv = even_mixer(h, w_in_even[j], pool_w[j], pool_scale[j], ccv_w[j],
                                        ccv_b[j], ccv_ln_g[j], ccv_ln_b[j], w_out_even[j],
                                        st_pool[j], st_ccv[j], pos0)
            new_pool.append(npool)
            new_ccv.append(nccv)
        else:
            y, v, nsc = odd_mixer(h, w_in_odd[j], sgu_ln_g[j], sgu_ln_b[j], sgu_ws[j], sgu_b[j],
                                  sconv_w[j], w_out_odd[j], st_sconv[j])
            new_v.append(v)
            new_sconv.append(nsc)
        x = x + y
        f, nf = conv_ffn(rmsnorm(x, norm_ffn_g[i]), ffn_w_up[i], ffn_conv_w[i], ffn_conv_b[i],
                         ffn_w_down[i], st_ffn[i])
        new_ffn.append(nf)
        x = x + f
    return (rmsnorm(x, norm_final_g), jnp.stack(new_pool), jnp.stack(new_ccv),
            jnp.stack(new_sconv), jnp.stack(new_v), jnp.stack(new_ffn))


def setup_inputs(seed: int = 0) -> dict:
    key = jax.random.key(seed)
    ks = jax.random.split(key, 32)
    f32 = jnp.float32
    nrm = lambda k, s: jax.random.normal(k, s, f32)
    return {
        'x_prompt': nrm(ks[0], (BATCH, SEQ, D_MODEL)),
        'x_sample': nrm(ks[1], (DEC_BATCH, DEC_SEQ, D_MODEL)),
        'state_pool': nrm(ks[2], (N_PAIR, DEC_BATCH, POOL_HIST, POOL_WIDTH)),
        'state_ccv': 0.5 * nrm(ks[3], (N_PAIR, DEC_BATCH, CCV_K - 1, CCV_WIDTH)),
        'state_sconv': 0.5 * nrm(ks[4], (N_PAIR, DEC_BATCH, SCONV_K - 1, SCONV_WIDTH)),
        'state_ffn_conv': nrm(ks[5], (DEPTH, DEC_BATCH, FFN_K - 1, 2 * D_FF)),
        'norm_mix_g': 1.0 + 0.05 * nrm(ks[6], (DEPTH, D_MODEL)),
        'norm_ffn_g': 1.0 + 0.05 * nrm(ks[7], (DEPTH, D_MODEL)),
        'norm_final_g': 1.0 + 0.05 * nrm(ks[8], (D_MODEL,)),
        'w_in_even': nrm(ks[9], (N_PAIR, D_MODEL, EVEN_IN)) * D_MODEL ** -0.5,
        'pool_w': nrm(ks[10], (N_PAIR, POOL_GROUPS, POOL_GC, POOL_GC)) * POOL_GC ** -0.5,
        'pool_scale': 0.5 * (1.0 + 0.1 * nrm(ks[11], (N_PAIR, POOL_WIDTH))),
        'ccv_w': nrm(ks[12], (N_PAIR, CCV_K, CCV_WIDTH)) * CCV_K ** -0.5,
        'ccv_b': 0.02 * nrm(ks[13], (N_PAIR, CCV_WIDTH)),
        'ccv_ln_g': 1.0 + 0.05 * nrm(ks[14], (N_PAIR, CCV_WIDTH)),
        'ccv_ln_b': 0.02 * nrm(ks[15], (N_PAIR, CCV_WIDTH)),
        'w_out_even': nrm(ks[16], (N_PAIR, MIX_OUT, D_MODEL)) * (0.5 * MIX_OUT ** -0.5),
        'w_in_odd': nrm(ks[17], (N_PAIR, D_MODEL, ODD_IN)) * D_MODEL ** -0.5,
        'sgu_ln_g': 1.0 + 0.05 * nrm(ks[18], (N_PAIR, SGU_WIDTH)),
        'sgu_ln_b': 0.02 * nrm(ks[19], (N_PAIR, SGU_WIDTH)),
        'sgu_ws': nrm(ks[20], (N_PAIR, SGU_HEADS, SGU_CHUNK, SGU_CHUNK)) * SGU_CHUNK ** -0.5,
        'sgu_b': 1.0 + 0.1 * nrm(ks[21], (N_PAIR, SGU_HEADS, SGU_CHUNK)),
        'sconv_w': nrm(ks[22], (N_PAIR, SCONV_K, SCONV_WIDTH)) * SCONV_K ** -0.5,
        'w_out_odd': nrm(ks[23], (N_PAIR, MIX_OUT, D_MODEL)) * (0.5 * MIX_OUT ** -0.5),
        'ffn_w_up': nrm(ks[24], (DEPTH, D_MODEL, 2 * D_FF)) * D_MODEL ** -0.5,
        'ffn_conv_w': nrm(ks[25], (DEPTH, FFN_K, 2 * D_FF)) * FFN_K ** -0.5,
        'ffn_conv_b': 0.02 * nrm(ks[26], (DEPTH, 2 * D_FF)),
        'ffn_w_down': nrm(ks[27], (DEPTH, D_FF, D_MODEL)) * (0.5 * D_FF ** -0.5),
    }


def reference(x_prompt, x_sample, state_pool, state_ccv, state_sconv, state_ffn_conv,
              norm_mix_g, norm_ffn_g, norm_final_g,
              w_in_even, pool_w, pool_scale, ccv_w, ccv_b, ccv_ln_g, ccv_ln_b, w_out_even,
              w_in_odd, sgu_ln_g, sgu_ln_b, sgu_ws, sgu_b, sconv_w, w_out_odd,
              ffn_w_up, ffn_conv_w, ffn_conv_b, ffn_w_down):
    weights = (norm_mix_g, norm_ffn_g, norm_final_g,
               w_in_even, pool_w, pool_scale, ccv_w, ccv_b, ccv_ln_g, ccv_ln_b, w_out_even,
               w_in_odd, sgu_ln_g, sgu_ln_b, sgu_ws, sgu_b, sconv_w, w_out_odd,
               ffn_w_up, ffn_conv_w, ffn_conv_b, ffn_w_down)
    B = x_prompt.shape[0]
    dt = x_prompt.dtype
    z_pool = jnp.zeros((N_PAIR, B, POOL_HIST, POOL_WIDTH), dt)
    z_ccv = jnp.zeros((N_PAIR, B, CCV_K - 1, CCV_WIDTH), dt)
    z_sconv = jnp.zeros((N_PAIR, B, SCONV_K - 1, SCONV_WIDTH), dt)
    z_ffn = jnp.zeros((DEPTH, B, FFN_K - 1, 2 * D_FF), dt)
    y_prompt, pool_p, ccv_p, sconv_p, _, ffn_p = trunk(
        x_prompt, z_pool, z_ccv, z_sconv, z_ffn, 0, *weights)
    y_sample, pool_s, ccv_s, sconv_s, v_s, ffn_s = trunk(
        x_sample, state_pool, state_ccv, state_sconv, state_ffn_conv, PAST_LEN, *weights)
    return (y_prompt, y_sample, pool_p, pool_s, ccv_p, ccv_s, sconv_p, sconv_s, v_s, ffn_p, ffn_s)
```

```python
import numpy as np
import concourse.bass as bass
import concourse.mybir as mybir
from concourse.bass_utils import run_bass_kernel_spmd

F32 = mybir.dt.float32
BF16 = mybir.dt.bfloat16
ALU = mybir.AluOpType
AF = mybir.ActivationFunctionType

NCORES = 8
D = 1024
SEQ = 16384
OWN = SEQ // NCORES
HALO = 288
NSS = 4
LS = 64
TOK = HALO + OWN + NSS * LS
NOUT = OWN + NSS * LS
DFF = 2816
NU = 11
EPS = 1e-6
NSLOT = 6

CST = {}
_off = 0
for _n, _sz in [("gmix", 32), ("gffn", 32), ("gfin", 8), ("pscale", 8), ("ccvw", 2 * 4 * 31),
                ("ccvb", 8), ("ccvg", 8), ("ccvbt", 8), ("scw", 2 * 4 * 3), ("fcw", 4 * 44 * 3),
                ("fcb", 4 * 44)]:
    CST[_n] = (_off, _sz)
    _off += _sz
CSTN = _off


def _unit_perm():
    perm = []
    for i in range(NU):
        perm += [2 * i, 2 * i + 1, 22 + 2 * i, 22 + 2 * i + 1]
    return np.array(perm)


class Plan:
    ENG = ("pe", "act", "dve", "pool", "sp")

    def __init__(self):
        self.ops = {e: [] for e in self.ENG}
        self.lastw = {}
        self.readers = {}
        self.dmasem_count = {}

    def op(self, eng, fn, reads=(), writes=(), dma_sem=None):
        idx = len(self.ops[eng])
        deps = set()
        for r in reads:
            if r in self.lastw:
                deps.add(self.lastw[r])
        for w in writes:
            if w in self.lastw:
                deps.add(self.lastw[w])
            for ev in self.readers.get(w, ()):
                deps.add(ev)
        if dma_sem is not None:
            self.dmasem_count[dma_sem] = self.dmasem_count.get(dma_sem, 0) + 16
            ev = ("sem", dma_sem, self.dmasem_count[dma_sem])
        else:
            ev = ("op", eng, idx)
        fdeps = set()
        for d in deps:
            if d[0] == "op" and d[1] == eng:
                if eng == "pe":
                    continue
                if idx - d[2] >= 3:
                    continue
            fdeps.add(d)
        self.ops[eng].append(dict(fn=fn, deps=fdeps, flag=False, dma=dma_sem is not None))
        for r in reads:
            self.readers.setdefault(r, []).append(ev)
        for w in writes:
            self.lastw[w] = ev
            self.readers[w] = []
        return ev

    def resolve(self):
        for e in self.ENG:
            for o in self.ops[e]:
                for d in o["deps"]:
                    if d[0] == "op":
                        self.ops[d[1]][d[2]]["flag"] = True
        self.count = {}
        for e in self.ENG:
            c = 0
            lst = []
            for o in self.ops[e]:
                if o["flag"]:
                    c += 1
                lst.append(c)
            self.count[e] = lst

    def emit(self, eng, e, engsem, dmasems):
        known = {}
        for o in self.ops[eng]:
            need = {}
            for d in o["deps"]:
                if d[0] == "op":
                    k = ("e", d[1])
                    v = self.count[d[1]][d[2]]
                else:
                    k = ("d", d[1])
                    v = d[2]
                if need.get(k, 0) < v:
                    need[k] = v
            for k, v in need.items():
                if known.get(k, 0) < v:
                    sem = engsem[k[1]] if k[0] == "e" else dmasems[k[1]]
                    e.wait_ge(sem, v)
                    known[k] = v
            ins = o["fn"](e)
            if o["flag"]:
                ins.then_inc(engsem[eng], 1)


def build_program(dbg_tiles=None, dbg_layers=4, dbg_stage=99, record=False, pieces=None):
    if not record and pieces is None:
        pieces = build_program(dbg_tiles, dbg_layers, dbg_stage, record=True, pieces=[])
    nc = bass.Bass("TRN2", target_bir_lowering=False)
    P = Plan()

    def din(name, shape):
        return nc.dram_tensor(name, list(shape), F32, kind="ExternalInput").ap()

    def dout(name, shape):
        return nc.dram_tensor(name, list(shape), F32, kind="ExternalOutput").ap()

    xT = din("xT", [128, 8, TOK])
    cst_d = din("cst", [128, CSTN])
    mask_d = din("mask", [HALO])
    invc_d = din("invc", [4 * 16])
    sgug_d = din("sgug", [2 * 512])
    sgub_d = din("sgubt", [2 * 512])
    sgubias_d = din("sgubias", [2 * 4 * 128])
    wsT_d = din("wsT", [128, 2 * 4 * 128])
    wsTs_d = din("wsTs", [128, 2 * 4 * 64])
    tril_d = din("tril", [128, 128])
    trils_d = din("trils", [128, 64])
    poolw_d = din("poolw", [128, 2 * 4 * 128])
    ident_d = din("ident", [128, 128])
    stpool_d = din("st_pool", [128, 2 * 4 * NSS * 15])
    stccv_d = din("st_ccv", [128, 2 * 4 * NSS * 30])
    stsc_d = din("st_sc", [128, 2 * 4 * NSS * 2])
    stffn_d = din("st_ffn", [128, 4 * 44 * NSS * 2])
    w_in_even = din("w_in_even", [2, D, 1536])
    w_out_even = din("w_out_even", [2, D, D])
    w_in_odd = din("w_in_odd", [2, D, 2560])
    w_out_odd = din("w_out_odd", [2, D, D])
    ffn_w_up = din("ffn_w_up", [4, D, 2 * DFF])
    ffn_w_down = din("ffn_w_down", [4, DFF, D])

    yT = dout("yT", [128, 8, NOUT])
    o_pool = dout("o_pool", [128, 2 * 4 * 5 * 15])
    o_ccv = dout("o_ccv", [128, 2 * 4 * 5 * 30])
    o_sc = dout("o_sc", [128, 2 * 4 * 5 * 2])
    o_ffn = dout("o_ffn", [128, 4 * 44 * 5 * 2])
    o_v = dout("o_v", [2, NSS, LS, 512])

    from contextlib import ExitStack
    es = ExitStack()

    def sb(name, shape, dt=F32):
        return es.enter_context(nc.sbuf_tensor("sb_" + name, list(shape), dt))

    xb = sb("xb", [128, 8, 512])
    hb = sb("hb", [128, 8, 512], BF16)
    yb = sb("yb", [128, 8, 512], BF16)
    rstd = sb("rstd", [128, 512])
    sd = sb("sd", [128, 512])
    cst = sb("cst", [128, CSTN])
    maskb = sb("maskb", [128, HALO])
    invc = sb("invc", [128, 4, 16])
    sgug = sb("sgug", [128, 2, 512])
    sgubt = sb("sgubt", [128, 2, 512])
    sgubias = sb("sgubias", [128, 8, 128])
    wsT = sb("wsT", [128, 8, 128], BF16)
    wsTs = sb("wsTs", [128, 8, 64], BF16)
    poolw = sb("poolw", [128, 8, 128], BF16)
    ones = sb("ones", [128, 128], BF16)
    dmy = sb("dmy", [128, 8])
    ident = sb("ident", [128, 128])
    stpool = sb("stpool", [128, 2, 4, NSS, 15])
    stccv = sb("stccv", [128, 2, 4, NSS, 30])
    stsc = sb("stsc", [128, 2, 4, NSS, 2])
    stffn = sb("stffn", [128, 4, 44, NSS, 2])
    npool = sb("npool", [128, 2, 4, 5, 15])
    nccv = sb("nccv", [128, 2, 4, 5, 30])
    nsc = sb("nsc", [128, 2, 4, 5, 2])
    nffn = sb("nffn", [128, 4, 44, 5, 2])
    ring = [sb(f"ring{i}", [128, 8, 512], BF16) for i in range(NSLOT)]
    ARENA_F = 18240
    arena = sb("arena", [128, ARENA_F])
    ps = [es.enter_context(nc.psum_tensor(f"ps{i}", [128, 512], F32)) for i in range(8)]

    import os
    if os.environ.get("SBUF_FREE"):
        print("SBUF free bytes/partition:", nc.sbuf_bytes_remaining)
    engsem = {e: es.enter_context(nc.semaphore(f"s_{e}")) for e in Plan.ENG}
    dmasems = {}

    def dsem(name):
        if name not in dmasems:
            dmasems[name] = es.enter_context(nc.semaphore(f"d_{name}"))
        return name

    def carve(off, words):
        return arena[:, off:off + words]

    A_ZAB = carve(0, 4 * 527)
    A_ZBB = carve(2108, 4 * 542)
    A_T1 = carve(4276, 527)
    A_T2 = carve(4803, 527)
    A_D = carve(5330, 1024).bitcast(BF16)
    A_SIG = carve(6354, 1024)
    A_CB = carve(7378, 2048)
    A_MEAN = carve(9426, 512)
    A_MSQ = carve(9938, 512)
    A_LRS = carve(10450, 512)
    A_TT = carve(10962, 1024)
    A_ZBH = carve(13010, 1084).bitcast(BF16)
    A_DG = carve(14094, 4096).bitcast(BF16)
    O_U = carve(0, 2048)
    O_VN = carve(2048, 1024).bitcast(BF16)
    O_V32 = carve(3072, 2048)
    O_VT = carve(5120, 1024)
    O_CX = carve(6144, 4 * 514)
    O_XIN = carve(8200, 1024)
    O_BG = carve(9224, 1024)
    O_CZ = carve(10248, 1024)
    O_MT = carve(11272, 1024)
    O_BN = carve(12296, 64)
    F_UP = carve(0, 2 * 4 * 514)
    F_C = carve(4112, 4 * 512)
    F_S = carve(6160, 1024)
    F_G = carve(7184, 5632).bitcast(BF16)
    F_PT = carve(12816, 1024)
    SQB = carve(0, 2048).bitcast(BF16)
    A_LNB = carve(10962 + 1024, 1070)
    YOUT = carve(7184, 4096)

    E_CBB = carve(11986, 1024).bitcast(BF16)
    assert 11986 + 1024 <= ARENA_F
    E_SQB = A_TT.bitcast(BF16)

    cview = lambda name: cst[:, CST[name][0]:CST[name][0] + CST[name][1]]
    gmix = cview("gmix").rearrange("p (l c) -> p l c", l=4)
    gffn = cview("gffn").rearrange("p (l c) -> p l c", l=4)
    gfin = cview("gfin")
    pscale = cview("pscale").rearrange("p (a c) -> p a c", a=2)
    ccvw = cview("ccvw").rearrange("p (a c k) -> p a c k", a=2, c=4)
    ccvb = cview("ccvb").rearrange("p (a c) -> p a c", a=2)
    ccvg = cview("ccvg").rearrange("p (a c) -> p a c", a=2)
    ccvbt = cview("ccvbt").rearrange("p (a c) -> p a c", a=2)
    scw = cview("scw").rearrange("p (a c k) -> p a c k", a=2, c=4)
    fcw = cview("fcw").rearrange("p (l c k) -> p l c k", l=4, c=44)
    fcb = cview("fcb").rearrange("p (l c) -> p l c", l=4)

    bank_ctr = [0]
    dg_ctr = [0]
    pt_ctr = [0]

    reserved_banks = set()

    def next_bank():
        while True:
            b = bank_ctr[0] % 8
            bank_ctr[0] += 1
            if b not in reserved_banks:
                return b

    WTS = dict(w_in_even=w_in_even, w_out_even=w_out_even, w_in_odd=w_in_odd, w_out_odd=w_out_odd,
               ffn_w_up=ffn_w_up, ffn_w_down=ffn_w_down)
    piece_ctr = [0]
    emitted = [0]
    KPRE = NSLOT - 3

    def emit_load(j):
        s = j % NSLOT
        sem = dsem(f"w{s}")
        for i, (wn, idx, r0, nrows, c0, ncol, k0, co) in enumerate(pieces[j]):
            src = WTS[wn][idx][r0:r0 + nrows, c0:c0 + ncol].rearrange("(k p) n -> p k n", p=128)
            nk = nrows // 128

            def fn(e, src=src, s=s, k0=k0, nk=nk, co=co, ncol=ncol, sem=sem):
                e.dma_start(out=ring[s][:, k0:k0 + nk, co:co + ncol], in_=src).then_inc(dmasems[sem], 16)
                return None
            P.op("pool", fn, writes=[("slot", s)] if i == 0 else [], dma_sem=sem)
            if i > 0:
                P.lastw[("slot", s)] = ("sem", sem, P.dmasem_count[sem])

    def load_piece(descs):
        i = piece_ctr[0]
        piece_ctr[0] += 1
        if record:
            pieces.append(list(descs))
            return i % NSLOT
        assert pieces[i] == list(descs)
        while emitted[0] <= min(i + KPRE, len(pieces) - 1):
            emit_load(emitted[0])
            emitted[0] += 1
        return i % NSLOT

    def wcols(w2d, c0, ncol):
        return w2d.rearrange("(k p) n -> p k n", p=128)[:, :, c0:c0 + ncol]

    def mm_group(lhs, rhs, K, M, N, reads, bank=None, start=True, stop=True):
        b = next_bank() if bank is None else bank

        def fn(e):
            ins = None
            for k in range(K):
                ins = e.matmul(ps[b][:M, :N], lhs(k), rhs(k), start=(start and k == 0), stop=(stop and k == K - 1))
            return ins
        P.op("pe", fn, reads=reads, writes=[("ps", b)])
        return b

    def ld(dst, src, res):
        sem = dsem("c_" + res)

        def fn(e):
            e.dma_start(out=dst, in_=src).then_inc(dmasems[sem], 16)
            return None
        P.op("sp", fn, writes=[res], dma_sem=sem)

    ld(cst[:], cst_d, "cst")
    ld(ident[:], ident_d, "ident")
    ld(maskb[:], mask_d.partition_broadcast(128), "maskb")
    ld(invc[:], invc_d.partition_broadcast(128).rearrange("p (g k) -> p g k", g=4), "invc")
    ld(sgug[:], sgug_d.partition_broadcast(128).rearrange("p (a n) -> p a n", a=2), "sgug")
    ld(sgubt[:], sgub_d.partition_broadcast(128).rearrange("p (a n) -> p a n", a=2), "sgubt")
    ld(sgubias[:], sgubias_d.partition_broadcast(128).rearrange("p (a n) -> p a n", a=8), "sgubias")
    ld(stpool[:], stpool_d.rearrange("p (a c s k) -> p a c s k", a=2, c=4, s=NSS), "stpool")
    ld(stccv[:], stccv_d.rearrange("p (a c s k) -> p a c s k", a=2, c=4, s=NSS), "stccv")
    ld(stsc[:], stsc_d.rearrange("p (a c s k) -> p a c s k", a=2, c=4, s=NSS), "stsc")
    ld(stffn[:], stffn_d.rearrange("p (a c s k) -> p a c s k", a=4, c=44, s=NSS), "stffn")
    T_WS = carve(0, 1024).rearrange("p (a n) -> p a n", a=8)
    T_WSS = carve(1024, 512).rearrange("p (a n) -> p a n", a=8)
    T_TR = carve(1536, 128)
    T_TRS = carve(1664, 64)
    T_PW = carve(1728, 1024).rearrange("p (a n) -> p a n", a=8)
    ld(T_WS, wsT_d.rearrange("p (a n) -> p a n", a=8), "t_ws")
    ld(T_WSS, wsTs_d.rearrange("p (a n) -> p a n", a=8), "t_wss")
    ld(T_TR, tril_d, "t_tr")
    ld(T_TRS, trils_d, "t_trs")
    ld(T_PW, poolw_d.rearrange("p (a n) -> p a n", a=8), "t_pw")

    P.op("dve", lambda e: e.tensor_tensor(out=wsT[:], in0=T_WS, in1=T_TR.unsqueeze(1).broadcast_to([128, 8, 128]), op=ALU.mult),
         reads=["t_ws", "t_tr"], writes=["wsT", "setup"])
    P.op("dve", lambda e: e.tensor_tensor(out=wsTs[:], in0=T_WSS, in1=T_TRS.unsqueeze(1).broadcast_to([128, 8, 64]), op=ALU.mult),
         reads=["t_wss", "t_trs"], writes=["wsTs", "setup2"])
    P.op("act", lambda e: e.activation(out=poolw[:], in_=T_PW, func=AF.Identity), reads=["t_pw"], writes=["poolw", "setup3"])
    P.op("dve", lambda e: e.memset(ones[:], 1.0), writes=["ones"])
    P.op("dve", lambda e: e.memset(dmy[:], 1.0), writes=["dmy"])
    P.op("dve", lambda e: e.memset(npool[:], 0.0), writes=["npool"])
    P.op("dve", lambda e: e.memset(nccv[:], 0.0), writes=["nccv"])
    P.op("dve", lambda e: e.memset(nsc[:], 0.0), writes=["nsc"])
    P.op("dve", lambda e: e.memset(nffn[:], 0.0), writes=["nffn"])

    SETUP = ["setup", "setup2", "setup3"]

    class T:
        pass

    tiles = []
    t0 = T(); t0.col0 = 0; t0.N = HALO; t0.S = 1; t0.L = HALO; t0.kind = "halo"
    t0.vblocks = [(0, 32), (32, 128), (160, 128)]
    tiles.append(t0)
    for i in range(4):
        t = T(); t.col0 = HALO + 512 * i; t.N = 512; t.S = 1; t.L = 512; t.kind = "own"; t.outcol = 512 * i
        t.vblocks = [(128 * b, 128) for b in range(4)]
        t.first = (i == 0)
        tiles.append(t)
    ts_ = T(); ts_.col0 = HALO + OWN; ts_.N = NSS * LS; ts_.S = NSS; ts_.L = LS; ts_.kind = "sample"; ts_.outcol = OWN
    ts_.vblocks = [(64 * q, 64) for q in range(NSS)]
    tiles.append(ts_)

    def v4(flat, C, S, W):
        return flat[:, 0:C * S * W].rearrange("p (c s w) -> p c s w", c=C, s=S)

    def pv(b, t):
        return ps[b][:, 0:t.N].rearrange("p (s l) -> p s l", s=t.S)

    def v3(flat2d, t):
        return flat2d.rearrange("p (s l) -> p s l", s=t.S)

    def seqsl(t):
        return slice(0, 1) if t.kind != "sample" else slice(1, 5)

    def rmsnorm(t, gain, final=False, extra_reads=(), h_extra=()):
        N = t.N
        sq = SQB.rearrange("p (c n) -> p c n", c=8)
        for c in range(8):
            P.op("act", lambda e, c=c: e.activation(out=sq[:, c, :N], in_=xb[:, c, :N], func=AF.Square),
                 reads=[("x", c)] + list(extra_reads), writes=[("sq", c)])
        import os
        NS = int(os.environ.get("DBG_NORM", "9"))
        if NS < 1:
            return
        b = next_bank()
        for k in range(8):
            P.op("pe", lambda e, k=k, b=b: e.matmul(ps[b][:, :N], ones[:, :], sq[:, k, :N], start=(k == 0), stop=(k == 7)),
                 reads=[("sq", k), "ones"], writes=[("ps", b)] if k == 0 else [("psacc_n", 0)])
        P.lastw[("ps", b)] = ("op", "pe", len(P.ops["pe"]) - 1)
        if NS < 2:
            return
        P.op("act", lambda e: e.activation(out=sd[:, :N], in_=ps[b][:, :N], func=AF.Ln, bias=EPS, scale=1.0 / D),
             reads=[("ps", b)], writes=["sd"])
        if NS < 3:
            return
        P.op("act", lambda e: e.activation(out=rstd[:, :N], in_=sd[:, :N], func=AF.Exp, scale=-0.5), reads=["sd"], writes=["rstd"])
        if NS < 4:
            return
        if t.kind == "halo":
            P.op("dve", lambda e: e.tensor_tensor(out=rstd[:, :N], in0=rstd[:, :N], in1=maskb[:, :N], op=ALU.mult),
                 reads=["rstd", "maskb"], writes=["rstd"])
        if not final:
            for c in range(8):
                P.op("dve", lambda e, c=c: e.scalar_tensor_tensor(out=hb[:, c, :N], in0=xb[:, c, :N], scalar=gain(c),
                                                                in1=rstd[:, :N], op0=ALU.mult, op1=ALU.mult),
                     reads=[("x", c), "rstd", "cst"] + list(h_extra), writes=[("h", c)])
        else:
            yo = YOUT.rearrange("p (c n) -> p c n", c=8)
            for c in range(8):
                P.op("dve", lambda e, c=c: e.scalar_tensor_tensor(out=yo[:, c, :N], in0=xb[:, c, :N], scalar=gain(c),
                                                                in1=rstd[:, :N], op0=ALU.mult, op1=ALU.mult),
                     reads=[("x", c), "rstd", "cst"], writes=[("yout", c)])

    HALL = [("h", c) for c in range(8)]

    def proj_group(slot, m, t, K=8, src=None, srcres=None):
        src = hb if src is None else src
        N = t.N
        return mm_group(lambda k: ring[slot][:, k, m * 128:(m + 1) * 128], lambda k: src[:, k, :N], K, 128, N,
                        reads=(HALL if srcres is None else srcres) + [("slot", slot)])

    dgbuf_all = A_DG.rearrange("p (i n) -> p i n", i=64)
    prebuilt = [None]

    pending_builds = []

    def prebuild_start(prn):
        pending_builds[:] = [(prn, j, k) for j in (0, 1) for k in range(31)]

    def prebuild_some(n):
        for _ in range(n):
            if not pending_builds:
                return
            prn, j, k = pending_builds.pop(0)
            di = j * 32 + k
            P.op("act", lambda e, prn=prn, j=j, k=k, di=di: e.activation(out=dgbuf_all[:, di, :], in_=ident[:], func=AF.Identity,
                                                                       scale=ccvw[:, prn, j, k:k + 1]),
                 reads=["cst", "ident"], writes=[("dg", j, k)])
            if not pending_builds:
                prebuilt[0] = prn

    def preload_sqrt():
        P.op("act", lambda e: e.activation(out=dmy[:, 0:1], in_=dmy[:, 1:2], func=AF.Ln), reads=["dmy"], writes=["dmy2"])

    def proj_kouter(slot, ms, t):
        N = t.N
        banks = [next_bank() for _ in ms]
        for k in range(8):
            def fn(e, k=k):
                ins = None
                for bi_, m in enumerate(ms):
                    ins = e.matmul(ps[banks[bi_]][:, :N], ring[slot][:, k, m * 128:(m + 1) * 128], hb[:, k, :N],
                                   start=(k == 0), stop=(k == 7))
                return ins
            P.op("pe", fn, reads=[("h", k), ("slot", slot)], writes=[("ps", b) for b in banks] if k == 0 else [("psacc_k", slot)])
        for b in banks:
            P.lastw[("ps", b)] = ("op", "pe", len(P.ops["pe"]) - 1)
        return banks

    def out_proj(t, wname, widx, korder=(0, 1, 2, 3, 4, 5, 6, 7)):
        N = t.N
        for half in range(2):
            s = load_piece([(wname, widx, 0, 1024, half * 512, 512, 0, 0)])
            banks = [next_bank() for _ in range(4)]
            if half == 0:
                for ki, k in enumerate(korder):
                    def fn(e, k=k, ki=ki, s=s, banks=banks):
                        ins = None
                        for mm in range(4):
                            ins = e.matmul(ps[banks[mm]][:, :N], ring[s][:, k, mm * 128:(mm + 1) * 128], yb[:, k, :N],
                                           start=(ki == 0), stop=(ki == 7))
                        return ins
                    P.op("pe", fn, reads=[("y", k), ("slot", s)], writes=[("ps", b) for b in banks] if ki == 0 else [("psacc_o", half)])
                for b in banks:
                    P.lastw[("ps", b)] = ("op", "pe", len(P.ops["pe"]) - 1)
            else:
                for mm in range(4):
                    def fn(e, mm=mm, s=s, b=banks[mm]):
                        ins = None
                        for k in range(8):
                            ins = e.matmul(ps[b][:, :N], ring[s][:, k, mm * 128:(mm + 1) * 128], yb[:, k, :N],
                                           start=(k == 0), stop=(k == 7))
                        return ins
                    P.op("pe", fn, reads=[("y", k) for k in range(8)] + [("slot", s)], writes=[("ps", banks[mm])])
            for mm in range(4):
                m = half * 4 + mm
                b = banks[mm]
                P.op("dve", lambda e, m=m, b=b: e.tensor_tensor(out=xb[:, m, :N], in0=ps[b][:, :N], in1=xb[:, m, :N], op=ALU.add),
                     reads=[("ps", b), ("x", m)], writes=[("x", m)])

    def even_mixer(t, l):
        pr = l // 2
        N, S, L = t.N, t.S, t.L
        zab = v4(A_ZAB, 4, S, 15 + L)
        zbb = v4(A_ZBB, 4, S, 30 + L)
        win = w_in_even[pr]
        if t.kind == "sample":
            P.op("act", lambda e: e.activation(out=zab[:, :, :, 0:15], in_=stpool[:, pr], func=AF.Identity),
                 reads=["rstd", "stpool"], writes=[("zab", m) for m in range(4)])
            P.op("act", lambda e: e.activation(out=zbb[:, :, :, 0:30], in_=stccv[:, pr], func=AF.Identity),
                 reads=["rstd", "stccv"], writes=[("zbb", m) for m in range(4)])
        else:
            P.op("act", lambda e: e.activation(out=zab[:, :, :, 0:15], in_=npool[:, pr, :, 0:1, :], func=AF.Identity),
                 reads=["rstd", "npool"], writes=[("zab", m) for m in range(4)])
            P.op("act", lambda e: e.activation(out=zbb[:, :, :, 0:30], in_=nccv[:, pr, :, 0:1, :], func=AF.Identity),
                 reads=["rstd", "nccv"], writes=[("zbb", m) for m in range(4)])
        dg = A_DG.rearrange("p (i n) -> p i n", i=64)
        zbh = v4(A_ZBH, 4, S, 30 + L)

        def build_diags(j, ks):
            for k in ks:
                di = (j % 2) * 32 + k
                if k % 2 == 0:
                    P.op("act", lambda e, j=j, k=k, di=di: e.activation(out=dg[:, di, :], in_=ident[:], func=AF.Identity,
                                                                       scale=ccvw[:, pr, j, k:k + 1]),
                         reads=["cst", "ident", "rstd"], writes=[("dg", j % 2, k)])
                else:
                    P.op("dve", lambda e, j=j, k=k, di=di: e.tensor_scalar(out=dg[:, di, :], in0=ident[:], scalar1=ccvw[:, pr, j, k:k + 1],
                                                                         scalar2=None, op0=ALU.mult),
                         reads=["cst", "ident", "rstd"], writes=[("dg", j % 2, k)])
        s0 = load_piece([("w_in_even", pr, 0, 1024, 0, 512, 0, 0)])
        sa = load_piece([("w_in_even", pr, 0, 1024, 512, 512, 0, 0)])
        sg = load_piece([("w_in_even", pr, 0, 1024, 1024, 512, 0, 0)])
        zbanks = proj_kouter(s0, [0, 1, 2, 3], t)
        for m in range(4):
            b = zbanks[m]
            P.op("act", lambda e, m=m, b=b: e.activation(out=zab[:, m, :, 15:15 + L], in_=pv(b, t), func=AF.Identity),
                 reads=[("ps", b)], writes=[("zab", m)])
        sig = A_SIG.rearrange("p (i n) -> p i n", i=2)
        for m in range(4):
            ba = proj_group(sa, m, t)
            bg = proj_group(sg, m, t)
            i = m % 2
            P.op("act", lambda e, i=i, bg=bg: e.activation(out=sig[:, i, :N], in_=ps[bg][:, :N], func=AF.Sigmoid),
                 reads=[("ps", bg)], writes=[("sig", i)])
            if m == 3:
                preload_sqrt()
            P.op("dve", lambda e, m=m, i=i, ba=ba: e.tensor_tensor(out=zbb[:, m, :, 30:30 + L], in0=pv(ba, t), in1=v3(sig[:, i, :N], t), op=ALU.mult),
                 reads=[("ps", ba), ("sig", i)], writes=[("zbb", m)])
            P.op("act", lambda e, m=m: e.activation(out=zbh[:, m], in_=zbb[:, m], func=AF.Identity),
                 reads=[("zbb", m)], writes=[("zbh", m)])
            if prebuilt[0] != pr:
                build_diags(0, range(8 * m, min(31, 8 * m + 8)))
        if prebuilt[0] != pr:
            build_diags(1, range(31))
        prebuilt[0] = None
        P.op("act", lambda e: e.activation(out=npool[:, pr, :, seqsl(t), :], in_=zab[:, :, :, L:L + 15], func=AF.Identity),
             reads=[("zab", m) for m in range(4)], writes=["npool"])
        P.op("act", lambda e: e.activation(out=nccv[:, pr, :, seqsl(t), :], in_=zbb[:, :, :, L:L + 30], func=AF.Identity),
             reads=[("zbb", m) for m in range(4)], writes=["nccv"])
        W = 15 + L
        T1 = A_T1[:, 0:S * W].rearrange("p (s w) -> p s w", s=S)
        T2 = A_T2[:, 0:S * W].rearrange("p (s w) -> p s w", s=S)
        dB = A_D.rearrange("p (c n) -> p c n", c=4)
        for g in range(4):
            z = zab[:, g]
            cur = z
            curres = ("zab", g)
            bufs = [(T1, "T1"), (T2, "T2")]
            sh = 1
            for step in range(g + 1):
                dst, dres = bufs[step % 2]
                lo = 2 * sh - 1
                P.op("dve", lambda e, dst=dst, cur=cur, lo=lo, sh=sh: e.tensor_tensor(
                    out=dst[:, :, lo:W], in0=cur[:, :, lo:W], in1=cur[:, :, lo - sh:W - sh], op=ALU.add),
                    reads=[curres], writes=[dres])
                cur, curres = dst, dres
                sh *= 2
            wlen = 2 ** (g + 1)
            P.op("dve", lambda e, g=g, cur=cur, z=z, wlen=wlen: e.scalar_tensor_tensor(
                out=v3(dB[:, g, :N], t), in0=cur[:, :, 15:W], scalar=1.0 / wlen, in1=z[:, :, 15:W],
                op0=ALU.mult, op1=ALU.subtract),
                reads=[curres, ("zab", g)], writes=[("d", g)])
            if t.kind == "own" and t.first:
                other = bufs[(g + 1) % 2]
                P.op("dve", lambda e, g=g, cur=cur, other=other: e.tensor_tensor(
                    out=other[0][:, 0, 0:16], in0=cur[:, 0, 15:31], in1=invc[:, g, :], op=ALU.mult),
                    reads=[curres, "invc"], writes=[other[1]])
                P.op("dve", lambda e, g=g, z=z, other=other: e.tensor_tensor(
                    out=dB[:, g, 0:16], in0=other[0][:, 0, 0:16], in1=z[:, 0, 15:31], op=ALU.subtract),
                    reads=[other[1], ("zab", g)], writes=[("d", g)])
        cb = A_CB.rearrange("p (c n) -> p c n", c=4)
        dg = A_DG.rearrange("p (i n) -> p i n", i=64)
        cbb = E_CBB.rearrange("p (c n) -> p c n", c=4)
        sqb = SQB.rearrange("p (c n) -> p c n", c=8)
        b1 = next_bank()
        b2 = next_bank()
        reserved_banks.update((b1, b2))

        def ln_stats(j):
            P.op("pe", lambda e, j=j: e.matmul(ps[b1][:, :N], ones[:, :], cbb[:, j, :N], start=(j == 0), stop=(j == 3)),
                 reads=[("cbb", j), "ones"], writes=[("ps", b1)] if j == 0 else [("psacc_l", 1)])
            P.op("pe", lambda e, j=j: e.matmul(ps[b2][:, :N], ones[:, :], sqb[:, j, :N], start=(j == 0), stop=(j == 3)),
                 reads=[("lsq", j), "ones"], writes=[("ps", b2)] if j == 0 else [("psacc_l", 2)])
            if j == 3:
                P.lastw[("ps", b1)] = ("op", "pe", len(P.ops["pe"]) - 2)
                P.lastw[("ps", b2)] = ("op", "pe", len(P.ops["pe"]) - 1)
        for j in range(4):
            b = next_bank()
            for k in range(31):
                di = (j % 2) * 32 + k
                P.op("pe", lambda e, j=j, k=k, di=di, b=b: e.matmul(pv(b, t), dg[:, di, :], zbh[:, j, :, k:k + L],
                                                                  start=(k == 0), stop=(k == 30)),
                     reads=[("dg", j % 2, k), ("zbh", j)], writes=[("ps", b)] if k == 0 else [("psacc_c", j)])
            P.lastw[("ps", b)] = ("op", "pe", len(P.ops["pe"]) - 1)
            P.op("act", lambda e, j=j, b=b: e.activation(out=cb[:, j, :N], in_=ps[b][:, :N], func=AF.Identity, bias=ccvb[:, pr, j:j + 1]),
                 reads=[("ps", b), "cst"], writes=[("cb", j)])
            P.op("act", lambda e, j=j, b=b: e.activation(out=cbb[:, j, :N], in_=ps[b][:, :N], func=AF.Identity, bias=ccvb[:, pr, j:j + 1]),
                 reads=[("ps", b), "cst"], writes=[("cbb", j)])
            P.op("act", lambda e, j=j, b=b: e.activation(out=sqb[:, j, :N], in_=ps[b][:, :N], func=AF.Square, bias=ccvb[:, pr, j:j + 1]),
                 reads=[("ps", b), "cst"] + [("zab", m) for m in range(4)] + [("d", g) for g in range(4)] + ["T1", "T2", "npool"],
                 writes=[("lsq", j), ("zab", 0), ("zab", 1)])
            if j + 2 < 4:
                build_diags(j + 2, range(31))
            if j >= 1:
                ln_stats(j - 1)
        for g in range(4):
            b = mm_group(lambda k, g=g: poolw[:, pr * 4 + g, :], lambda k, g=g: dB[:, g, :N], 1, 128, N,
                         reads=[("d", g), "poolw"])
            P.op("act", lambda e, g=g, b=b: e.activation(out=yb[:, g, :N], in_=ps[b][:, :N], func=AF.Identity, scale=pscale[:, pr, g:g + 1]),
                 reads=[("ps", b), "cst"], writes=[("y", g)])
        ln_stats(3)
        reserved_banks.difference_update((b1, b2))
        P.op("act", lambda e: e.activation(out=A_MEAN[:, :N], in_=ps[b1][:, :N], func=AF.Identity, scale=1.0 / 512),
             reads=[("ps", b1)], writes=["mean"])
        P.op("act", lambda e: e.activation(out=A_MSQ[:, :N], in_=ps[b1][:, :N], func=AF.Square, scale=1.0 / 512),
             reads=[("ps", b1)], writes=["msq"])
        P.op("dve", lambda e: e.scalar_tensor_tensor(out=A_LRS[:, :N], in0=ps[b2][:, :N], scalar=1.0 / 512, in1=A_MSQ[:, :N],
                                                    op0=ALU.mult, op1=ALU.subtract),
             reads=[("ps", b2), "msq"], writes=["lrs"])
        P.op("act", lambda e: e.activation(out=A_MSQ[:, :N], in_=A_LRS[:, :N], func=AF.Ln, bias=EPS, scale=1.0),
             reads=["lrs"], writes=["msq"])
        P.op("act", lambda e: e.activation(out=A_LRS[:, :N], in_=A_MSQ[:, :N], func=AF.Exp, scale=-0.5), reads=["msq"], writes=["lrs"])
        tt = A_TT.rearrange("p (i n) -> p i n", i=2)
        for j in range(4):
            i = j % 2
            P.op("dve", lambda e, j=j, i=i: e.tensor_tensor(out=tt[:, i, :N], in0=cb[:, j, :N], in1=A_MEAN[:, :N], op=ALU.subtract),
                 reads=[("cb", j), "mean"], writes=[("tt", i)])
            P.op("dve", lambda e, j=j, i=i: e.tensor_tensor(out=tt[:, i, :N], in0=tt[:, i, :N], in1=A_LRS[:, :N], op=ALU.mult),
                 reads=[("tt", i), "lrs"], writes=[("tt", i)])
            P.op("act", lambda e, j=j, i=i: e.activation(out=yb[:, 4 + j, :N], in_=tt[:, i, :N], func=AF.Silu,
                                                       bias=ccvbt[:, pr, j:j + 1], scale=ccvg[:, pr, j:j + 1]),
                 reads=[("tt", i), "cst"], writes=[("y", 4 + j)])
        preload_sqrt()
        out_proj(t, "w_out_even", pr)

    def odd_mixer(t, l):
        pr = l // 2
        N, S, L = t.N, t.S, t.L
        win = w_in_odd[pr]
        cx = v4(O_CX, 4, S, 2 + L)
        hist_src = stsc[:, pr] if t.kind == "sample" else nsc[:, pr, :, 0:1, :]
        P.op("act", lambda e: e.activation(out=cx[:, :, :, 0:2], in_=hist_src, func=AF.Identity),
             reads=["rstd", "stsc", "nsc"], writes=[("cx", j) for j in range(4)])
        if not (t.kind == "sample" and l == 3):
            prebuild_start((pr + 1) % 2)
        su = load_piece([("w_in_odd", pr, 0, 1024, 0, 512, 0, 0)])
        sv = load_piece([("w_in_odd", pr, 0, 1024, 512, 512, 0, 0)])
        vn = O_VN.rearrange("p (b n) -> p b n", b=4)
        v32 = O_V32.rearrange("p (b n) -> p b n", b=4)
        vt = O_VT.rearrange("p (i n) -> p i n", i=2)
        bn = O_BN
        vbanks = [next_bank() for _ in t.vblocks]
        for k in range(8):
            def fnv(e, k=k):
                ins = None
                for bi, (c0, nb) in enumerate(t.vblocks):
                    ins = e.matmul(ps[vbanks[bi]][:nb, :], hb[:, k, c0:c0 + nb], ring[sv][:, k, :], start=(k == 0), stop=(k == 7))
                return ins
            P.op("pe", fnv, reads=[("h", k), ("slot", sv)], writes=[("ps", b) for b in vbanks] if k == 0 else [("psacc_v", 0)])
        for b in vbanks:
            P.lastw[("ps", b)] = ("op", "pe", len(P.ops["pe"]) - 1)
        for bi, (c0, nb) in enumerate(t.vblocks):
            b = vbanks[bi]
            i = bi % 2
            P.op("dve", lambda e, b=b, nb=nb: e.bn_stats(out=bn[:nb, 0:6], in_=ps[b][:nb, :]), reads=[("ps", b)], writes=["bn6"])
            P.op("dve", lambda e, nb=nb: e.bn_aggr(out=bn[:nb, 8:10], in_=bn[:nb, 0:6]), reads=["bn6"], writes=["bnmv"])
            P.op("act", lambda e, nb=nb: e.activation(out=bn[:nb, 10:11], in_=bn[:nb, 9:10], func=AF.Ln, bias=EPS, scale=1.0),
                 reads=["bnmv"], writes=["bnsd"])
            P.op("act", lambda e, nb=nb: e.activation(out=bn[:nb, 11:12], in_=bn[:nb, 10:11], func=AF.Exp, scale=-0.5),
                 reads=["bnsd"], writes=["bnrs"])
            P.op("dve", lambda e, b=b, nb=nb, i=i: e.tensor_scalar(out=vt[:nb, i, :], in0=ps[b][:nb, :], scalar1=bn[:nb, 8:9],
                                                                  scalar2=bn[:nb, 11:12], op0=ALU.subtract, op1=ALU.mult),
                 reads=[("ps", b), "bnmv", "bnrs"], writes=[("vt", i)])
            P.op("dve", lambda e, nb=nb, i=i: e.tensor_tensor(out=vt[:nb, i, :], in0=vt[:nb, i, :], in1=sgug[:nb, pr, :], op=ALU.mult),
                 reads=[("vt", i), "sgug"], writes=[("vt", i)])
            if t.kind == "sample":
                P.op("dve", lambda e, nb=nb, i=i, bi=bi: e.tensor_tensor(out=v32[:nb, bi, :], in0=vt[:nb, i, :], in1=sgubt[:nb, pr, :], op=ALU.add),
                     reads=[("vt", i), "sgubt"], writes=[("v32", bi)])
                P.op("act", lambda e, nb=nb, bi=bi: e.activation(out=vn[:nb, bi, :], in_=v32[:nb, bi, :], func=AF.Identity),
                     reads=[("v32", bi)], writes=[("vn", bi)])
                sem = dsem("ov")

                def fn(e, bi=bi, sem=sem, nb=nb):
                    e.dma_start(out=o_v[pr, bi], in_=v32[:nb, bi, :]).then_inc(dmasems[sem], 16)
                    return None
                P.op("sp", fn, reads=[("v32", bi)], dma_sem=sem)
            else:
                P.op("dve", lambda e, nb=nb, i=i, bi=bi: e.tensor_tensor(out=vn[:nb, bi, :], in0=vt[:nb, i, :], in1=sgubt[:nb, pr, :], op=ALU.add),
                     reads=[("vt", i), "sgubt"], writes=[("vn", bi)])
        u = O_U.rearrange("p (c n) -> p c n", c=4)
        for m in range(4):
            b = proj_group(su, m, t)
            P.op("act", lambda e, m=m, b=b: e.activation(out=u[:, m, :N], in_=ps[b][:, :N], func=AF.Identity),
                 reads=[("ps", b)], writes=[("u", m)])
            prebuild_some(8)
        sb_ = load_piece([("w_in_odd", pr, 0, 1024, 1024, 512, 0, 0)])
        sc_ = load_piece([("w_in_odd", pr, 0, 1024, 1536, 512, 0, 0)])
        sx_ = load_piece([("w_in_odd", pr, 0, 1024, 2048, 512, 0, 0)])
        xin = O_XIN.rearrange("p (i n) -> p i n", i=2)
        bgs = O_BG.rearrange("p (i n) -> p i n", i=2)
        cz = O_CZ.rearrange("p (i n) -> p i n", i=2)
        for j in range(4):
            i = j % 2
            bx = proj_group(sx_, j, t)
            bc = proj_group(sc_, j, t)
            bb = proj_group(sb_, j, t)
            P.op("act", lambda e, i=i, bx=bx: e.activation(out=xin[:, i, :N], in_=ps[bx][:, :N], func=AF.Identity),
                 reads=[("ps", bx)], writes=[("xin", i)])
            P.op("dve", lambda e, j=j, i=i, bc=bc: e.tensor_tensor(out=cx[:, j, :, 2:2 + L], in0=pv(bc, t), in1=v3(xin[:, i, :N], t), op=ALU.mult),
                 reads=[("ps", bc), ("xin", i)], writes=[("cx", j)])
            P.op("act", lambda e, i=i, bb=bb: e.activation(out=bgs[:, i, :N], in_=ps[bb][:, :N], func=AF.Identity),
                 reads=[("ps", bb)], writes=[("bgs", i)])
            prebuild_some(8)
            P.op("dve", lambda e, j=j, i=i: e.tensor_scalar(out=v3(cz[:, i, :N], t), in0=cx[:, j, :, 2:2 + L], scalar1=scw[:, pr, j, 2:3],
                                                          scalar2=None, op0=ALU.mult),
                 reads=[("cx", j), "cst"], writes=[("cz", i)])
            for k in (1, 0):
                P.op("dve", lambda e, j=j, i=i, k=k: e.scalar_tensor_tensor(out=v3(cz[:, i, :N], t), in0=cx[:, j, :, k:k + L],
                                                                           scalar=scw[:, pr, j, k:k + 1], in1=v3(cz[:, i, :N], t),
                                                                           op0=ALU.mult, op1=ALU.add),
                     reads=[("cx", j), ("cz", i), "cst"], writes=[("cz", i)])
            P.op("dve", lambda e, j=j, i=i: e.tensor_tensor(out=yb[:, 4 + j, :N], in0=cz[:, i, :N], in1=bgs[:, i, :N], op=ALU.mult),
                 reads=[("cz", i), ("bgs", i)], writes=[("y", 4 + j)])
        mt = O_MT.rearrange("p (i n) -> p i n", i=2)
        for j in range(4):
            b = next_bank()
            segs = []
            if t.kind == "sample":
                for q in range(NSS):
                    segs.append((q, 0, 64, q * 64, False))
            else:
                for bi, (c0, nb) in enumerate(t.vblocks):
                    segs.append((bi, 0, nb, c0, False))

            def fn(e, j=j, b=b, segs=segs):
                ins = None
                for (bi, po, ln, c0, smp) in segs:
                    rhs = wsTs[po:po + ln, pr * 4 + j, 0:ln] if smp else wsT[0:ln, pr * 4 + j, 0:ln]
                    ins = e.matmul(ps[b][:, c0:c0 + ln], vn[po:po + ln, bi, j * 128:(j + 1) * 128], rhs, start=True, stop=True)
                return ins
            P.op("pe", fn, reads=[("vn", bi) for bi in range(len(t.vblocks))] + ["wsT", "wsTs"], writes=[("ps", b)])
            i = j % 2
            if t.kind == "sample":
                groups = [(0, NSS, 64)]
            elif t.kind == "halo":
                groups = [(0, 1, 32), (32, 2, 128)]
            else:
                groups = [(0, 4, 128)]
            for gi, (c0, nblk, bl) in enumerate(groups):
                P.op("dve", lambda e, j=j, b=b, i=i, c0=c0, nblk=nblk, bl=bl: e.tensor_tensor(
                    out=mt[:, i, c0:c0 + nblk * bl].rearrange("p (a n) -> p a n", a=nblk),
                    in0=ps[b][:, c0:c0 + nblk * bl].rearrange("p (a n) -> p a n", a=nblk),
                    in1=sgubias[:, pr * 4 + j, 0:bl].unsqueeze(1).broadcast_to([128, nblk, bl]), op=ALU.add),
                    reads=[("ps", b), "sgubias"], writes=[("mt", i)] if gi == 0 else [("mt", i)])
            P.op("dve", lambda e, j=j, i=i: e.tensor_tensor(out=yb[:, j, :N], in0=mt[:, i, :N], in1=u[:, j, :N], op=ALU.mult),
                 reads=[("mt", i), ("u", j)], writes=[("y", j)])
        P.op("act", lambda e: e.activation(out=nsc[:, pr, :, seqsl(t), :], in_=cx[:, :, :, L:L + 2], func=AF.Identity),
             reads=[("cx", j) for j in range(4)], writes=["nsc"])
        prebuild_some(64)
        out_proj(t, "w_out_odd", pr, korder=(4, 5, 6, 7, 0, 1, 2, 3))

    def conv_ffn(t, l):
        N, S, L = t.N, t.S, t.L
        wup = ffn_w_up[l]
        wdn = ffn_w_down[l]
        gB = F_G.rearrange("p (c n) -> p c n", c=22)
        sbuf_ = F_S.rearrange("p (i n) -> p i n", i=2)
        ptb = F_PT.rearrange("p (i n) -> p i n", i=2)
        trim = (t.kind == "halo" and l == 3)
        d0 = dict(banks=[] if trim else [next_bank() for _ in range(4)])
        LAG = 2

        def down0_mini(u):
            banks = d0["banks"]
            if u == 0:
                reserved_banks.update(banks)
            kk, nk = 2 * u, 2
            s = load_piece([("ffn_w_down", l, kk * 128, nk * 128, 0, 512, 0, 0)])
            for k in range(nk):
                kg = kk + k

                def fn(e, s=s, k=k, kg=kg, banks=banks):
                    ins = None
                    for mm in range(4):
                        ins = e.matmul(ps[banks[mm]][:, :N], ring[s][:, k, mm * 128:(mm + 1) * 128], gB[:, kg, :N],
                                       start=(kg == 0), stop=(kg == 21))
                    return ins
                P.op("pe", fn, reads=[("g", kg), ("slot", s)], writes=[("ps", bq) for bq in banks] if kg == 0 else [("psacc", 0)])
            if u == NU - 1:
                for mm in range(4):
                    P.lastw[("ps", banks[mm])] = ("op", "pe", len(P.ops["pe"]) - 1)
        for ui in range(NU):
            s = load_piece([("ffn_w_up", l, 0, 1024, 256 * ui, 256, 0, 0), ("ffn_w_up", l, 0, 1024, DFF + 256 * ui, 256, 0, 256)])
            ub = ui % 2
            up = F_UP[:, ub * 2056: ub * 2056 + 4 * S * (2 + L)].rearrange("p (c s w) -> p c s w", c=4, s=S)
            hist_src = stffn[:, l, 4 * ui:4 * ui + 4] if t.kind == "sample" else nffn[:, l, 4 * ui:4 * ui + 4, 0:1, :]
            P.op("act", lambda e, up=up, hist_src=hist_src: e.activation(out=up[:, :, :, 0:2], in_=hist_src, func=AF.Identity),
                 reads=["rstd", "stffn", "nffn"], writes=[("up", ub, q) for q in range(4)])
            ubanks = proj_kouter(s, [0, 1, 2, 3], t) if ui == 0 else None
            for pp in range(2):
                cbuf = F_C[:, pp * 1024:(pp + 1) * 1024].rearrange("p (i n) -> p i n", i=2)
                banks = []
                for half in range(2):
                    q = half * 2 + pp
                    b = ubanks[q] if ui == 0 else proj_group(s, q, t)
                    banks.append(b)
                    ch = 4 * ui + q
                    P.op("act", lambda e, up=up, q=q, b=b: e.activation(out=up[:, q, :, 2:2 + L], in_=pv(b, t), func=AF.Identity),
                         reads=[("ps", b)], writes=[("up", ub, q)])
                    if trim:
                        continue
                    P.op("act", lambda e, cbuf=cbuf, half=half, b=b, ch=ch: e.activation(
                        out=cbuf[:, half, :N], in_=ps[b][:, :N], func=AF.Identity, bias=fcb[:, l, ch:ch + 1], scale=fcw[:, l, ch, 2:3]),
                        reads=[("ps", b), "cst"], writes=[("c", pp, half)])
                    P.op("dve", lambda e, cbuf=cbuf, half=half, up=up, q=q, ch=ch: e.scalar_tensor_tensor(
                        out=v3(cbuf[:, half, :N], t), in0=up[:, q, :, 1:1 + L], scalar=fcw[:, l, ch, 1:2],
                        in1=v3(cbuf[:, half, :N], t), op0=ALU.mult, op1=ALU.add),
                        reads=[("up", ub, q), ("c", pp, half), "cst"], writes=[("c", pp, half)])
                    P.op("dve", lambda e, cbuf=cbuf, half=half, up=up, q=q, ch=ch: e.scalar_tensor_tensor(
                        out=v3(cbuf[:, half, :N], t), in0=up[:, q, :, 0:L], scalar=fcw[:, l, ch, 0:1],
                        in1=v3(cbuf[:, half, :N], t), op0=ALU.mult, op1=ALU.add),
                        reads=[("up", ub, q), ("c", pp, half), "cst"], writes=[("c", pp, half)])
                if trim:
                    continue
                P.op("act", lambda e, cbuf=cbuf, pp=pp: e.activation(out=sbuf_[:, pp, :N], in_=cbuf[:, 0, :N], func=AF.Silu),
                     reads=[("c", pp, 0)], writes=[("s", pp)])
                if ui == NU - 1 and pp == 1:
                    preload_sqrt()
                gch = 2 * ui + pp
                P.op("dve", lambda e, cbuf=cbuf, pp=pp, gch=gch: e.tensor_tensor(out=gB[:, gch, :N], in0=sbuf_[:, pp, :N], in1=cbuf[:, 1, :N], op=ALU.mult),
                     reads=[("s", pp), ("c", pp, 1)], writes=[("g", gch)])
            P.op("act", lambda e, up=up, ui=ui: e.activation(out=nffn[:, l, 4 * ui:4 * ui + 4, seqsl(t), :], in_=up[:, :, :, L:L + 2], func=AF.Identity),
                 reads=[("up", ub, q) for q in range(4)], writes=["nffn"])
            if ui >= LAG and not trim:
                down0_mini(ui - LAG)
        if trim:
            return
        for u in range(NU - LAG, NU):
            down0_mini(u)
        for half in range(2):
            if half == 0:
                banks = d0["banks"]
                reserved_banks.difference_update(banks)
            else:
                slots = []
                kk = 0
                for pi, nk in enumerate((8, 8, 6)):
                    slots.append((load_piece([("ffn_w_down", l, kk * 128, nk * 128, half * 512, 512, 0, 0)]), kk, nk))
                    kk += nk
                banks = []
                for mm in range(4):
                    b = next_bank()
                    banks.append(b)

                    def fn(e, mm=mm, b=b, slots=slots):
                        ins = None
                        for (s, k0, nk) in slots:
                            for k in range(nk):
                                kg = k0 + k
                                ins = e.matmul(ps[b][:, :N], ring[s][:, k, mm * 128:(mm + 1) * 128], gB[:, kg, :N],
                                               start=(kg == 0), stop=(kg == 21))
                        return ins
                    P.op("pe", fn, reads=[("g", kg) for kg in range(22)] + [("slot", s) for (s, _, _) in slots], writes=[("ps", b)])
            for mm in range(4):
                m = half * 4 + mm
                b = banks[mm]
                P.op("dve", lambda e, m=m, b=b: e.tensor_tensor(out=xb[:, m, :N], in0=ps[b][:, :N], in1=xb[:, m, :N], op=ALU.add),
                     reads=[("ps", b), ("x", m)], writes=[("x", m)])

    first = True
    for ti, t in enumerate(tiles):
        if dbg_tiles is not None and ti not in dbg_tiles:
            continue
        N = t.N
        for c in range(8):
            sem = dsem(f"xin{c}")

            def fnx(e, t=t, sem=sem, c=c):
                e.dma_start(out=xb[:, c, :t.N], in_=xT[:, c, t.col0:t.col0 + t.N]).then_inc(dmasems[sem], 16)
                return None
            P.op("sp", fnx, writes=[("x", c)], dma_sem=sem)
        for l in range(dbg_layers):
            rmsnorm(t, lambda c, l=l: gmix[:, l, c:c + 1], extra_reads=SETUP if first else (),
                    h_extra=["youtdma"] if l == 0 else ())
            first = False
            if dbg_stage < 1:
                continue
            if l % 2 == 0:
                even_mixer(t, l)
            else:
                odd_mixer(t, l)
            if dbg_stage < 2:
                continue
            rmsnorm(t, lambda c, l=l: gffn[:, l, c:c + 1])
            conv_ffn(t, l)
        if t.kind != "halo":
            rmsnorm(t, lambda c: gfin[:, c:c + 1], final=True)
            yo = YOUT.rearrange("p (c n) -> p c n", c=8)
            sem = dsem("yout")

            def fny(e, t=t, sem=sem):
                e.dma_start(out=yT[:, :, t.outcol:t.outcol + t.N], in_=yo[:, :, :t.N]).then_inc(dmasems[sem], 16)
                return None
            ev = P.op("sp", fny, reads=[("yout", c) for c in range(8)], dma_sem=sem)
            P.lastw["youtdma"] = ev

    sem = dsem("fin")
    for dst, srcb, res in ((o_pool, npool, "npool"), (o_ccv, nccv, "nccv"), (o_sc, nsc, "nsc"), (o_ffn, nffn, "nffn")):
        def fno(e, dst=dst, srcb=srcb, sem=sem):
            e.dma_start(out=dst, in_=srcb[:].rearrange("p a c s k -> p (a c s k)")).then_inc(dmasems[sem], 16)
            return None
        P.op("sp", fno, reads=[res], dma_sem=sem)

    finals = [("sem", n, P.dmasem_count[n]) for n in ("fin", "yout", "ov") if n in P.dmasem_count]


    if record:
        es.close()
        return pieces
    P.resolve()

    with nc.Block() as block:
        @block.tensor
        def _(e):
            P.emit("pe", e, engsem, dmasems)

        @block.scalar
        def _(e):
            P.emit("act", e, engsem, dmasems)

        @block.vector
        def _(e):
            P.emit("dve", e, engsem, dmasems)

        @block.gpsimd
        def _(e):
            P.emit("pool", e, engsem, dmasems)

        @block.sync
        def _(e):
            P.emit("sp", e, engsem, dmasems)
            for (_, n, v) in finals:
                e.wait_ge(dmasems[n], v)
    es.close()
    return nc


_NC_CACHE = {}


def _fm(a):
    a = np.asarray(a)
    C = a.shape[-1]
    T_ = a.shape[-2]
    lead = a.shape[:-2]
    b = a.reshape(lead + (T_, C // 128, 128))
    nd = b.ndim
    perm = (nd - 1,) + tuple(range(len(lead))) + (nd - 2, nd - 3)
    return np.ascontiguousarray(b.transpose(perm))


def kernel(x_prompt, x_sample, state_pool, state_ccv, state_sconv, state_ffn_conv,
           norm_mix_g, norm_ffn_g, norm_final_g,
           w_in_even, pool_w, pool_scale, ccv_w, ccv_b, ccv_ln_g, ccv_ln_b, w_out_even,
           w_in_odd, sgu_ln_g, sgu_ln_b, sgu_ws, sgu_b, sconv_w, w_out_odd,
           ffn_w_up, ffn_conv_w, ffn_conv_b, ffn_w_down):
    f32 = np.float32
    A = lambda a: np.ascontiguousarray(np.asarray(a, dtype=f32))
    x_prompt = A(x_prompt); x_sample = A(x_sample)
    perm = _unit_perm()

    def vec_fm(v):
        v = A(v)
        lead = v.shape[:-1]
        b = v.reshape(lead + (v.shape[-1] // 128, 128))
        nd = b.ndim
        return np.ascontiguousarray(b.transpose((nd - 1,) + tuple(range(nd - 1))))

    cst = np.zeros((128, CSTN), f32)

    def put(name, arr):
        o, s = CST[name]
        cst[:, o:o + s] = arr.reshape(128, s)
    put("gmix", vec_fm(norm_mix_g))
    put("gffn", vec_fm(norm_ffn_g))
    put("gfin", vec_fm(norm_final_g))
    put("pscale", vec_fm(pool_scale))
    put("ccvw", vec_fm(ccv_w).transpose(0, 1, 3, 2))
    put("ccvb", vec_fm(ccv_b))
    put("ccvg", vec_fm(ccv_ln_g))
    put("ccvbt", vec_fm(ccv_ln_b))
    put("scw", vec_fm(sconv_w).transpose(0, 1, 3, 2))
    fw = vec_fm(ffn_conv_w)[:, :, :, perm]
    put("fcw", fw.transpose(0, 1, 3, 2))
    put("fcb", vec_fm(ffn_conv_b)[:, :, perm])

    ws = A(sgu_ws)
    wsT = np.ascontiguousarray(ws.transpose(3, 0, 1, 2)).reshape(128, 2 * 4 * 128)
    wsTs = np.ascontiguousarray(ws[:, :, :64, :64].transpose(3, 0, 1, 2))
    wsTs = np.concatenate([wsTs, wsTs], axis=0).reshape(128, 2 * 4 * 64)
    tril = (np.arange(128)[None, :] >= np.arange(128)[:, None]).astype(f32)
    trils = np.concatenate([tril[:64, :64], tril[:64, :64]], axis=0)
    poolw = np.ascontiguousarray(A(pool_w).transpose(2, 0, 1, 3)).reshape(128, 2 * 4 * 128)

    shared = dict(
        cst=cst, sgug=A(sgu_ln_g).reshape(-1), sgubt=A(sgu_ln_b).reshape(-1), sgubias=A(sgu_b).reshape(-1),
        wsT=wsT, wsTs=wsTs, tril=tril, ident=np.eye(128, dtype=f32), trils=np.ascontiguousarray(trils), poolw=poolw,
        w_in_even=A(w_in_even), w_out_even=A(w_out_even), w_in_odd=A(w_in_odd), w_out_odd=A(w_out_odd),
        ffn_w_up=A(ffn_w_up), ffn_w_down=A(ffn_w_down),
    )

    xpT = _fm(x_prompt[0])
    xsT = _fm(x_sample)
    sp_fm = _fm(A(state_pool))
    scv_fm = _fm(A(state_ccv))
    ssc_fm = _fm(A(state_sconv))
    sff_fm = _fm(A(state_ffn_conv))[:, :, :, perm, :]

    in_maps = []
    for c in range(NCORES):
        a = c * OWN
        xT = np.zeros((128, 8, TOK), f32)
        if c > 0:
            xT[:, :, 0:HALO] = xpT[:, :, a - HALO:a]
        xT[:, :, HALO:HALO + OWN] = xpT[:, :, a:a + OWN]
        xs = xsT[:, NSS * c:NSS * c + NSS]
        xT[:, :, HALO + OWN:] = xs.transpose(0, 2, 1, 3).reshape(128, 8, NSS * LS)
        mask = np.full((HALO,), 0.0 if c == 0 else 1.0, f32)
        invc = np.zeros((4, 16), f32)
        for g, w in enumerate((2, 4, 8, 16)):
            if c == 0:
                invc[g] = 1.0 / np.minimum(np.arange(16) + 1, w)
            else:
                invc[g] = 1.0 / w
        sl = slice(NSS * c, NSS * c + NSS)
        m = dict(shared)
        m.update(
            xT=xT, mask=mask, invc=invc.reshape(-1),
            st_pool=np.ascontiguousarray(sp_fm[:, :, sl].transpose(0, 1, 3, 2, 4)).reshape(128, -1),
            st_ccv=np.ascontiguousarray(scv_fm[:, :, sl].transpose(0, 1, 3, 2, 4)).reshape(128, -1),
            st_sc=np.ascontiguousarray(ssc_fm[:, :, sl].transpose(0, 1, 3, 2, 4)).reshape(128, -1),
            st_ffn=np.ascontiguousarray(sff_fm[:, :, sl].transpose(0, 1, 3, 2, 4)).reshape(128, -1),
        )
        in_maps.append(m)

    if "nc" not in _NC_CACHE:
        _NC_CACHE["nc"] = build_program()
    nc = _NC_CACHE["nc"]
    res = run_bass_kernel_spmd(nc, in_maps, core_ids=list(range(NCORES)))
    R = res.results

    def tm(a):
        return np.ascontiguousarray(a.transpose(2, 1, 0).reshape(a.shape[2], -1))

    y_prompt = np.zeros((1, SEQ, D), f32)
    y_sample = np.zeros((32, LS, D), f32)
    for c in range(NCORES):
        yT = R[c]["yT"]
        y_prompt[0, c * OWN:(c + 1) * OWN] = tm(yT[:, :, :OWN])
        ys = yT[:, :, OWN:].reshape(128, 8, NSS, LS)
        for q in range(NSS):
            y_sample[NSS * c + q] = tm(ys[:, :, q, :])

    def states(key, npair, C, H, permute=None):
        pp = np.zeros((npair, 1, H, C * 128), f32)
        ss = np.zeros((npair, 32, H, C * 128), f32)
        for c in range(NCORES):
            a = R[c][key].reshape(128, npair, C, 5, H)
            if permute is not None:
                b = np.zeros_like(a)
                b[:, :, permute] = a
                a = b
            full = a.transpose(1, 3, 4, 2, 0).reshape(npair, 5, H, C * 128)
            if c == NCORES - 1:
                pp[:, 0] = full[:, 0]
            ss[:, NSS * c:NSS * c + NSS] = full[:, 1:5]
        return pp, ss

    pool_p, pool_s = states("o_pool", 2, 4, 15)
    ccv_p, ccv_s = states("o_ccv", 2, 4, 30)
    sc_p, sc_s = states("o_sc", 2, 4, 2)
    ffn_p, ffn_s = states("o_ffn", 4, 44, 2, permute=perm)
    v_s = np.zeros((2, 32, LS, 512), f32)
    for c in range(NCORES):
        v_s[:, NSS * c:NSS * c + NSS] = R[c]["o_v"]
    return (y_prompt, y_sample, pool_p, pool_s, ccv_p, ccv_s, sc_p, sc_s, v_s, ffn_p, ffn_s)
```

```python
import numpy as np
import concourse.bass as bass
import concourse.mybir as mybir
from concourse.bass_utils import run_bass_kernel_spmd

F32 = mybir.dt.float32
BF16 = mybir.dt.bfloat16
ALU = mybir.AluOpType
AF = mybir.ActivationFunctionType

NCORES = 8
D = 1024
SEQ = 16384
OWN = SEQ // NCORES
HALO = 288
NSS = 4
LS = 64
TOK = HALO + OWN + NSS * LS
NOUT = OWN + NSS * LS
DFF = 2816
NU = 11
EPS = 1e-6
NSLOT = 6

CST = {}
_off = 0
for _n, _sz in [("gmix", 32), ("gffn", 32), ("gfin", 8), ("pscale", 8), ("ccvw", 2 * 4 * 31),
                ("ccvb", 8), ("ccvg", 8), ("ccvbt", 8), ("scw", 2 * 4 * 3), ("fcw", 4 * 44 * 3),
                ("fcb", 4 * 44)]:
    CST[_n] = (_off, _sz)
    _off += _sz
CSTN = _off


def _unit_perm():
    perm = []
    for i in range(NU):
        perm += [2 * i, 2 * i + 1, 22 + 2 * i, 22 + 2 * i + 1]
    return np.array(perm)


class Plan:
    ENG = ("pe", "act", "dve", "pool", "sp")

    def __init__(self):
        self.ops = {e: [] for e in self.ENG}
        self.lastw = {}
        self.readers = {}
        self.dmasem_count = {}

    def op(self, eng, fn, reads=(), writes=(), dma_sem=None):
        idx = len(self.ops[eng])
        deps = set()
        for r in reads:
            if r in self.lastw:
                deps.add(self.lastw[r])
        for w in writes:
            if w in self.lastw:
                deps.add(self.lastw[w])
            for ev in self.readers.get(w, ()):
                deps.add(ev)
        if dma_sem is not None:
            self.dmasem_count[dma_sem] = self.dmasem_count.get(dma_sem, 0) + 16
            ev = ("sem", dma_sem, self.dmasem_count[dma_sem])
        else:
            ev = ("op", eng, idx)
        fdeps = set()
        for d in deps:
            if d[0] == "op" and d[1] == eng:
                if eng == "pe":
                    continue
                if idx - d[2] >= 3:
                    continue
            fdeps.add(d)
        self.ops[eng].append(dict(fn=fn, deps=fdeps, flag=False, dma=dma_sem is not None))
        for r in reads:
            self.readers.setdefault(r, []).append(ev)
        for w in writes:
            self.lastw[w] = ev
            self.readers[w] = []
        return ev

    def resolve(self):
        for e in self.ENG:
            for o in self.ops[e]:
                for d in o["deps"]:
                    if d[0] == "op":
                        self.ops[d[1]][d[2]]["flag"] = True
        self.count = {}
        for e in self.ENG:
            c = 0
            lst = []
            for o in self.ops[e]:
                if o["flag"]:
                    c += 1
                lst.append(c)
            self.count[e] = lst

    def emit(self, eng, e, engsem, dmasems):
        known = {}
        for o in self.ops[eng]:
            need = {}
            for d in o["deps"]:
                if d[0] == "op":
                    k = ("e", d[1])
                    v = self.count[d[1]][d[2]]
                else:
                    k = ("d", d[1])
                    v = d[2]
                if need.get(k, 0) < v:
                    need[k] = v
            for k, v in need.items():
                if known.get(k, 0) < v:
                    sem = engsem[k[1]] if k[0] == "e" else dmasems[k[1]]
                    e.wait_ge(sem, v)
                    known[k] = v
            ins = o["fn"](e)
            if o["flag"]:
                ins.then_inc(engsem[eng], 1)


def build_program(dbg_tiles=None, dbg_layers=4, dbg_stage=99, record=False, pieces=None):
    if not record and pieces is None:
        pieces = build_program(dbg_tiles, dbg_layers, dbg_stage, record=True, pieces=[])
    nc = bass.Bass("TRN2", target_bir_lowering=False)
    P = Plan()

    def din(name, shape):
        return nc.dram_tensor(name, list(shape), F32, kind="ExternalInput").ap()

    def dout(name, shape):
        return nc.dram_tensor(name, list(shape), F32, kind="ExternalOutput").ap()

    xT = din("xT", [128, 8, TOK])
    cst_d = din("cst", [128, CSTN])
    mask_d = din("mask", [HALO])
    invc_d = din("invc", [4 * 16])
    sgug_d = din("sgug", [2 * 512])
    sgub_d = din("sgubt", [2 * 512])
    sgubias_d = din("sgubias", [2 * 4 * 128])
    wsT_d = din("wsT", [128, 2 * 4 * 128])
    wsTs_d = din("wsTs", [128, 2 * 4 * 64])
    tril_d = din("tril", [128, 128])
    trils_d = din("trils", [128, 64])
    poolw_d = din("poolw", [128, 2 * 4 * 128])
    ident_d = din("ident", [128, 128])
    stpool_d = din("st_pool", [128, 2 * 4 * NSS * 15])
    stccv_d = din("st_ccv", [128, 2 * 4 * NSS * 30])
    stsc_d = din("st_sc", [128, 2 * 4 * NSS * 2])
    stffn_d = din("st_ffn", [128, 4 * 44 * NSS * 2])
    w_in_even = din("w_in_even", [2, D, 1536])
    w_out_even = din("w_out_even", [2, D, D])
    w_in_odd = din("w_in_odd", [2, D, 2560])
    w_out_odd = din("w_out_odd", [2, D, D])
    ffn_w_up = din("ffn_w_up", [4, D, 2 * DFF])
    ffn_w_down = din("ffn_w_down", [4, DFF, D])

    yT = dout("yT", [128, 8, NOUT])
    o_pool = dout("o_pool", [128, 2 * 4 * 5 * 15])
    o_ccv = dout("o_ccv", [128, 2 * 4 * 5 * 30])
    o_sc = dout("o_sc", [128, 2 * 4 * 5 * 2])
    o_ffn = dout("o_ffn", [128, 4 * 44 * 5 * 2])
    o_v = dout("o_v", [2, NSS, LS, 512])

    from contextlib import ExitStack
    es = ExitStack()

    def sb(name, shape, dt=F32):
        return es.enter_context(nc.sbuf_tensor("sb_" + name, list(shape), dt))

    xb = sb("xb", [128, 8, 512])
    hb = sb("hb", [128, 8, 512], BF16)
    yb = sb("yb", [128, 8, 512], BF16)
    rstd = sb("rstd", [128, 512])
    sd = sb("sd", [128, 512])
    cst = sb("cst", [128, CSTN])
    maskb = sb("maskb", [128, HALO])
    invc = sb("invc", [128, 4, 16])
    sgug = sb("sgug", [128, 2, 512])
    sgubt = sb("sgubt", [128, 2, 512])
    sgubias = sb("sgubias", [128, 8, 128])
    wsT = sb("wsT", [128, 8, 128], BF16)
    wsTs = sb("wsTs", [128, 8, 64], BF16)
    poolw = sb("poolw", [128, 8, 128], BF16)
    ones = sb("ones", [128, 128], BF16)
    dmy = sb("dmy", [128, 8])
    ident = sb("ident", [128, 128])
    stpool = sb("stpool", [128, 2, 4, NSS, 15])
    stccv = sb("stccv", [128, 2, 4, NSS, 30])
    stsc = sb("stsc", [128, 2, 4, NSS, 2])
    stffn = sb("stffn", [128, 4, 44, NSS, 2])
    npool = sb("npool", [128, 2, 4, 5, 15])
    nccv = sb("nccv", [128, 2, 4, 5, 30])
    nsc = sb("nsc", [128, 2, 4, 5, 2])
    nffn = sb("nffn", [128, 4, 44, 5, 2])
    ring = [sb(f"ring{i}", [128, 8, 512], BF16) for i in range(NSLOT)]
    ARENA_F = 18240
    arena = sb("arena", [128, ARENA_F])
    ps = [es.enter_context(nc.psum_tensor(f"ps{i}", [128, 512], F32)) for i in range(8)]

    import os
    if os.environ.get("SBUF_FREE"):
        print("SBUF free bytes/partition:", nc.sbuf_bytes_remaining)
    engsem = {e: es.enter_context(nc.semaphore(f"s_{e}")) for e in Plan.ENG}
    dmasems = {}

    def dsem(name):
        if name not in dmasems:
            dmasems[name] = es.enter_context(nc.semaphore(f"d_{name}"))
        return name

    def carve(off, words):
        return arena[:, off:off + words]

    A_ZAB = carve(0, 4 * 527)
    A_ZBB = carve(2108, 4 * 542)
    A_T1 = carve(4276, 527)
    A_T2 = carve(4803, 527)
    A_D = carve(5330, 1024).bitcast(BF16)
    A_SIG = carve(6354, 1024)
    A_CB = carve(7378, 2048)
    A_MEAN = carve(9426, 512)
    A_MSQ = carve(9938, 512)
    A_LRS = carve(10450, 512)
    A_TT = carve(10962, 1024)
    A_ZBH = carve(13010, 1084).bitcast(BF16)
    A_DG = carve(14094, 4096).bitcast(BF16)
    O_U = carve(0, 2048)
    O_VN = carve(2048, 1024).bitcast(BF16)
    O_V32 = carve(3072, 2048)
    O_VT = carve(5120, 1024)
    O_CX = carve(6144, 4 * 514)
    O_XIN = carve(8200, 1024)
    O_BG = carve(9224, 1024)
    O_CZ = carve(10248, 1024)
    O_MT = carve(11272, 1024)
    O_BN = carve(12296, 64)
    F_UP = carve(0, 2 * 4 * 514)
    F_C = carve(4112, 4 * 512)
    F_S = carve(6160, 1024)
    F_G = carve(7184, 5632).bitcast(BF16)
    F_PT = carve(12816, 1024)
    SQB = carve(0, 2048).bitcast(BF16)
    A_LNB = carve(10962 + 1024, 1070)
    YOUT = carve(2048, 4096)

    E_CBB = carve(11986, 1024).bitcast(BF16)
    assert 11986 + 1024 <= ARENA_F
    E_SQB = A_TT.bitcast(BF16)

    cview = lambda name: cst[:, CST[name][0]:CST[name][0] + CST[name][1]]
    gmix = cview("gmix").rearrange("p (l c) -> p l c", l=4)
    gffn = cview("gffn").rearrange("p (l c) -> p l c", l=4)
    gfin = cview("gfin")
    pscale = cview("pscale").rearrange("p (a c) -> p a c", a=2)
    ccvw = cview("ccvw").rearrange("p (a c k) -> p a c k", a=2, c=4)
    ccvb = cview("ccvb").rearrange("p (a c) -> p a c", a=2)
    ccvg = cview("ccvg").rearrange("p (a c) -> p a c", a=2)
    ccvbt = cview("ccvbt").rearrange("p (a c) -> p a c", a=2)
    scw = cview("scw").rearrange("p (a c k) -> p a c k", a=2, c=4)
    fcw = cview("fcw").rearrange("p (l c k) -> p l c k", l=4, c=44)
    fcb = cview("fcb").rearrange("p (l c) -> p l c", l=4)

    bank_ctr = [0]
    dg_ctr = [0]
    pt_ctr = [0]

    reserved_banks = set()

    def next_bank():
        while True:
            b = bank_ctr[0] % 8
            bank_ctr[0] += 1
            if b not in reserved_banks:
                return b

    WTS = dict(w_in_even=w_in_even, w_out_even=w_out_even, w_in_odd=w_in_odd, w_out_odd=w_out_odd,
               ffn_w_up=ffn_w_up, ffn_w_down=ffn_w_down)
    piece_ctr = [0]
    emitted = [0]
    KPRE = NSLOT - 3

    def emit_load(j):
        s = j % NSLOT
        sem = dsem(f"w{s}")
        for i, (wn, idx, r0, nrows, c0, ncol, k0, co) in enumerate(pieces[j]):
            src = WTS[wn][idx][r0:r0 + nrows, c0:c0 + ncol].rearrange("(k p) n -> p k n", p=128)
            nk = nrows // 128

            def fn(e, src=src, s=s, k0=k0, nk=nk, co=co, ncol=ncol, sem=sem):
                e.dma_start(out=ring[s][:, k0:k0 + nk, co:co + ncol], in_=src).then_inc(dmasems[sem], 16)
                return None
            P.op("pool", fn, writes=[("slot", s)] if i == 0 else [], dma_sem=sem)
            if i > 0:
                P.lastw[("slot", s)] = ("sem", sem, P.dmasem_count[sem])

    def load_piece(descs):
        i = piece_ctr[0]
        piece_ctr[0] += 1
        if record:
            pieces.append(list(descs))
            return i % NSLOT
        assert pieces[i] == list(descs)
        while emitted[0] <= min(i + KPRE, len(pieces) - 1):
            emit_load(emitted[0])
            emitted[0] += 1
        return i % NSLOT

    def wcols(w2d, c0, ncol):
        return w2d.rearrange("(k p) n -> p k n", p=128)[:, :, c0:c0 + ncol]

    def mm_group(lhs, rhs, K, M, N, reads, bank=None, start=True, stop=True):
        b = next_bank() if bank is None else bank

        def fn(e):
            ins = None
            for k in range(K):
                ins = e.matmul(ps[b][:M, :N], lhs(k), rhs(k), start=(start and k == 0), stop=(stop and k == K - 1))
            return ins
        P.op("pe", fn, reads=reads, writes=[("ps", b)])
        return b

    def ld(dst, src, res):
        sem = dsem("c_" + res)

        def fn(e):
            e.dma_start(out=dst, in_=src).then_inc(dmasems[sem], 16)
            return None
        P.op("sp", fn, writes=[res], dma_sem=sem)

    ld(cst[:], cst_d, "cst")
    ld(ident[:], ident_d, "ident")
    ld(maskb[:], mask_d.partition_broadcast(128), "maskb")
    ld(invc[:], invc_d.partition_broadcast(128).rearrange("p (g k) -> p g k", g=4), "invc")
    ld(sgug[:], sgug_d.partition_broadcast(128).rearrange("p (a n) -> p a n", a=2), "sgug")
    ld(sgubt[:], sgub_d.partition_broadcast(128).rearrange("p (a n) -> p a n", a=2), "sgubt")
    ld(sgubias[:], sgubias_d.partition_broadcast(128).rearrange("p (a n) -> p a n", a=8), "sgubias")
    ld(stpool[:], stpool_d.rearrange("p (a c s k) -> p a c s k", a=2, c=4, s=NSS), "stpool")
    ld(stccv[:], stccv_d.rearrange("p (a c s k) -> p a c s k", a=2, c=4, s=NSS), "stccv")
    ld(stsc[:], stsc_d.rearrange("p (a c s k) -> p a c s k", a=2, c=4, s=NSS), "stsc")
    ld(stffn[:], stffn_d.rearrange("p (a c s k) -> p a c s k", a=4, c=44, s=NSS), "stffn")
    T_WS = carve(0, 1024).rearrange("p (a n) -> p a n", a=8)
    T_WSS = carve(1024, 512).rearrange("p (a n) -> p a n", a=8)
    T_TR = carve(1536, 128)
    T_TRS = carve(1664, 64)
    T_PW = carve(1728, 1024).rearrange("p (a n) -> p a n", a=8)
    ld(T_WS, wsT_d.rearrange("p (a n) -> p a n", a=8), "t_ws")
    ld(T_WSS, wsTs_d.rearrange("p (a n) -> p a n", a=8), "t_wss")
    ld(T_TR, tril_d, "t_tr")
    ld(T_TRS, trils_d, "t_trs")
    ld(T_PW, poolw_d.rearrange("p (a n) -> p a n", a=8), "t_pw")

    P.op("dve", lambda e: e.tensor_tensor(out=wsT[:], in0=T_WS, in1=T_TR.unsqueeze(1).broadcast_to([128, 8, 128]), op=ALU.mult),
         reads=["t_ws", "t_tr"], writes=["wsT", "setup"])
    P.op("dve", lambda e: e.tensor_tensor(out=wsTs[:], in0=T_WSS, in1=T_TRS.unsqueeze(1).broadcast_to([128, 8, 64]), op=ALU.mult),
         reads=["t_wss", "t_trs"], writes=["wsTs", "setup2"])
    P.op("act", lambda e: e.activation(out=poolw[:], in_=T_PW, func=AF.Identity), reads=["t_pw"], writes=["poolw", "setup3"])
    P.op("dve", lambda e: e.memset(ones[:], 1.0), writes=["ones"])
    P.op("dve", lambda e: e.memset(dmy[:], 1.0), writes=["dmy"])
    P.op("dve", lambda e: e.memset(npool[:], 0.0), writes=["npool"])
    P.op("dve", lambda e: e.memset(nccv[:], 0.0), writes=["nccv"])
    P.op("dve", lambda e: e.memset(nsc[:], 0.0), writes=["nsc"])
    P.op("dve", lambda e: e.memset(nffn[:], 0.0), writes=["nffn"])

    SETUP = ["setup", "setup2", "setup3"]

    class T:
        pass

    tiles = []
    t0 = T(); t0.col0 = 0; t0.N = HALO; t0.S = 1; t0.L = HALO; t0.kind = "halo"
    t0.vblocks = [(0, 32), (32, 128), (160, 128)]
    tiles.append(t0)
    for i in range(4):
        t = T(); t.col0 = HALO + 512 * i; t.N = 512; t.S = 1; t.L = 512; t.kind = "own"; t.outcol = 512 * i
        t.vblocks = [(128 * b, 128) for b in range(4)]
        t.first = (i == 0)
        tiles.append(t)
    ts_ = T(); ts_.col0 = HALO + OWN; ts_.N = NSS * LS; ts_.S = NSS; ts_.L = LS; ts_.kind = "sample"; ts_.outcol = OWN
    ts_.vblocks = [(64 * q, 64) for q in range(NSS)]
    tiles.append(ts_)

    def v4(flat, C, S, W):
        return flat[:, 0:C * S * W].rearrange("p (c s w) -> p c s w", c=C, s=S)

    def pv(b, t):
        return ps[b][:, 0:t.N].rearrange("p (s l) -> p s l", s=t.S)

    def v3(flat2d, t):
        return flat2d.rearrange("p (s l) -> p s l", s=t.S)

    def seqsl(t):
        return slice(0, 1) if t.kind != "sample" else slice(1, 5)

    def rmsnorm(t, gain, final=False, extra_reads=(), h_extra=()):
        N = t.N
        sq = SQB.rearrange("p (c n) -> p c n", c=8)
        yo = YOUT.rearrange("p (c n) -> p c n", c=8)
        for c in range(8):
            P.op("act", lambda e, c=c: e.activation(out=sq[:, c, :N], in_=xb[:, c, :N], func=AF.Square),
                 reads=[("x", c)] + list(extra_reads), writes=[("sq", c)])
            if final:
                P.op("act", lambda e, c=c: e.activation(out=yo[:, c, :N], in_=xb[:, c, :N], func=AF.Identity, scale=gain(c)),
                     reads=[("x", c), "cst"], writes=[("yout", c)])
        import os
        NS = int(os.environ.get("DBG_NORM", "9"))
        if NS < 1:
            return
        b = next_bank()
        for k in range(8):
            P.op("pe", lambda e, k=k, b=b: e.matmul(ps[b][:, :N], ones[:, :], sq[:, k, :N], start=(k == 0), stop=(k == 7)),
                 reads=[("sq", k), "ones"], writes=[("ps", b)] if k == 0 else [("psacc_n", 0)])
        P.lastw[("ps", b)] = ("op", "pe", len(P.ops["pe"]) - 1)
        if NS < 2:
            return
        P.op("act", lambda e: e.activation(out=sd[:, :N], in_=ps[b][:, :N], func=AF.Ln, bias=EPS, scale=1.0 / D),
             reads=[("ps", b)], writes=["sd"])
        if NS < 3:
            return
        P.op("act", lambda e: e.activation(out=rstd[:, :N], in_=sd[:, :N], func=AF.Exp, scale=-0.5), reads=["sd"], writes=["rstd"])
        if NS < 4:
            return
        if t.kind == "halo":
            P.op("dve", lambda e: e.tensor_tensor(out=rstd[:, :N], in0=rstd[:, :N], in1=maskb[:, :N], op=ALU.mult),
                 reads=["rstd", "maskb"], writes=["rstd"])
        if not final:
            for c in range(8):
                P.op("dve", lambda e, c=c: e.scalar_tensor_tensor(out=hb[:, c, :N], in0=xb[:, c, :N], scalar=gain(c),
                                                                in1=rstd[:, :N], op0=ALU.mult, op1=ALU.mult),
                     reads=[("x", c), "rstd", "cst"] + list(h_extra), writes=[("h", c)])
        else:
            for c in range(8):
                P.op("dve", lambda e, c=c: e.tensor_tensor(out=yo[:, c, :N], in0=yo[:, c, :N], in1=rstd[:, :N], op=ALU.mult),
                     reads=[("yout", c), "rstd"], writes=[("yout", c)])

    HALL = [("h", c) for c in range(8)]

    def proj_group(slot, m, t, K=8, src=None, srcres=None):
        src = hb if src is None else src
        N = t.N
        return mm_group(lambda k: ring[slot][:, k, m * 128:(m + 1) * 128], lambda k: src[:, k, :N], K, 128, N,
                        reads=(HALL if srcres is None else srcres) + [("slot", slot)])

    dgbuf_all = A_DG.rearrange("p (i n) -> p i n", i=64)
    prebuilt = [None]

    pending_builds = []

    def prebuild_start(prn):
        pending_builds[:] = [(prn, j, k) for j in (0, 1) for k in range(31)]

    def prebuild_some(n):
        for _ in range(n):
            if not pending_builds:
                return
            prn, j, k = pending_builds.pop(0)
            di = j * 32 + k
            P.op("act", lambda e, prn=prn, j=j, k=k, di=di: e.activation(out=dgbuf_all[:, di, :], in_=ident[:], func=AF.Identity,
                                                                       scale=ccvw[:, prn, j, k:k + 1]),
                 reads=["cst", "ident"], writes=[("dg", j, k)])
            if not pending_builds:
                prebuilt[0] = prn

    def preload_sqrt():
        P.op("act", lambda e: e.activation(out=dmy[:, 0:1], in_=dmy[:, 1:2], func=AF.Ln), reads=["dmy"], writes=["dmy2"])

    def proj_kouter(slot, ms, t):
        N = t.N
        banks = [next_bank() for _ in ms]
        for k in range(8):
            def fn(e, k=k):
                ins = None
                for bi_, m in enumerate(ms):
                    ins = e.matmul(ps[banks[bi_]][:, :N], ring[slot][:, k, m * 128:(m + 1) * 128], hb[:, k, :N],
                                   start=(k == 0), stop=(k == 7))
                return ins
            P.op("pe", fn, reads=[("h", k), ("slot", slot)], writes=[("ps", b) for b in banks] if k == 0 else [("psacc_k", slot)])
        for b in banks:
            P.lastw[("ps", b)] = ("op", "pe", len(P.ops["pe"]) - 1)
        return banks

    def out_proj(t, wname, widx, korder=(0, 1, 2, 3, 4, 5, 6, 7)):
        N = t.N
        for half in range(2):
            s = load_piece([(wname, widx, 0, 1024, half * 512, 512, 0, 0)])
            banks = [next_bank() for _ in range(4)]
            if half == 0:
                for ki, k in enumerate(korder):
                    def fn(e, k=k, ki=ki, s=s, banks=banks):
                        ins = None
                        for mm in range(4):
                            ins = e.matmul(ps[banks[mm]][:, :N], ring[s][:, k, mm * 128:(mm + 1) * 128], yb[:, k, :N],
                                           start=(ki == 0), stop=(ki == 7))
                        return ins
                    P.op("pe", fn, reads=[("y", k), ("slot", s)], writes=[("ps", b) for b in banks] if ki == 0 else [("psacc_o", half)])
                for b in banks:
                    P.lastw[("ps", b)] = ("op", "pe", len(P.ops["pe"]) - 1)
            else:
                for mm in range(4):
                    def fn(e, mm=mm, s=s, b=banks[mm]):
                        ins = None
                        for k in range(8):
                            ins = e.matmul(ps[b][:, :N], ring[s][:, k, mm * 128:(mm + 1) * 128], yb[:, k, :N],
                                           start=(k == 0), stop=(k == 7))
                        return ins
                    P.op("pe", fn, reads=[("y", k) for k in range(8)] + [("slot", s)], writes=[("ps", banks[mm])])
            for mm in range(4):
                m = half * 4 + mm
                b = banks[mm]
                P.op("dve", lambda e, m=m, b=b: e.tensor_tensor(out=xb[:, m, :N], in0=ps[b][:, :N], in1=xb[:, m, :N], op=ALU.add),
                     reads=[("ps", b), ("x", m)], writes=[("x", m)])

    def even_mixer(t, l):
        pr = l // 2
        N, S, L = t.N, t.S, t.L
        zab = v4(A_ZAB, 4, S, 15 + L)
        zbb = v4(A_ZBB, 4, S, 30 + L)
        win = w_in_even[pr]
        if t.kind == "sample":
            P.op("act", lambda e: e.activation(out=zab[:, :, :, 0:15], in_=stpool[:, pr], func=AF.Identity),
                 reads=["rstd", "stpool", "youtdma"], writes=[("zab", m) for m in range(4)])
            P.op("act", lambda e: e.activation(out=zbb[:, :, :, 0:30], in_=stccv[:, pr], func=AF.Identity),
                 reads=["rstd", "stccv", "youtdma"], writes=[("zbb", m) for m in range(4)])
        else:
            P.op("act", lambda e: e.activation(out=zab[:, :, :, 0:15], in_=npool[:, pr, :, 0:1, :], func=AF.Identity),
                 reads=["rstd", "npool", "youtdma"], writes=[("zab", m) for m in range(4)])
            P.op("act", lambda e: e.activation(out=zbb[:, :, :, 0:30], in_=nccv[:, pr, :, 0:1, :], func=AF.Identity),
                 reads=["rstd", "nccv", "youtdma"], writes=[("zbb", m) for m in range(4)])
        dg = A_DG.rearrange("p (i n) -> p i n", i=64)
        zbh = v4(A_ZBH, 4, S, 30 + L)

        def build_diags(j, ks):
            for k in ks:
                di = (j % 2) * 32 + k
                if k % 2 == 0:
                    P.op("act", lambda e, j=j, k=k, di=di: e.activation(out=dg[:, di, :], in_=ident[:], func=AF.Identity,
                                                                       scale=ccvw[:, pr, j, k:k + 1]),
                         reads=["cst", "ident", "rstd"], writes=[("dg", j % 2, k)])
                else:
                    P.op("dve", lambda e, j=j, k=k, di=di: e.tensor_scalar(out=dg[:, di, :], in0=ident[:], scalar1=ccvw[:, pr, j, k:k + 1],
                                                                         scalar2=None, op0=ALU.mult),
                         reads=["cst", "ident", "rstd"], writes=[("dg", j % 2, k)])
        s0 = load_piece([("w_in_even", pr, 0, 1024, 0, 512, 0, 0)])
        sa = load_piece([("w_in_even", pr, 0, 1024, 512, 512, 0, 0)])
        sg = load_piece([("w_in_even", pr, 0, 1024, 1024, 512, 0, 0)])
        zbanks = proj_kouter(s0, [0, 1, 2, 3], t)
        for m in range(4):
            b = zbanks[m]
            P.op("act", lambda e, m=m, b=b: e.activation(out=zab[:, m, :, 15:15 + L], in_=pv(b, t), func=AF.Identity),
                 reads=[("ps", b)], writes=[("zab", m)])
        sig = A_SIG.rearrange("p (i n) -> p i n", i=2)
        for m in range(4):
            ba = proj_group(sa, m, t)
            bg = proj_group(sg, m, t)
            i = m % 2
            P.op("act", lambda e, i=i, bg=bg: e.activation(out=sig[:, i, :N], in_=ps[bg][:, :N], func=AF.Sigmoid),
                 reads=[("ps", bg)], writes=[("sig", i)])
            if m == 3:
                preload_sqrt()
            P.op("dve", lambda e, m=m, i=i, ba=ba: e.tensor_tensor(out=zbb[:, m, :, 30:30 + L], in0=pv(ba, t), in1=v3(sig[:, i, :N], t), op=ALU.mult),
                 reads=[("ps", ba), ("sig", i)], writes=[("zbb", m)])
            P.op("act", lambda e, m=m: e.activation(out=zbh[:, m], in_=zbb[:, m], func=AF.Identity),
                 reads=[("zbb", m)], writes=[("zbh", m)])
            if prebuilt[0] != pr:
                build_diags(0, range(8 * m, min(31, 8 * m + 8)))
        if prebuilt[0] != pr:
            build_diags(1, range(31))
        prebuilt[0] = None
        P.op("act", lambda e: e.activation(out=npool[:, pr, :, seqsl(t), :], in_=zab[:, :, :, L:L + 15], func=AF.Identity),
             reads=[("zab", m) for m in range(4)], writes=["npool"])
        P.op("act", lambda e: e.activation(out=nccv[:, pr, :, seqsl(t), :], in_=zbb[:, :, :, L:L + 30], func=AF.Identity),
             reads=[("zbb", m) for m in range(4)], writes=["nccv"])
        W = 15 + L
        T1 = A_T1[:, 0:S * W].rearrange("p (s w) -> p s w", s=S)
        T2 = A_T2[:, 0:S * W].rearrange("p (s w) -> p s w", s=S)
        dB = A_D.rearrange("p (c n) -> p c n", c=4)
        for g in range(4):
            z = zab[:, g]
            cur = z
            curres = ("zab", g)
            bufs = [(T1, "T1"), (T2, "T2")]
            sh = 1
            for step in range(g + 1):
                dst, dres = bufs[step % 2]
                lo = 2 * sh - 1
                P.op("dve", lambda e, dst=dst, cur=cur, lo=lo, sh=sh: e.tensor_tensor(
                    out=dst[:, :, lo:W], in0=cur[:, :, lo:W], in1=cur[:, :, lo - sh:W - sh], op=ALU.add),
                    reads=[curres], writes=[dres])
                cur, curres = dst, dres
                sh *= 2
            wlen = 2 ** (g + 1)
            P.op("dve", lambda e, g=g, cur=cur, z=z, wlen=wlen: e.scalar_tensor_tensor(
                out=v3(dB[:, g, :N], t), in0=cur[:, :, 15:W], scalar=1.0 / wlen, in1=z[:, :, 15:W],
                op0=ALU.mult, op1=ALU.subtract),
                reads=[curres, ("zab", g)], writes=[("d", g)])
            if t.kind == "own" and t.first:
                other = bufs[(g + 1) % 2]
                P.op("dve", lambda e, g=g, cur=cur, other=other: e.tensor_tensor(
                    out=other[0][:, 0, 0:16], in0=cur[:, 0, 15:31], in1=invc[:, g, :], op=ALU.mult),
                    reads=[curres, "invc"], writes=[other[1]])
                P.op("dve", lambda e, g=g, z=z, other=other: e.tensor_tensor(
                    out=dB[:, g, 0:16], in0=other[0][:, 0, 0:16], in1=z[:, 0, 15:31], op=ALU.subtract),
                    reads=[other[1], ("zab", g)], writes=[("d", g)])
        cb = A_CB.rearrange("p (c n) -> p c n", c=4)
        dg = A_DG.rearrange("p (i n) -> p i n", i=64)
        cbb = E_CBB.rearrange("p (c n) -> p c n", c=4)
        sqb = SQB.rearrange("p (c n) -> p c n", c=8)
        b1 = next_bank()
        b2 = next_bank()
        reserved_banks.update((b1, b2))

        def ln_stats(j):
            P.op("pe", lambda e, j=j: e.matmul(ps[b1][:, :N], ones[:, :], cbb[:, j, :N], start=(j == 0), stop=(j == 3)),
                 reads=[("cbb", j), "ones"], writes=[("ps", b1)] if j == 0 else [("psacc_l", 1)])
            P.op("pe", lambda e, j=j: e.matmul(ps[b2][:, :N], ones[:, :], sqb[:, j, :N], start=(j == 0), stop=(j == 3)),
                 reads=[("lsq", j), "ones"], writes=[("ps", b2)] if j == 0 else [("psacc_l", 2)])
            if j == 3:
                P.lastw[("ps", b1)] = ("op", "pe", len(P.ops["pe"]) - 2)
                P.lastw[("ps", b2)] = ("op", "pe", len(P.ops["pe"]) - 1)
        for j in range(4):
            b = next_bank()
            for k in range(31):
                di = (j % 2) * 32 + k
                P.op("pe", lambda e, j=j, k=k, di=di, b=b: e.matmul(pv(b, t), dg[:, di, :], zbh[:, j, :, k:k + L],
                                                                  start=(k == 0), stop=(k == 30)),
                     reads=[("dg", j % 2, k), ("zbh", j)], writes=[("ps", b)] if k == 0 else [("psacc_c", j)])
            P.lastw[("ps", b)] = ("op", "pe", len(P.ops["pe"]) - 1)
            P.op("act", lambda e, j=j, b=b: e.activation(out=cb[:, j, :N], in_=ps[b][:, :N], func=AF.Identity, bias=ccvb[:, pr, j:j + 1]),
                 reads=[("ps", b), "cst"], writes=[("cb", j)])
            P.op("act", lambda e, j=j, b=b: e.activation(out=cbb[:, j, :N], in_=ps[b][:, :N], func=AF.Identity, bias=ccvb[:, pr, j:j + 1]),
                 reads=[("ps", b), "cst"], writes=[("cbb", j)])
            P.op("act", lambda e, j=j, b=b: e.activation(out=sqb[:, j, :N], in_=ps[b][:, :N], func=AF.Square, bias=ccvb[:, pr, j:j + 1]),
                 reads=[("ps", b), "cst"] + [("zab", m) for m in range(4)] + [("d", g) for g in range(4)] + ["T1", "T2", "npool"],
                 writes=[("lsq", j), ("zab", 0), ("zab", 1)])
            if j + 2 < 4:
                build_diags(j + 2, range(31))
            if j >= 1:
                ln_stats(j - 1)
        for g in range(4):
            b = mm_group(lambda k, g=g: poolw[:, pr * 4 + g, :], lambda k, g=g: dB[:, g, :N], 1, 128, N,
                         reads=[("d", g), "poolw"])
            P.op("act", lambda e, g=g, b=b: e.activation(out=yb[:, g, :N], in_=ps[b][:, :N], func=AF.Identity, scale=pscale[:, pr, g:g + 1]),
                 reads=[("ps", b), "cst"], writes=[("y", g)])
        ln_stats(3)
        reserved_banks.difference_update((b1, b2))
        P.op("act", lambda e: e.activation(out=A_MEAN[:, :N], in_=ps[b1][:, :N], func=AF.Identity, scale=1.0 / 512),
             reads=[("ps", b1)], writes=["mean"])
        P.op("act", lambda e: e.activation(out=A_MSQ[:, :N], in_=ps[b1][:, :N], func=AF.Square, scale=1.0 / 512),
             reads=[("ps", b1)], writes=["msq"])
        P.op("dve", lambda e: e.scalar_tensor_tensor(out=A_LRS[:, :N], in0=ps[b2][:, :N], scalar=1.0 / 512, in1=A_MSQ[:, :N],
                                                    op0=ALU.mult, op1=ALU.subtract),
             reads=[("ps", b2), "msq"], writes=["lrs"])
        P.op("act", lambda e: e.activation(out=A_MSQ[:, :N], in_=A_LRS[:, :N], func=AF.Ln, bias=EPS, scale=1.0),
             reads=["lrs"], writes=["msq"])
        P.op("act", lambda e: e.activation(out=A_LRS[:, :N], in_=A_MSQ[:, :N], func=AF.Exp, scale=-0.5), reads=["msq"], writes=["lrs"])
        tt = A_TT.rearrange("p (i n) -> p i n", i=2)
        for j in range(4):
            i = j % 2
            P.op("dve", lambda e, j=j, i=i: e.tensor_tensor(out=tt[:, i, :N], in0=cb[:, j, :N], in1=A_MEAN[:, :N], op=ALU.subtract),
                 reads=[("cb", j), "mean"], writes=[("tt", i)])
            P.op("dve", lambda e, j=j, i=i: e.tensor_tensor(out=tt[:, i, :N], in0=tt[:, i, :N], in1=A_LRS[:, :N], op=ALU.mult),
                 reads=[("tt", i), "lrs"], writes=[("tt", i)])
            P.op("act", lambda e, j=j, i=i: e.activation(out=yb[:, 4 + j, :N], in_=tt[:, i, :N], func=AF.Silu,
                                                       bias=ccvbt[:, pr, j:j + 1], scale=ccvg[:, pr, j:j + 1]),
                 reads=[("tt", i), "cst"], writes=[("y", 4 + j)])
        preload_sqrt()
        out_proj(t, "w_out_even", pr)

    def odd_mixer(t, l):
        pr = l // 2
        N, S, L = t.N, t.S, t.L
        win = w_in_odd[pr]
        cx = v4(O_CX, 4, S, 2 + L)
        hist_src = stsc[:, pr] if t.kind == "sample" else nsc[:, pr, :, 0:1, :]
        P.op("act", lambda e: e.activation(out=cx[:, :, :, 0:2], in_=hist_src, func=AF.Identity),
             reads=["rstd", "stsc", "nsc", "youtdma"], writes=[("cx", j) for j in range(4)])
        if not (t.kind == "sample" and l == 3):
            prebuild_start((pr + 1) % 2)
        su = load_piece([("w_in_odd", pr, 0, 1024, 0, 512, 0, 0)])
        sv = load_piece([("w_in_odd", pr, 0, 1024, 512, 512, 0, 0)])
        vn = O_VN.rearrange("p (b n) -> p b n", b=4)
        v32 = O_V32.rearrange("p (b n) -> p b n", b=4)
        vt = O_VT.rearrange("p (i n) -> p i n", i=2)
        bn = O_BN
        vbanks = [next_bank() for _ in t.vblocks]
        for k in range(8):
            def fnv(e, k=k):
                ins = None
                for bi, (c0, nb) in enumerate(t.vblocks):
                    ins = e.matmul(ps[vbanks[bi]][:nb, :], hb[:, k, c0:c0 + nb], ring[sv][:, k, :], start=(k == 0), stop=(k == 7))
                return ins
            P.op("pe", fnv, reads=[("h", k), ("slot", sv)], writes=[("ps", b) for b in vbanks] if k == 0 else [("psacc_v", 0)])
        for b in vbanks:
            P.lastw[("ps", b)] = ("op", "pe", len(P.ops["pe"]) - 1)
        for bi, (c0, nb) in enumerate(t.vblocks):
            b = vbanks[bi]
            i = bi % 2
            P.op("dve", lambda e, b=b, nb=nb: e.bn_stats(out=bn[:nb, 0:6], in_=ps[b][:nb, :]), reads=[("ps", b)], writes=["bn6"])
            P.op("dve", lambda e, nb=nb: e.bn_aggr(out=bn[:nb, 8:10], in_=bn[:nb, 0:6]), reads=["bn6"], writes=["bnmv"])
            P.op("act", lambda e, nb=nb: e.activation(out=bn[:nb, 10:11], in_=bn[:nb, 9:10], func=AF.Ln, bias=EPS, scale=1.0),
                 reads=["bnmv"], writes=["bnsd"])
            P.op("act", lambda e, nb=nb: e.activation(out=bn[:nb, 11:12], in_=bn[:nb, 10:11], func=AF.Exp, scale=-0.5),
                 reads=["bnsd"], writes=["bnrs"])
            P.op("dve", lambda e, b=b, nb=nb, i=i: e.tensor_scalar(out=vt[:nb, i, :], in0=ps[b][:nb, :], scalar1=bn[:nb, 8:9],
                                                                  scalar2=bn[:nb, 11:12], op0=ALU.subtract, op1=ALU.mult),
                 reads=[("ps", b), "bnmv", "bnrs"], writes=[("vt", i)])
            P.op("dve", lambda e, nb=nb, i=i: e.tensor_tensor(out=vt[:nb, i, :], in0=vt[:nb, i, :], in1=sgug[:nb, pr, :], op=ALU.mult),
                 reads=[("vt", i), "sgug"], writes=[("vt", i)])
            if t.kind == "sample":
                P.op("dve", lambda e, nb=nb, i=i, bi=bi: e.tensor_tensor(out=v32[:nb, bi, :], in0=vt[:nb, i, :], in1=sgubt[:nb, pr, :], op=ALU.add),
                     reads=[("vt", i), "sgubt"], writes=[("v32", bi)])
                P.op("act", lambda e, nb=nb, bi=bi: e.activation(out=vn[:nb, bi, :], in_=v32[:nb, bi, :], func=AF.Identity),
                     reads=[("v32", bi)], writes=[("vn", bi)])
                sem = dsem("ov")

                def fn(e, bi=bi, sem=sem, nb=nb):
                    e.dma_start(out=o_v[pr, bi], in_=v32[:nb, bi, :]).then_inc(dmasems[sem], 16)
                    return None
                P.op("sp", fn, reads=[("v32", bi)], dma_sem=sem)
            else:
                P.op("dve", lambda e, nb=nb, i=i, bi=bi: e.tensor_tensor(out=vn[:nb, bi, :], in0=vt[:nb, i, :], in1=sgubt[:nb, pr, :], op=ALU.add),
                     reads=[("vt", i), "sgubt"], writes=[("vn", bi)])
        u = O_U.rearrange("p (c n) -> p c n", c=4)
        for m in range(4):
            b = proj_group(su, m, t)
            P.op("act", lambda e, m=m, b=b: e.activation(out=u[:, m, :N], in_=ps[b][:, :N], func=AF.Identity),
                 reads=[("ps", b)], writes=[("u", m)])
            prebuild_some(8)
        sb_ = load_piece([("w_in_odd", pr, 0, 1024, 1024, 512, 0, 0)])
        sc_ = load_piece([("w_in_odd", pr, 0, 1024, 1536, 512, 0, 0)])
        sx_ = load_piece([("w_in_odd", pr, 0, 1024, 2048, 512, 0, 0)])
        xin = O_XIN.rearrange("p (i n) -> p i n", i=2)
        bgs = O_BG.rearrange("p (i n) -> p i n", i=2)
        cz = O_CZ.rearrange("p (i n) -> p i n", i=2)
        for j in range(4):
            i = j % 2
            bx = proj_group(sx_, j, t)
            bc = proj_group(sc_, j, t)
            bb = proj_group(sb_, j, t)
            P.op("act", lambda e, i=i, bx=bx: e.activation(out=xin[:, i, :N], in_=ps[bx][:, :N], func=AF.Identity),
                 reads=[("ps", bx)], writes=[("xin", i)])
            P.op("dve", lambda e, j=j, i=i, bc=bc: e.tensor_tensor(out=cx[:, j, :, 2:2 + L], in0=pv(bc, t), in1=v3(xin[:, i, :N], t), op=ALU.mult),
                 reads=[("ps", bc), ("xin", i)], writes=[("cx", j)])
            P.op("act", lambda e, i=i, bb=bb: e.activation(out=bgs[:, i, :N], in_=ps[bb][:, :N], func=AF.Identity),
                 reads=[("ps", bb)], writes=[("bgs", i)])
            prebuild_some(8)
            P.op("dve", lambda e, j=j, i=i: e.tensor_scalar(out=v3(cz[:, i, :N], t), in0=cx[:, j, :, 2:2 + L], scalar1=scw[:, pr, j, 2:3],
                                                          scalar2=None, op0=ALU.mult),
                 reads=[("cx", j), "cst"], writes=[("cz", i)])
            for k in (1, 0):
                P.op("dve", lambda e, j=j, i=i, k=k: e.scalar_tensor_tensor(out=v3(cz[:, i, :N], t), in0=cx[:, j, :, k:k + L],
                                                                           scalar=scw[:, pr, j, k:k + 1], in1=v3(cz[:, i, :N], t),
                                                                           op0=ALU.mult, op1=ALU.add),
                     reads=[("cx", j), ("cz", i), "cst"], writes=[("cz", i)])
            P.op("dve", lambda e, j=j, i=i: e.tensor_tensor(out=yb[:, 4 + j, :N], in0=cz[:, i, :N], in1=bgs[:, i, :N], op=ALU.mult),
                 reads=[("cz", i), ("bgs", i)], writes=[("y", 4 + j)])
        mt = O_MT.rearrange("p (i n) -> p i n", i=2)
        for j in range(4):
            b = next_bank()
            segs = []
            if t.kind == "sample":
                for q in range(NSS):
                    segs.append((q, 0, 64, q * 64, False))
            else:
                for bi, (c0, nb) in enumerate(t.vblocks):
                    segs.append((bi, 0, nb, c0, False))

            def fn(e, j=j, b=b, segs=segs):
                ins = None
                for (bi, po, ln, c0, smp) in segs:
                    rhs = wsTs[po:po + ln, pr * 4 + j, 0:ln] if smp else wsT[0:ln, pr * 4 + j, 0:ln]
                    ins = e.matmul(ps[b][:, c0:c0 + ln], vn[po:po + ln, bi, j * 128:(j + 1) * 128], rhs, start=True, stop=True)
                return ins
            P.op("pe", fn, reads=[("vn", bi) for bi in range(len(t.vblocks))] + ["wsT", "wsTs"], writes=[("ps", b)])
            i = j % 2
            if t.kind == "sample":
                groups = [(0, NSS, 64)]
            elif t.kind == "halo":
                groups = [(0, 1, 32), (32, 2, 128)]
            else:
                groups = [(0, 4, 128)]
            for gi, (c0, nblk, bl) in enumerate(groups):
                P.op("dve", lambda e, j=j, b=b, i=i, c0=c0, nblk=nblk, bl=bl: e.tensor_tensor(
                    out=mt[:, i, c0:c0 + nblk * bl].rearrange("p (a n) -> p a n", a=nblk),
                    in0=ps[b][:, c0:c0 + nblk * bl].rearrange("p (a n) -> p a n", a=nblk),
                    in1=sgubias[:, pr * 4 + j, 0:bl].unsqueeze(1).broadcast_to([128, nblk, bl]), op=ALU.add),
                    reads=[("ps", b), "sgubias"], writes=[("mt", i)] if gi == 0 else [("mt", i)])
            P.op("dve", lambda e, j=j, i=i: e.tensor_tensor(out=yb[:, j, :N], in0=mt[:, i, :N], in1=u[:, j, :N], op=ALU.mult),
                 reads=[("mt", i), ("u", j)], writes=[("y", j)])
        P.op("act", lambda e: e.activation(out=nsc[:, pr, :, seqsl(t), :], in_=cx[:, :, :, L:L + 2], func=AF.Identity),
             reads=[("cx", j) for j in range(4)], writes=["nsc"])
        prebuild_some(64)
        out_proj(t, "w_out_odd", pr, korder=(4, 5, 6, 7, 0, 1, 2, 3))

    def conv_ffn(t, l):
        N, S, L = t.N, t.S, t.L
        wup = ffn_w_up[l]
        wdn = ffn_w_down[l]
        gB = F_G.rearrange("p (c n) -> p c n", c=22)
        sbuf_ = F_S.rearrange("p (i n) -> p i n", i=2)
        ptb = F_PT.rearrange("p (i n) -> p i n", i=2)
        trim = (t.kind == "halo" and l == 3)
        d0 = dict(banks=[] if trim else [next_bank() for _ in range(4)])
        LAG = 2

        def down0_mini(u):
            banks = d0["banks"]
            if u == 0:
                reserved_banks.update(banks)
            kk, nk = 2 * u, 2
            s = load_piece([("ffn_w_down", l, kk * 128, nk * 128, 0, 512, 0, 0)])
            for k in range(nk):
                kg = kk + k

                def fn(e, s=s, k=k, kg=kg, banks=banks):
                    ins = None
                    for mm in range(4):
                        ins = e.matmul(ps[banks[mm]][:, :N], ring[s][:, k, mm * 128:(mm + 1) * 128], gB[:, kg, :N],
                                       start=(kg == 0), stop=(kg == 21))
                    return ins
                P.op("pe", fn, reads=[("g", kg), ("slot", s)], writes=[("ps", bq) for bq in banks] if kg == 0 else [("psacc", 0)])
            if u == NU - 1:
                for mm in range(4):
                    P.lastw[("ps", banks[mm])] = ("op", "pe", len(P.ops["pe"]) - 1)
        for ui in range(NU):
            s = load_piece([("ffn_w_up", l, 0, 1024, 256 * ui, 256, 0, 0), ("ffn_w_up", l, 0, 1024, DFF + 256 * ui, 256, 0, 256)])
            ub = ui % 2
            up = F_UP[:, ub * 2056: ub * 2056 + 4 * S * (2 + L)].rearrange("p (c s w) -> p c s w", c=4, s=S)
            hist_src = stffn[:, l, 4 * ui:4 * ui + 4] if t.kind == "sample" else nffn[:, l, 4 * ui:4 * ui + 4, 0:1, :]
            P.op("act", lambda e, up=up, hist_src=hist_src: e.activation(out=up[:, :, :, 0:2], in_=hist_src, func=AF.Identity),
                 reads=["rstd", "stffn", "nffn", "youtdma"], writes=[("up", ub, q) for q in range(4)])
            ubanks = proj_kouter(s, [0, 1, 2, 3], t) if ui == 0 else None
            for pp in range(2):
                cbuf = F_C[:, pp * 1024:(pp + 1) * 1024].rearrange("p (i n) -> p i n", i=2)
                banks = []
                for half in range(2):
                    q = half * 2 + pp
                    b = ubanks[q] if ui == 0 else proj_group(s, q, t)
                    banks.append(b)
                    ch = 4 * ui + q
                    P.op("act", lambda e, up=up, q=q, b=b: e.activation(out=up[:, q, :, 2:2 + L], in_=pv(b, t), func=AF.Identity),
                         reads=[("ps", b)], writes=[("up", ub, q)])
                    if trim:
                        continue
                    P.op("act", lambda e, cbuf=cbuf, half=half, b=b, ch=ch: e.activation(
                        out=cbuf[:, half, :N], in_=ps[b][:, :N], func=AF.Identity, bias=fcb[:, l, ch:ch + 1], scale=fcw[:, l, ch, 2:3]),
                        reads=[("ps", b), "cst"], writes=[("c", pp, half)])
                    P.op("dve", lambda e, cbuf=cbuf, half=half, up=up, q=q, ch=ch: e.scalar_tensor_tensor(
                        out=v3(cbuf[:, half, :N], t), in0=up[:, q, :, 1:1 + L], scalar=fcw[:, l, ch, 1:2],
                        in1=v3(cbuf[:, half, :N], t), op0=ALU.mult, op1=ALU.add),
                        reads=[("up", ub, q), ("c", pp, half), "cst"], writes=[("c", pp, half)])
                    P.op("dve", lambda e, cbuf=cbuf, half=half, up=up, q=q, ch=ch: e.scalar_tensor_tensor(
                        out=v3(cbuf[:, half, :N], t), in0=up[:, q, :, 0:L], scalar=fcw[:, l, ch, 0:1],
                        in1=v3(cbuf[:, half, :N], t), op0=ALU.mult, op1=ALU.add),
                        reads=[("up", ub, q), ("c", pp, half), "cst"], writes=[("c", pp, half)])
                if trim:
                    continue
                P.op("act", lambda e, cbuf=cbuf, pp=pp: e.activation(out=sbuf_[:, pp, :N], in_=cbuf[:, 0, :N], func=AF.Silu),
                     reads=[("c", pp, 0)], writes=[("s", pp)])
                if ui == NU - 1 and pp == 1:
                    preload_sqrt()
                gch = 2 * ui + pp
                P.op("dve", lambda e, cbuf=cbuf, pp=pp, gch=gch: e.tensor_tensor(out=gB[:, gch, :N], in0=sbuf_[:, pp, :N], in1=cbuf[:, 1, :N], op=ALU.mult),
                     reads=[("s", pp), ("c", pp, 1)], writes=[("g", gch)])
            P.op("act", lambda e, up=up, ui=ui: e.activation(out=nffn[:, l, 4 * ui:4 * ui + 4, seqsl(t), :], in_=up[:, :, :, L:L + 2], func=AF.Identity),
                 reads=[("up", ub, q) for q in range(4)], writes=["nffn"])
            if ui >= LAG and not trim:
                down0_mini(ui - LAG)
        if trim:
            return
        for u in range(NU - LAG, NU):
            down0_mini(u)
        for half in range(2):
            if half == 0:
                banks = d0["banks"]
                reserved_banks.difference_update(banks)
            else:
                slots = []
                kk = 0
                for pi, nk in enumerate((8, 8, 6)):
                    slots.append((load_piece([("ffn_w_down", l, kk * 128, nk * 128, half * 512, 512, 0, 0)]), kk, nk))
                    kk += nk
                banks = []
                for mm in range(4):
                    b = next_bank()
                    banks.append(b)

                    def fn(e, mm=mm, b=b, slots=slots):
                        ins = None
                        for (s, k0, nk) in slots:
                            for k in range(nk):
                                kg = k0 + k
                                ins = e.matmul(ps[b][:, :N], ring[s][:, k, mm * 128:(mm + 1) * 128], gB[:, kg, :N],
                                               start=(kg == 0), stop=(kg == 21))
                        return ins
                    P.op("pe", fn, reads=[("g", kg) for kg in range(22)] + [("slot", s) for (s, _, _) in slots], writes=[("ps", b)])
            for mm in range(4):
                m = half * 4 + mm
                b = banks[mm]
                P.op("dve", lambda e, m=m, b=b: e.tensor_tensor(out=xb[:, m, :N], in0=ps[b][:, :N], in1=xb[:, m, :N], op=ALU.add),
                     reads=[("ps", b), ("x", m)], writes=[("x", m)])

    first = True
    active = [t for ti, t in enumerate(tiles) if dbg_tiles is None or ti in dbg_tiles]

    def plan_xload(t):
        for c in range(8):
            sem = dsem(f"xin{c}")

            def fnx(e, t=t, sem=sem, c=c):
                e.dma_start(out=xb[:, c, :t.N], in_=xT[:, c, t.col0:t.col0 + t.N]).then_inc(dmasems[sem], 16)
                return None
            P.op("sp", fnx, writes=[("x", c)], dma_sem=sem)

    for ai, t in enumerate(active):
        N = t.N
        if ai == 0:
            plan_xload(t)
        for l in range(dbg_layers):
            rmsnorm(t, lambda c, l=l: gmix[:, l, c:c + 1], extra_reads=SETUP if first else (),
                    h_extra=["youtdma"] if l == 0 else ())
            first = False
            if dbg_stage < 1:
                continue
            if l % 2 == 0:
                even_mixer(t, l)
            else:
                odd_mixer(t, l)
            if dbg_stage < 2:
                continue
            rmsnorm(t, lambda c, l=l: gffn[:, l, c:c + 1])
            conv_ffn(t, l)
        if t.kind != "halo":
            rmsnorm(t, lambda c: gfin[:, c:c + 1], final=True)
        if ai + 1 < len(active):
            plan_xload(active[ai + 1])
        if t.kind != "halo":
            yo = YOUT.rearrange("p (c n) -> p c n", c=8)
            sem = dsem("yout")

            def fny(e, t=t, sem=sem):
                e.dma_start(out=yT[:, :, t.outcol:t.outcol + t.N], in_=yo[:, :, :t.N]).then_inc(dmasems[sem], 16)
                return None
            ev = P.op("sp", fny, reads=[("yout", c) for c in range(8)], dma_sem=sem)
            P.lastw["youtdma"] = ev

    sem = dsem("fin")
    for dst, srcb, res in ((o_pool, npool, "npool"), (o_ccv, nccv, "nccv"), (o_sc, nsc, "nsc"), (o_ffn, nffn, "nffn")):
        def fno(e, dst=dst, srcb=srcb, sem=sem):
            e.dma_start(out=dst, in_=srcb[:].rearrange("p a c s k -> p (a c s k)")).then_inc(dmasems[sem], 16)
            return None
        P.op("sp", fno, reads=[res], dma_sem=sem)

    finals = [("sem", n, P.dmasem_count[n]) for n in ("fin", "yout", "ov") if n in P.dmasem_count]


    if record:
        es.close()
        return pieces
    P.resolve()

    with nc.Block() as block:
        @block.tensor
        def _(e):
            P.emit("pe", e, engsem, dmasems)

        @block.scalar
        def _(e):
            P.emit("act", e, engsem, dmasems)

        @block.vector
        def _(e):
            P.emit("dve", e, engsem, dmasems)

        @block.gpsimd
        def _(e):
            P.emit("pool", e, engsem, dmasems)

        @block.sync
        def _(e):
            P.emit("sp", e, engsem, dmasems)
            for (_, n, v) in finals:
                e.wait_ge(dmasems[n], v)
    es.close()
    return nc


_NC_CACHE = {}


def _fm(a):
    a = np.asarray(a)
    C = a.shape[-1]
    T_ = a.shape[-2]
    lead = a.shape[:-2]
    b = a.reshape(lead + (T_, C // 128, 128))
    nd = b.ndim
    perm = (nd - 1,) + tuple(range(len(lead))) + (nd - 2, nd - 3)
    return np.ascontiguousarray(b.transpose(perm))


def kernel(x_prompt, x_sample, state_pool, state_ccv, state_sconv, state_ffn_conv,
           norm_mix_g, norm_ffn_g, norm_final_g,
           w_in_even, pool_w, pool_scale, ccv_w, ccv_b, ccv_ln_g, ccv_ln_b, w_out_even,
           w_in_odd, sgu_ln_g, sgu_ln_b, sgu_ws, sgu_b, sconv_w, w_out_odd,
           ffn_w_up, ffn_conv_w, ffn_conv_b, ffn_w_down):
    f32 = np.float32
    A = lambda a: np.ascontiguousarray(np.asarray(a, dtype=f32))
    x_prompt = A(x_prompt); x_sample = A(x_sample)
    perm = _unit_perm()

    def vec_fm(v):
        v = A(v)
        lead = v.shape[:-1]
        b = v.reshape(lead + (v.shape[-1] // 128, 128))
        nd = b.ndim
        return np.ascontiguousarray(b.transpose((nd - 1,) + tuple(range(nd - 1))))

    cst = np.zeros((128, CSTN), f32)

    def put(name, arr):
        o, s = CST[name]
        cst[:, o:o + s] = arr.reshape(128, s)
    put("gmix", vec_fm(norm_mix_g))
    put("gffn", vec_fm(norm_ffn_g))
    put("gfin", vec_fm(norm_final_g))
    put("pscale", vec_fm(pool_scale))
    put("ccvw", vec_fm(ccv_w).transpose(0, 1, 3, 2))
    put("ccvb", vec_fm(ccv_b))
    put("ccvg", vec_fm(ccv_ln_g))
    put("ccvbt", vec_fm(ccv_ln_b))
    put("scw", vec_fm(sconv_w).transpose(0, 1, 3, 2))
    fw = vec_fm(ffn_conv_w)[:, :, :, perm]
    put("fcw", fw.transpose(0, 1, 3, 2))
    put("fcb", vec_fm(ffn_conv_b)[:, :, perm])

    ws = A(sgu_ws)
    wsT = np.ascontiguousarray(ws.transpose(3, 0, 1, 2)).reshape(128, 2 * 4 * 128)
    wsTs = np.ascontiguousarray(ws[:, :, :64, :64].transpose(3, 0, 1, 2))
    wsTs = np.concatenate([wsTs, wsTs], axis=0).reshape(128, 2 * 4 * 64)
    tril = (np.arange(128)[None, :] >= np.arange(128)[:, None]).astype(f32)
    trils = np.concatenate([tril[:64, :64], tril[:64, :64]], axis=0)
    poolw = np.ascontiguousarray(A(pool_w).transpose(2, 0, 1, 3)).reshape(128, 2 * 4 * 128)

    shared = dict(
        cst=cst, sgug=A(sgu_ln_g).reshape(-1), sgubt=A(sgu_ln_b).reshape(-1), sgubias=A(sgu_b).reshape(-1),
        wsT=wsT, wsTs=wsTs, tril=tril, ident=np.eye(128, dtype=f32), trils=np.ascontiguousarray(trils), poolw=poolw,
        w_in_even=A(w_in_even), w_out_even=A(w_out_even), w_in_odd=A(w_in_odd), w_out_odd=A(w_out_odd),
        ffn_w_up=A(ffn_w_up), ffn_w_down=A(ffn_w_down),
    )

    xpT = _fm(x_prompt[0])
    xsT = _fm(x_sample)
    sp_fm = _fm(A(state_pool))
    scv_fm = _fm(A(state_ccv))
    ssc_fm = _fm(A(state_sconv))
    sff_fm = _fm(A(state_ffn_conv))[:, :, :, perm, :]

    in_maps = []
    for c in range(NCORES):
        a = c * OWN
        xT = np.zeros((128, 8, TOK), f32)
        if c > 0:
            xT[:, :, 0:HALO] = xpT[:, :, a - HALO:a]
        xT[:, :, HALO:HALO + OWN] = xpT[:, :, a:a + OWN]
        xs = xsT[:, NSS * c:NSS * c + NSS]
        xT[:, :, HALO + OWN:] = xs.transpose(0, 2, 1, 3).reshape(128, 8, NSS * LS)
        mask = np.full((HALO,), 0.0 if c == 0 else 1.0, f32)
        invc = np.zeros((4, 16), f32)
        for g, w in enumerate((2, 4, 8, 16)):
            if c == 0:
                invc[g] = 1.0 / np.minimum(np.arange(16) + 1, w)
            else:
                invc[g] = 1.0 / w
        sl = slice(NSS * c, NSS * c + NSS)
        m = dict(shared)
        m.update(
            xT=xT, mask=mask, invc=invc.reshape(-1),
            st_pool=np.ascontiguousarray(sp_fm[:, :, sl].transpose(0, 1, 3, 2, 4)).reshape(128, -1),
            st_ccv=np.ascontiguousarray(scv_fm[:, :, sl].transpose(0, 1, 3, 2, 4)).reshape(128, -1),
            st_sc=np.ascontiguousarray(ssc_fm[:, :, sl].transpose(0, 1, 3, 2, 4)).reshape(128, -1),
            st_ffn=np.ascontiguousarray(sff_fm[:, :, sl].transpose(0, 1, 3, 2, 4)).reshape(128, -1),
        )
        in_maps.append(m)

    if "nc" not in _NC_CACHE:
        _NC_CACHE["nc"] = build_program()
    nc = _NC_CACHE["nc"]
    res = run_bass_kernel_spmd(nc, in_maps, core_ids=list(range(NCORES)))
    R = res.results

    def tm(a):
        return np.ascontiguousarray(a.transpose(2, 1, 0).reshape(a.shape[2], -1))

    y_prompt = np.zeros((1, SEQ, D), f32)
    y_sample = np.zeros((32, LS, D), f32)
    for c in range(NCORES):
        yT = R[c]["yT"]
        y_prompt[0, c * OWN:(c + 1) * OWN] = tm(yT[:, :, :OWN])
        ys = yT[:, :, OWN:].reshape(128, 8, NSS, LS)
        for q in range(NSS):
            y_sample[NSS * c + q] = tm(ys[:, :, q, :])

    def states(key, npair, C, H, permute=None):
        pp = np.zeros((npair, 1, H, C * 128), f32)
        ss = np.zeros((npair, 32, H, C * 128), f32)
        for c in range(NCORES):
            a = R[c][key].reshape(128, npair, C, 5, H)
            if permute is not None:
                b = np.zeros_like(a)
                b[:, :, permute] = a
                a = b
            full = a.transpose(1, 3, 4, 2, 0).reshape(npair, 5, H, C * 128)
            if c == NCORES - 1:
                pp[:, 0] = full[:, 0]
            ss[:, NSS * c:NSS * c + NSS] = full[:, 1:5]
        return pp, ss

    pool_p, pool_s = states("o_pool", 2, 4, 15)
    ccv_p, ccv_s = states("o_ccv", 2, 4, 30)
    sc_p, sc_s = states("o_sc", 2, 4, 2)
    ffn_p, ffn_s = states("o_ffn", 4, 44, 2, permute=perm)
    v_s = np.zeros((2, 32, LS, 512), f32)
    for c in range(NCORES):
        v_s[:, NSS * c:NSS * c + NSS] = R[c]["o_v"]
    return (y_prompt, y_sample, pool_p, pool_s, ccv_p, ccv_s, sc_p, sc_s, v_s, ffn_p, ffn_s)
```

```python
import numpy as np
import concourse.bass as bass
import concourse.mybir as mybir
from concourse.bass_utils import run_bass_kernel_spmd

F32 = mybir.dt.float32
BF16 = mybir.dt.bfloat16
ALU = mybir.AluOpType
AF = mybir.ActivationFunctionType

NCORES = 8
D = 1024
SEQ = 16384
OWN = SEQ // NCORES
HALO = 288
NSS = 4
LS = 64
TOK = HALO + OWN + NSS * LS
NOUT = OWN + NSS * LS
DFF = 2816
NU = 11
EPS = 1e-6
NSLOT = 6

CST = {}
_off = 0
for _n, _sz in [("gmix", 32), ("gffn", 32), ("gfin", 8), ("pscale", 8), ("ccvw", 2 * 4 * 31),
                ("ccvb", 8), ("ccvg", 8), ("ccvbt", 8), ("scw", 2 * 4 * 3), ("fcw", 4 * 44 * 3),
                ("fcb", 4 * 44)]:
    CST[_n] = (_off, _sz)
    _off += _sz
CSTN = _off


def _unit_perm():
    perm = []
    for i in range(NU):
        perm += [2 * i, 2 * i + 1, 22 + 2 * i, 22 + 2 * i + 1]
    return np.array(perm)


class Plan:
    ENG = ("pe", "act", "dve", "pool", "sp")

    def __init__(self):
        self.ops = {e: [] for e in self.ENG}
        self.lastw = {}
        self.readers = {}
        self.dmasem_count = {}

    def op(self, eng, fn, reads=(), writes=(), dma_sem=None):
        idx = len(self.ops[eng])
        deps = set()
        for r in reads:
            if r in self.lastw:
                deps.add(self.lastw[r])
        for w in writes:
            if w in self.lastw:
                deps.add(self.lastw[w])
            for ev in self.readers.get(w, ()):
                deps.add(ev)
        if dma_sem is not None:
            self.dmasem_count[dma_sem] = self.dmasem_count.get(dma_sem, 0) + 16
            ev = ("sem", dma_sem, self.dmasem_count[dma_sem])
        else:
            ev = ("op", eng, idx)
        fdeps = set()
        for d in deps:
            if d[0] == "op" and d[1] == eng:
                if eng == "pe":
                    continue
                if idx - d[2] >= 3:
                    continue
            fdeps.add(d)
        self.ops[eng].append(dict(fn=fn, deps=fdeps, flag=False, dma=dma_sem is not None))
        for r in reads:
            self.readers.setdefault(r, []).append(ev)
        for w in writes:
            self.lastw[w] = ev
            self.readers[w] = []
        return ev

    def resolve(self):
        for e in self.ENG:
            for o in self.ops[e]:
                for d in o["deps"]:
                    if d[0] == "op":
                        self.ops[d[1]][d[2]]["flag"] = True
        self.count = {}
        for e in self.ENG:
            c = 0
            lst = []
            for o in self.ops[e]:
                if o["flag"]:
                    c += 1
                lst.append(c)
            self.count[e] = lst

    def emit(self, eng, e, engsem, dmasems):
        known = {}
        for o in self.ops[eng]:
            need = {}
            for d in o["deps"]:
                if d[0] == "op":
                    k = ("e", d[1])
                    v = self.count[d[1]][d[2]]
                else:
                    k = ("d", d[1])
                    v = d[2]
                if need.get(k, 0) < v:
                    need[k] = v
            for k, v in need.items():
                if known.get(k, 0) < v:
                    sem = engsem[k[1]] if k[0] == "e" else dmasems[k[1]]
                    e.wait_ge(sem, v)
                    known[k] = v
            ins = o["fn"](e)
            if o["flag"]:
                ins.then_inc(engsem[eng], 1)


def build_program(dbg_tiles=None, dbg_layers=4, dbg_stage=99, record=False, pieces=None):
    if not record and pieces is None:
        pieces = build_program(dbg_tiles, dbg_layers, dbg_stage, record=True, pieces=[])
    nc = bass.Bass("TRN2", target_bir_lowering=False)
    P = Plan()

    def din(name, shape):
        return nc.dram_tensor(name, list(shape), F32, kind="ExternalInput").ap()

    def dout(name, shape):
        return nc.dram_tensor(name, list(shape), F32, kind="ExternalOutput").ap()

    xT = din("xT", [128, 8, TOK])
    cst_d = din("cst", [128, CSTN])
    mask_d = din("mask", [HALO])
    invc_d = din("invc", [4 * 16])
    sgug_d = din("sgug", [2 * 512])
    sgub_d = din("sgubt", [2 * 512])
    sgubias_d = din("sgubias", [2 * 4 * 128])
    wsT_d = din("wsT", [128, 2 * 4 * 128])
    wsTs_d = din("wsTs", [128, 2 * 4 * 64])
    tril_d = din("tril", [128, 128])
    trils_d = din("trils", [128, 64])
    poolw_d = din("poolw", [128, 2 * 4 * 128])
    ident_d = din("ident", [128, 128])
    stpool_d = din("st_pool", [128, 2 * 4 * NSS * 15])
    stccv_d = din("st_ccv", [128, 2 * 4 * NSS * 30])
    stsc_d = din("st_sc", [128, 2 * 4 * NSS * 2])
    stffn_d = din("st_ffn", [128, 4 * 44 * NSS * 2])
    w_in_even = din("w_in_even", [2, D, 1536])
    w_out_even = din("w_out_even", [2, D, D])
    w_in_odd = din("w_in_odd", [2, D, 2560])
    w_out_odd = din("w_out_odd", [2, D, D])
    ffn_w_up = din("ffn_w_up", [4, D, 2 * DFF])
    ffn_w_down = din("ffn_w_down", [4, DFF, D])

    yT = dout("yT", [128, 8, NOUT])
    o_pool = dout("o_pool", [128, 2 * 4 * 5 * 15])
    o_ccv = dout("o_ccv", [128, 2 * 4 * 5 * 30])
    o_sc = dout("o_sc", [128, 2 * 4 * 5 * 2])
    o_ffn = dout("o_ffn", [128, 4 * 44 * 5 * 2])
    o_v = dout("o_v", [2, NSS, LS, 512])

    from contextlib import ExitStack
    es = ExitStack()

    def sb(name, shape, dt=F32):
        return es.enter_context(nc.sbuf_tensor("sb_" + name, list(shape), dt))

    xb = sb("xb", [128, 8, 512])
    hb = sb("hb", [128, 8, 512], BF16)
    yb = sb("yb", [128, 8, 512], BF16)
    rstd = sb("rstd", [128, 512])
    sd = sb("sd", [128, 512])
    cst = sb("cst", [128, CSTN])
    maskb = sb("maskb", [128, HALO])
    invc = sb("invc", [128, 4, 16])
    sgug = sb("sgug", [128, 2, 512])
    sgubt = sb("sgubt", [128, 2, 512])
    sgubias = sb("sgubias", [128, 8, 128])
    wsT = sb("wsT", [128, 8, 128], BF16)
    wsTs = sb("wsTs", [128, 8, 64], BF16)
    poolw = sb("poolw", [128, 8, 128], BF16)
    ones = sb("ones", [128, 128], BF16)
    dmy = sb("dmy", [128, 8])
    ident = sb("ident", [128, 128])
    stpool = sb("stpool", [128, 2, 4, NSS, 15])
    stccv = sb("stccv", [128, 2, 4, NSS, 30])
    stsc = sb("stsc", [128, 2, 4, NSS, 2])
    stffn = sb("stffn", [128, 4, 44, NSS, 2])
    npool = sb("npool", [128, 2, 4, 5, 15])
    nccv = sb("nccv", [128, 2, 4, 5, 30])
    nsc = sb("nsc", [128, 2, 4, 5, 2])
    nffn = sb("nffn", [128, 4, 44, 5, 2])
    ring = [sb(f"ring{i}", [128, 8, 512], BF16) for i in range(NSLOT)]
    ARENA_F = 18240
    arena = sb("arena", [128, ARENA_F])
    ps = [es.enter_context(nc.psum_tensor(f"ps{i}", [128, 512], F32)) for i in range(8)]

    import os
    if os.environ.get("SBUF_FREE"):
        print("SBUF free bytes/partition:", nc.sbuf_bytes_remaining)
    engsem = {e: es.enter_context(nc.semaphore(f"s_{e}")) for e in Plan.ENG}
    dmasems = {}

    def dsem(name):
        if name not in dmasems:
            dmasems[name] = es.enter_context(nc.semaphore(f"d_{name}"))
        return name

    def carve(off, words):
        return arena[:, off:off + words]

    A_ZAB = carve(0, 4 * 527)
    A_ZBB = carve(2108, 4 * 542)
    A_T1 = carve(4276, 527)
    A_T2 = carve(4803, 527)
    A_D = carve(5330, 1024).bitcast(BF16)
    A_SIG = carve(6354, 1024)
    A_CB = carve(7378, 2048)
    A_MEAN = carve(9426, 512)
    A_MSQ = carve(9938, 512)
    A_LRS = carve(10450, 512)
    A_TT = carve(10962, 1024)
    A_ZBH = carve(13010, 1084).bitcast(BF16)
    A_DG = carve(14094, 4096).bitcast(BF16)
    O_U = carve(0, 2048)
    O_VN = carve(2048, 1024).bitcast(BF16)
    O_V32 = carve(3072, 2048)
    O_VT = carve(5120, 1024)
    O_CX = carve(6144, 4 * 514)
    O_XIN = carve(8200, 1024)
    O_BG = carve(9224, 1024)
    O_CZ = carve(10248, 1024)
    O_MT = carve(11272, 1024)
    O_BN = carve(12296, 64)
    F_UP = carve(0, 2 * 4 * 514)
    F_C = carve(4112, 4 * 512)
    F_S = carve(6160, 1024)
    F_G = carve(7184, 5632).bitcast(BF16)
    F_PT = carve(12816, 1024)
    SQB = carve(0, 2048).bitcast(BF16)
    A_LNB = carve(10962 + 1024, 1070)
    YOUT = carve(2048, 4096)

    E_CBB = carve(11986, 1024).bitcast(BF16)
    assert 11986 + 1024 <= ARENA_F
    E_SQB = A_TT.bitcast(BF16)

    cview = lambda name: cst[:, CST[name][0]:CST[name][0] + CST[name][1]]
    gmix = cview("gmix").rearrange("p (l c) -> p l c", l=4)
    gffn = cview("gffn").rearrange("p (l c) -> p l c", l=4)
    gfin = cview("gfin")
    pscale = cview("pscale").rearrange("p (a c) -> p a c", a=2)
    ccvw = cview("ccvw").rearrange("p (a c k) -> p a c k", a=2, c=4)
    ccvb = cview("ccvb").rearrange("p (a c) -> p a c", a=2)
    ccvg = cview("ccvg").rearrange("p (a c) -> p a c", a=2)
    ccvbt = cview("ccvbt").rearrange("p (a c) -> p a c", a=2)
    scw = cview("scw").rearrange("p (a c k) -> p a c k", a=2, c=4)
    fcw = cview("fcw").rearrange("p (l c k) -> p l c k", l=4, c=44)
    fcb = cview("fcb").rearrange("p (l c) -> p l c", l=4)

    bank_ctr = [0]
    dg_ctr = [0]
    pt_ctr = [0]

    reserved_banks = set()

    def next_bank():
        while True:
            b = bank_ctr[0] % 8
            bank_ctr[0] += 1
            if b not in reserved_banks:
                return b

    WTS = dict(w_in_even=w_in_even, w_out_even=w_out_even, w_in_odd=w_in_odd, w_out_odd=w_out_odd,
               ffn_w_up=ffn_w_up, ffn_w_down=ffn_w_down)
    piece_ctr = [0]
    emitted = [0]
    KPRE = NSLOT - 3

    def emit_load(j):
        s = j % NSLOT
        sem = dsem(f"w{s}")
        for i, (wn, idx, r0, nrows, c0, ncol, k0, co) in enumerate(pieces[j]):
            src = WTS[wn][idx][r0:r0 + nrows, c0:c0 + ncol].rearrange("(k p) n -> p k n", p=128)
            nk = nrows // 128

            def fn(e, src=src, s=s, k0=k0, nk=nk, co=co, ncol=ncol, sem=sem):
                e.dma_start(out=ring[s][:, k0:k0 + nk, co:co + ncol], in_=src).then_inc(dmasems[sem], 16)
                return None
            P.op("pool", fn, writes=[("slot", s)] if i == 0 else [], dma_sem=sem)
            if i > 0:
                P.lastw[("slot", s)] = ("sem", sem, P.dmasem_count[sem])

    def load_piece(descs):
        i = piece_ctr[0]
        piece_ctr[0] += 1
        if record:
            pieces.append(list(descs))
            return i % NSLOT
        assert pieces[i] == list(descs)
        while emitted[0] <= min(i + KPRE, len(pieces) - 1):
            emit_load(emitted[0])
            emitted[0] += 1
        return i % NSLOT

    def wcols(w2d, c0, ncol):
        return w2d.rearrange("(k p) n -> p k n", p=128)[:, :, c0:c0 + ncol]

    def mm_group(lhs, rhs, K, M, N, reads, bank=None, start=True, stop=True):
        b = next_bank() if bank is None else bank

        def fn(e):
            ins = None
            for k in range(K):
                ins = e.matmul(ps[b][:M, :N], lhs(k), rhs(k), start=(start and k == 0), stop=(stop and k == K - 1))
            return ins
        P.op("pe", fn, reads=reads, writes=[("ps", b)])
        return b

    def ld(dst, src, res):
        sem = dsem("c_" + res)

        def fn(e):
            e.dma_start(out=dst, in_=src).then_inc(dmasems[sem], 16)
            return None
        P.op("sp", fn, writes=[res], dma_sem=sem)

    ld(cst[:], cst_d, "cst")
    ld(ident[:], ident_d, "ident")
    ld(maskb[:], mask_d.partition_broadcast(128), "maskb")
    ld(invc[:], invc_d.partition_broadcast(128).rearrange("p (g k) -> p g k", g=4), "invc")
    ld(sgug[:], sgug_d.partition_broadcast(128).rearrange("p (a n) -> p a n", a=2), "sgug")
    ld(sgubt[:], sgub_d.partition_broadcast(128).rearrange("p (a n) -> p a n", a=2), "sgubt")
    ld(sgubias[:], sgubias_d.partition_broadcast(128).rearrange("p (a n) -> p a n", a=8), "sgubias")
    ld(stpool[:], stpool_d.rearrange("p (a c s k) -> p a c s k", a=2, c=4, s=NSS), "stpool")
    ld(stccv[:], stccv_d.rearrange("p (a c s k) -> p a c s k", a=2, c=4, s=NSS), "stccv")
    ld(stsc[:], stsc_d.rearrange("p (a c s k) -> p a c s k", a=2, c=4, s=NSS), "stsc")
    ld(stffn[:], stffn_d.rearrange("p (a c s k) -> p a c s k", a=4, c=44, s=NSS), "stffn")
    T_WS = carve(0, 1024).rearrange("p (a n) -> p a n", a=8)
    T_WSS = carve(1024, 512).rearrange("p (a n) -> p a n", a=8)
    T_TR = carve(1536, 128)
    T_TRS = carve(1664, 64)
    T_PW = carve(1728, 1024).rearrange("p (a n) -> p a n", a=8)
    ld(T_WS, wsT_d.rearrange("p (a n) -> p a n", a=8), "t_ws")
    ld(T_WSS, wsTs_d.rearrange("p (a n) -> p a n", a=8), "t_wss")
    ld(T_TR, tril_d, "t_tr")
    ld(T_TRS, trils_d, "t_trs")
    ld(T_PW, poolw_d.rearrange("p (a n) -> p a n", a=8), "t_pw")

    P.op("dve", lambda e: e.tensor_tensor(out=wsT[:], in0=T_WS, in1=T_TR.unsqueeze(1).broadcast_to([128, 8, 128]), op=ALU.mult),
         reads=["t_ws", "t_tr"], writes=["wsT", "setup"])
    P.op("dve", lambda e: e.tensor_tensor(out=wsTs[:], in0=T_WSS, in1=T_TRS.unsqueeze(1).broadcast_to([128, 8, 64]), op=ALU.mult),
         reads=["t_wss", "t_trs"], writes=["wsTs", "setup2"])
    P.op("act", lambda e: e.activation(out=poolw[:], in_=T_PW, func=AF.Identity), reads=["t_pw"], writes=["poolw", "setup3"])
    P.op("dve", lambda e: e.memset(ones[:], 1.0), writes=["ones"])
    P.op("dve", lambda e: e.memset(dmy[:], 1.0), writes=["dmy"])
    P.op("dve", lambda e: e.memset(npool[:], 0.0), writes=["npool"])
    P.op("dve", lambda e: e.memset(nccv[:], 0.0), writes=["nccv"])
    P.op("dve", lambda e: e.memset(nsc[:], 0.0), writes=["nsc"])
    P.op("dve", lambda e: e.memset(nffn[:], 0.0), writes=["nffn"])

    SETUP = ["setup", "setup2", "setup3"]

    class T:
        pass

    tiles = []
    t0 = T(); t0.col0 = 0; t0.N = HALO; t0.S = 1; t0.L = HALO; t0.kind = "halo"
    t0.vblocks = [(0, 32), (32, 128), (160, 128)]
    tiles.append(t0)
    for i in range(4):
        t = T(); t.col0 = HALO + 512 * i; t.N = 512; t.S = 1; t.L = 512; t.kind = "own"; t.outcol = 512 * i
        t.vblocks = [(128 * b, 128) for b in range(4)]
        t.first = (i == 0)
        tiles.append(t)
    ts_ = T(); ts_.col0 = HALO + OWN; ts_.N = NSS * LS; ts_.S = NSS; ts_.L = LS; ts_.kind = "sample"; ts_.outcol = OWN
    ts_.vblocks = [(64 * q, 64) for q in range(NSS)]
    tiles.append(ts_)

    def v4(flat, C, S, W):
        return flat[:, 0:C * S * W].rearrange("p (c s w) -> p c s w", c=C, s=S)

    def pv(b, t):
        return ps[b][:, 0:t.N].rearrange("p (s l) -> p s l", s=t.S)

    def v3(flat2d, t):
        return flat2d.rearrange("p (s l) -> p s l", s=t.S)

    def seqsl(t):
        return slice(0, 1) if t.kind != "sample" else slice(1, 5)

    def rmsnorm(t, gain, final=False, extra_reads=(), h_extra=()):
        N = t.N
        sq = SQB.rearrange("p (c n) -> p c n", c=8)
        yo = YOUT.rearrange("p (c n) -> p c n", c=8)
        for c in range(8):
            P.op("act", lambda e, c=c: e.activation(out=sq[:, c, :N], in_=xb[:, c, :N], func=AF.Square),
                 reads=[("x", c)] + list(extra_reads), writes=[("sq", c)])
            if final:
                P.op("act", lambda e, c=c: e.activation(out=yo[:, c, :N], in_=xb[:, c, :N], func=AF.Identity, scale=gain(c)),
                     reads=[("x", c), "cst"], writes=[("yout", c)])
        import os
        NS = int(os.environ.get("DBG_NORM", "9"))
        if NS < 1:
            return
        b = next_bank()
        for k in range(8):
            P.op("pe", lambda e, k=k, b=b: e.matmul(ps[b][:, :N], ones[:, :], sq[:, k, :N], start=(k == 0), stop=(k == 7)),
                 reads=[("sq", k), "ones"], writes=[("ps", b)] if k == 0 else [("psacc_n", 0)])
        P.lastw[("ps", b)] = ("op", "pe", len(P.ops["pe"]) - 1)
        if NS < 2:
            return
        P.op("act", lambda e: e.activation(out=sd[:, :N], in_=ps[b][:, :N], func=AF.Ln, bias=EPS, scale=1.0 / D),
             reads=[("ps", b)], writes=["sd"])
        if NS < 3:
            return
        P.op("act", lambda e: e.activation(out=rstd[:, :N], in_=sd[:, :N], func=AF.Exp, scale=-0.5), reads=["sd"], writes=["rstd"])
        if NS < 4:
            return
        if t.kind == "halo":
            P.op("dve", lambda e: e.tensor_tensor(out=rstd[:, :N], in0=rstd[:, :N], in1=maskb[:, :N], op=ALU.mult),
                 reads=["rstd", "maskb"], writes=["rstd"])
        if not final:
            for c in range(8):
                P.op("dve", lambda e, c=c: e.scalar_tensor_tensor(out=hb[:, c, :N], in0=xb[:, c, :N], scalar=gain(c),
                                                                in1=rstd[:, :N], op0=ALU.mult, op1=ALU.mult),
                     reads=[("x", c), "rstd", "cst"] + list(h_extra), writes=[("h", c)])
        else:
            for c in range(8):
                P.op("dve", lambda e, c=c: e.tensor_tensor(out=yo[:, c, :N], in0=yo[:, c, :N], in1=rstd[:, :N], op=ALU.mult),
                     reads=[("yout", c), "rstd"], writes=[("yout", c)])

    HALL = [("h", c) for c in range(8)]

    def proj_group(slot, m, t, K=8, src=None, srcres=None):
        src = hb if src is None else src
        N = t.N
        return mm_group(lambda k: ring[slot][:, k, m * 128:(m + 1) * 128], lambda k: src[:, k, :N], K, 128, N,
                        reads=(HALL if srcres is None else srcres) + [("slot", slot)])

    dgbuf_all = A_DG.rearrange("p (i n) -> p i n", i=64)
    prebuilt = [None]

    pending_builds = []

    def prebuild_start(prn):
        pending_builds[:] = [(prn, j, k) for j in (0, 1) for k in range(31)]

    def prebuild_some(n):
        for _ in range(n):
            if not pending_builds:
                return
            prn, j, k = pending_builds.pop(0)
            di = j * 32 + k
            P.op("act", lambda e, prn=prn, j=j, k=k, di=di: e.activation(out=dgbuf_all[:, di, :], in_=ident[:], func=AF.Identity,
                                                                       scale=ccvw[:, prn, j, k:k + 1]),
                 reads=["cst", "ident"], writes=[("dg", j, k)])
            if not pending_builds:
                prebuilt[0] = prn

    def preload_sqrt():
        P.op("act", lambda e: e.activation(out=dmy[:, 0:1], in_=dmy[:, 1:2], func=AF.Ln), reads=["dmy"], writes=["dmy2"])

    def proj_kouter(slot, ms, t):
        N = t.N
        banks = [next_bank() for _ in ms]
        for k in range(8):
            def fn(e, k=k):
                ins = None
                for bi_, m in enumerate(ms):
                    ins = e.matmul(ps[banks[bi_]][:, :N], ring[slot][:, k, m * 128:(m + 1) * 128], hb[:, k, :N],
                                   start=(k == 0), stop=(k == 7))
                return ins
            P.op("pe", fn, reads=[("h", k), ("slot", slot)], writes=[("ps", b) for b in banks] if k == 0 else [("psacc_k", slot)])
        for b in banks:
            P.lastw[("ps", b)] = ("op", "pe", len(P.ops["pe"]) - 1)
        return banks

    def out_proj(t, wname, widx, korder=(0, 1, 2, 3, 4, 5, 6, 7)):
        N = t.N
        for half in range(2):
            s = load_piece([(wname, widx, 0, 1024, half * 512, 512, 0, 0)])
            banks = [next_bank() for _ in range(4)]
            if half == 0:
                for ki, k in enumerate(korder):
                    def fn(e, k=k, ki=ki, s=s, banks=banks):
                        ins = None
                        for mm in range(4):
                            ins = e.matmul(ps[banks[mm]][:, :N], ring[s][:, k, mm * 128:(mm + 1) * 128], yb[:, k, :N],
                                           start=(ki == 0), stop=(ki == 7))
                        return ins
                    P.op("pe", fn, reads=[("y", k), ("slot", s)], writes=[("ps", b) for b in banks] if ki == 0 else [("psacc_o", half)])
                for b in banks:
                    P.lastw[("ps", b)] = ("op", "pe", len(P.ops["pe"]) - 1)
            else:
                for mm in range(4):
                    def fn(e, mm=mm, s=s, b=banks[mm]):
                        ins = None
                        for k in range(8):
                            ins = e.matmul(ps[b][:, :N], ring[s][:, k, mm * 128:(mm + 1) * 128], yb[:, k, :N],
                                           start=(k == 0), stop=(k == 7))
                        return ins
                    P.op("pe", fn, reads=[("y", k) for k in range(8)] + [("slot", s)], writes=[("ps", banks[mm])])
            for mm in range(4):
                m = half * 4 + mm
                b = banks[mm]
                P.op("dve", lambda e, m=m, b=b: e.tensor_tensor(out=xb[:, m, :N], in0=ps[b][:, :N], in1=xb[:, m, :N], op=ALU.add),
                     reads=[("ps", b), ("x", m)], writes=[("x", m)])

    def even_mixer(t, l):
        pr = l // 2
        N, S, L = t.N, t.S, t.L
        zab = v4(A_ZAB, 4, S, 15 + L)
        zbb = v4(A_ZBB, 4, S, 30 + L)
        win = w_in_even[pr]
        if t.kind == "sample":
            P.op("act", lambda e: e.activation(out=zab[:, :, :, 0:15], in_=stpool[:, pr], func=AF.Identity),
                 reads=["rstd", "stpool", "youtdma"], writes=[("zab", m) for m in range(4)])
            P.op("act", lambda e: e.activation(out=zbb[:, :, :, 0:30], in_=stccv[:, pr], func=AF.Identity),
                 reads=["rstd", "stccv", "youtdma"], writes=[("zbb", m) for m in range(4)])
        else:
            P.op("act", lambda e: e.activation(out=zab[:, :, :, 0:15], in_=npool[:, pr, :, 0:1, :], func=AF.Identity),
                 reads=["rstd", "npool", "youtdma"], writes=[("zab", m) for m in range(4)])
            P.op("act", lambda e: e.activation(out=zbb[:, :, :, 0:30], in_=nccv[:, pr, :, 0:1, :], func=AF.Identity),
                 reads=["rstd", "nccv", "youtdma"], writes=[("zbb", m) for m in range(4)])
        dg = A_DG.rearrange("p (i n) -> p i n", i=64)
        zbh = v4(A_ZBH, 4, S, 30 + L)

        def build_diags(j, ks):
            for k in ks:
                di = (j % 2) * 32 + k
                if k % 2 == 0:
                    P.op("act", lambda e, j=j, k=k, di=di: e.activation(out=dg[:, di, :], in_=ident[:], func=AF.Identity,
                                                                       scale=ccvw[:, pr, j, k:k + 1]),
                         reads=["cst", "ident", "rstd"], writes=[("dg", j % 2, k)])
                else:
                    P.op("dve", lambda e, j=j, k=k, di=di: e.tensor_scalar(out=dg[:, di, :], in0=ident[:], scalar1=ccvw[:, pr, j, k:k + 1],
                                                                         scalar2=None, op0=ALU.mult),
                         reads=["cst", "ident", "rstd"], writes=[("dg", j % 2, k)])
        s0 = load_piece([("w_in_even", pr, 0, 1024, 0, 512, 0, 0)])
        sa = load_piece([("w_in_even", pr, 0, 1024, 512, 512, 0, 0)])
        sg = load_piece([("w_in_even", pr, 0, 1024, 1024, 512, 0, 0)])
        zbanks = proj_kouter(s0, [0, 1, 2, 3], t)
        for m in range(4):
            b = zbanks[m]
            P.op("act", lambda e, m=m, b=b: e.activation(out=zab[:, m, :, 15:15 + L], in_=pv(b, t), func=AF.Identity),
                 reads=[("ps", b)], writes=[("zab", m)])
        sig = A_SIG.rearrange("p (i n) -> p i n", i=2)
        for m in range(4):
            ba = proj_group(sa, m, t)
            bg = proj_group(sg, m, t)
            i = m % 2
            P.op("act", lambda e, i=i, bg=bg: e.activation(out=sig[:, i, :N], in_=ps[bg][:, :N], func=AF.Sigmoid),
                 reads=[("ps", bg)], writes=[("sig", i)])
            if m == 3:
                preload_sqrt()
            P.op("dve", lambda e, m=m, i=i, ba=ba: e.tensor_tensor(out=zbb[:, m, :, 30:30 + L], in0=pv(ba, t), in1=v3(sig[:, i, :N], t), op=ALU.mult),
                 reads=[("ps", ba), ("sig", i)], writes=[("zbb", m)])
            P.op("act", lambda e, m=m: e.activation(out=zbh[:, m], in_=zbb[:, m], func=AF.Identity),
                 reads=[("zbb", m)], writes=[("zbh", m)])
            if prebuilt[0] != pr:
                build_diags(0, range(8 * m, min(31, 8 * m + 8)))
        if prebuilt[0] != pr:
            build_diags(1, range(31))
        prebuilt[0] = None
        P.op("act", lambda e: e.activation(out=npool[:, pr, :, seqsl(t), :], in_=zab[:, :, :, L:L + 15], func=AF.Identity),
             reads=[("zab", m) for m in range(4)], writes=["npool"])
        P.op("act", lambda e: e.activation(out=nccv[:, pr, :, seqsl(t), :], in_=zbb[:, :, :, L:L + 30], func=AF.Identity),
             reads=[("zbb", m) for m in range(4)], writes=["nccv"])
        W = 15 + L
        T1 = A_T1[:, 0:S * W].rearrange("p (s w) -> p s w", s=S)
        T2 = A_T2[:, 0:S * W].rearrange("p (s w) -> p s w", s=S)
        dB = A_D.rearrange("p (c n) -> p c n", c=4)
        for g in range(4):
            z = zab[:, g]
            cur = z
            curres = ("zab", g)
            bufs = [(T1, "T1"), (T2, "T2")]
            sh = 1
            for step in range(g + 1):
                dst, dres = bufs[step % 2]
                lo = 2 * sh - 1
                P.op("dve", lambda e, dst=dst, cur=cur, lo=lo, sh=sh: e.tensor_tensor(
                    out=dst[:, :, lo:W], in0=cur[:, :, lo:W], in1=cur[:, :, lo - sh:W - sh], op=ALU.add),
                    reads=[curres], writes=[dres])
                cur, curres = dst, dres
                sh *= 2
            wlen = 2 ** (g + 1)
            P.op("dve", lambda e, g=g, cur=cur, z=z, wlen=wlen: e.scalar_tensor_tensor(
                out=v3(dB[:, g, :N], t), in0=cur[:, :, 15:W], scalar=1.0 / wlen, in1=z[:, :, 15:W],
                op0=ALU.mult, op1=ALU.subtract),
                reads=[curres, ("zab", g)], writes=[("d", g)])
            if t.kind == "own" and t.first:
                other = bufs[(g + 1) % 2]
                P.op("dve", lambda e, g=g, cur=cur, other=other: e.tensor_tensor(
                    out=other[0][:, 0, 0:16], in0=cur[:, 0, 15:31], in1=invc[:, g, :], op=ALU.mult),
                    reads=[curres, "invc"], writes=[other[1]])
                P.op("dve", lambda e, g=g, z=z, other=other: e.tensor_tensor(
                    out=dB[:, g, 0:16], in0=other[0][:, 0, 0:16], in1=z[:, 0, 15:31], op=ALU.subtract),
                    reads=[other[1], ("zab", g)], writes=[("d", g)])
        cb = A_CB.rearrange("p (c n) -> p c n", c=4)
        dg = A_DG.rearrange("p (i n) -> p i n", i=64)
        cbb = E_CBB.rearrange("p (c n) -> p c n", c=4)
        sqb = SQB.rearrange("p (c n) -> p c n", c=8)
        b1 = next_bank()
        b2 = next_bank()
        reserved_banks.update((b1, b2))

        def ln_stats(j):
            P.op("pe", lambda e, j=j: e.matmul(ps[b1][:, :N], ones[:, :], cbb[:, j, :N], start=(j == 0), stop=(j == 3)),
                 reads=[("cbb", j), "ones"], writes=[("ps", b1)] if j == 0 else [("psacc_l", 1)])
            P.op("pe", lambda e, j=j: e.matmul(ps[b2][:, :N], ones[:, :], sqb[:, j, :N], start=(j == 0), stop=(j == 3)),
                 reads=[("lsq", j), "ones"], writes=[("ps", b2)] if j == 0 else [("psacc_l", 2)])
            if j == 3:
                P.lastw[("ps", b1)] = ("op", "pe", len(P.ops["pe"]) - 2)
                P.lastw[("ps", b2)] = ("op", "pe", len(P.ops["pe"]) - 1)
        for j in range(4):
            b = next_bank()
            for k in range(31):
                di = (j % 2) * 32 + k
                P.op("pe", lambda e, j=j, k=k, di=di, b=b: e.matmul(pv(b, t), dg[:, di, :], zbh[:, j, :, k:k + L],
                                                                  start=(k == 0), stop=(k == 30)),
                     reads=[("dg", j % 2, k), ("zbh", j)], writes=[("ps", b)] if k == 0 else [("psacc_c", j)])
            P.lastw[("ps", b)] = ("op", "pe", len(P.ops["pe"]) - 1)
            P.op("act", lambda e, j=j, b=b: e.activation(out=cb[:, j, :N], in_=ps[b][:, :N], func=AF.Identity, bias=ccvb[:, pr, j:j + 1]),
                 reads=[("ps", b), "cst"], writes=[("cb", j)])
            P.op("act", lambda e, j=j, b=b: e.activation(out=cbb[:, j, :N], in_=ps[b][:, :N], func=AF.Identity, bias=ccvb[:, pr, j:j + 1]),
                 reads=[("ps", b), "cst"], writes=[("cbb", j)])
            P.op("act", lambda e, j=j, b=b: e.activation(out=sqb[:, j, :N], in_=ps[b][:, :N], func=AF.Square, bias=ccvb[:, pr, j:j + 1]),
                 reads=[("ps", b), "cst"] + [("zab", m) for m in range(4)] + [("d", g) for g in range(4)] + ["T1", "T2", "npool"],
                 writes=[("lsq", j), ("zab", 0), ("zab", 1)])
            if j + 2 < 4:
                build_diags(j + 2, range(31))
            if j >= 1:
                ln_stats(j - 1)
        for g in range(4):
            b = mm_group(lambda k, g=g: poolw[:, pr * 4 + g, :], lambda k, g=g: dB[:, g, :N], 1, 128, N,
                         reads=[("d", g), "poolw"])
            P.op("act", lambda e, g=g, b=b: e.activation(out=yb[:, g, :N], in_=ps[b][:, :N], func=AF.Identity, scale=pscale[:, pr, g:g + 1]),
                 reads=[("ps", b), "cst"], writes=[("y", g)])
        ln_stats(3)
        reserved_banks.difference_update((b1, b2))
        P.op("act", lambda e: e.activation(out=A_MEAN[:, :N], in_=ps[b1][:, :N], func=AF.Identity, scale=1.0 / 512),
             reads=[("ps", b1)], writes=["mean"])
        P.op("act", lambda e: e.activation(out=A_MSQ[:, :N], in_=ps[b1][:, :N], func=AF.Square, scale=1.0 / 512),
             reads=[("ps", b1)], writes=["msq"])
        P.op("dve", lambda e: e.scalar_tensor_tensor(out=A_LRS[:, :N], in0=ps[b2][:, :N], scalar=1.0 / 512, in1=A_MSQ[:, :N],
                                                    op0=ALU.mult, op1=ALU.subtract),
             reads=[("ps", b2), "msq"], writes=["lrs"])
        P.op("act", lambda e: e.activation(out=A_MSQ[:, :N], in_=A_LRS[:, :N], func=AF.Ln, bias=EPS, scale=1.0),
             reads=["lrs"], writes=["msq"])
        P.op("act", lambda e: e.activation(out=A_LRS[:, :N], in_=A_MSQ[:, :N], func=AF.Exp, scale=-0.5), reads=["msq"], writes=["lrs"])
        tt = A_TT.rearrange("p (i n) -> p i n", i=2)
        for j in range(4):
            i = j % 2
            P.op("dve", lambda e, j=j, i=i: e.tensor_tensor(out=tt[:, i, :N], in0=cb[:, j, :N], in1=A_MEAN[:, :N], op=ALU.subtract),
                 reads=[("cb", j), "mean"], writes=[("tt", i)])
            P.op("dve", lambda e, j=j, i=i: e.tensor_tensor(out=tt[:, i, :N], in0=tt[:, i, :N], in1=A_LRS[:, :N], op=ALU.mult),
                 reads=[("tt", i), "lrs"], writes=[("tt", i)])
            P.op("act", lambda e, j=j, i=i: e.activation(out=yb[:, 4 + j, :N], in_=tt[:, i, :N], func=AF.Silu,
                                                       bias=ccvbt[:, pr, j:j + 1], scale=ccvg[:, pr, j:j + 1]),
                 reads=[("tt", i), "cst"], writes=[("y", 4 + j)])
        preload_sqrt()
        out_proj(t, "w_out_even", pr)

    def odd_mixer(t, l):
        pr = l // 2
        N, S, L = t.N, t.S, t.L
        win = w_in_odd[pr]
        cx = v4(O_CX, 4, S, 2 + L)
        hist_src = stsc[:, pr] if t.kind == "sample" else nsc[:, pr, :, 0:1, :]
        P.op("act", lambda e: e.activation(out=cx[:, :, :, 0:2], in_=hist_src, func=AF.Identity),
             reads=["rstd", "stsc", "nsc", "youtdma"], writes=[("cx", j) for j in range(4)])
        if not (t.kind == "sample" and l == 3):
            prebuild_start((pr + 1) % 2)
        su = load_piece([("w_in_odd", pr, 0, 1024, 0, 512, 0, 0)])
        sv = load_piece([("w_in_odd", pr, 0, 1024, 512, 512, 0, 0)])
        vn = O_VN.rearrange("p (b n) -> p b n", b=4)
        v32 = O_V32.rearrange("p (b n) -> p b n", b=4)
        vt = O_VT.rearrange("p (i n) -> p i n", i=2)
        bn = O_BN
        vbanks = [next_bank() for _ in t.vblocks]
        for k in range(8):
            def fnv(e, k=k):
                ins = None
                for bi, (c0, nb) in enumerate(t.vblocks):
                    ins = e.matmul(ps[vbanks[bi]][:nb, :], hb[:, k, c0:c0 + nb], ring[sv][:, k, :], start=(k == 0), stop=(k == 7))
                return ins
            P.op("pe", fnv, reads=[("h", k), ("slot", sv)], writes=[("ps", b) for b in vbanks] if k == 0 else [("psacc_v", 0)])
        for b in vbanks:
            P.lastw[("ps", b)] = ("op", "pe", len(P.ops["pe"]) - 1)
        for bi, (c0, nb) in enumerate(t.vblocks):
            b = vbanks[bi]
            i = bi % 2
            P.op("dve", lambda e, b=b, nb=nb: e.bn_stats(out=bn[:nb, 0:6], in_=ps[b][:nb, :]), reads=[("ps", b)], writes=["bn6"])
            P.op("dve", lambda e, nb=nb: e.bn_aggr(out=bn[:nb, 8:10], in_=bn[:nb, 0:6]), reads=["bn6"], writes=["bnmv"])
            P.op("act", lambda e, nb=nb: e.activation(out=bn[:nb, 10:11], in_=bn[:nb, 9:10], func=AF.Ln, bias=EPS, scale=1.0),
                 reads=["bnmv"], writes=["bnsd"])
            P.op("act", lambda e, nb=nb: e.activation(out=bn[:nb, 11:12], in_=bn[:nb, 10:11], func=AF.Exp, scale=-0.5),
                 reads=["bnsd"], writes=["bnrs"])
            P.op("dve", lambda e, b=b, nb=nb, i=i: e.tensor_scalar(out=vt[:nb, i, :], in0=ps[b][:nb, :], scalar1=bn[:nb, 8:9],
                                                                  scalar2=bn[:nb, 11:12], op0=ALU.subtract, op1=ALU.mult),
                 reads=[("ps", b), "bnmv", "bnrs"], writes=[("vt", i)])
            P.op("dve", lambda e, nb=nb, i=i: e.tensor_tensor(out=vt[:nb, i, :], in0=vt[:nb, i, :], in1=sgug[:nb, pr, :], op=ALU.mult),
                 reads=[("vt", i), "sgug"], writes=[("vt", i)])
            if t.kind == "sample":
                P.op("dve", lambda e, nb=nb, i=i, bi=bi: e.tensor_tensor(out=v32[:nb, bi, :], in0=vt[:nb, i, :], in1=sgubt[:nb, pr, :], op=ALU.add),
                     reads=[("vt", i), "sgubt"], writes=[("v32", bi)])
                P.op("act", lambda e, nb=nb, bi=bi: e.activation(out=vn[:nb, bi, :], in_=v32[:nb, bi, :], func=AF.Identity),
                     reads=[("v32", bi)], writes=[("vn", bi)])
                sem = dsem("ov")

                def fn(e, bi=bi, sem=sem, nb=nb):
                    e.dma_start(out=o_v[pr, bi], in_=v32[:nb, bi, :]).then_inc(dmasems[sem], 16)
                    return None
                P.op("sp", fn, reads=[("v32", bi)], dma_sem=sem)
            else:
                P.op("dve", lambda e, nb=nb, i=i, bi=bi: e.tensor_tensor(out=vn[:nb, bi, :], in0=vt[:nb, i, :], in1=sgubt[:nb, pr, :], op=ALU.add),
                     reads=[("vt", i), "sgubt"], writes=[("vn", bi)])
        u = O_U.rearrange("p (c n) -> p c n", c=4)
        for m in range(4):
            b = proj_group(su, m, t)
            P.op("act", lambda e, m=m, b=b: e.activation(out=u[:, m, :N], in_=ps[b][:, :N], func=AF.Identity),
                 reads=[("ps", b)], writes=[("u", m)])
            prebuild_some(8)
        sb_ = load_piece([("w_in_odd", pr, 0, 1024, 1024, 512, 0, 0)])
        sc_ = load_piece([("w_in_odd", pr, 0, 1024, 1536, 512, 0, 0)])
        sx_ = load_piece([("w_in_odd", pr, 0, 1024, 2048, 512, 0, 0)])
        xin = O_XIN.rearrange("p (i n) -> p i n", i=2)
        bgs = O_BG.rearrange("p (i n) -> p i n", i=2)
        cz = O_CZ.rearrange("p (i n) -> p i n", i=2)
        for j in range(4):
            i = j % 2
            bx = proj_group(sx_, j, t)
            bc = proj_group(sc_, j, t)
            bb = proj_group(sb_, j, t)
            P.op("act", lambda e, i=i, bx=bx: e.activation(out=xin[:, i, :N], in_=ps[bx][:, :N], func=AF.Identity),
                 reads=[("ps", bx)], writes=[("xin", i)])
            P.op("dve", lambda e, j=j, i=i, bc=bc: e.tensor_tensor(out=cx[:, j, :, 2:2 + L], in0=pv(bc, t), in1=v3(xin[:, i, :N], t), op=ALU.mult),
                 reads=[("ps", bc), ("xin", i)], writes=[("cx", j)])
            P.op("act", lambda e, i=i, bb=bb: e.activation(out=bgs[:, i, :N], in_=ps[bb][:, :N], func=AF.Identity),
                 reads=[("ps", bb)], writes=[("bgs", i)])
            prebuild_some(8)
            P.op("dve", lambda e, j=j, i=i: e.tensor_scalar(out=v3(cz[:, i, :N], t), in0=cx[:, j, :, 2:2 + L], scalar1=scw[:, pr, j, 2:3],
                                                          scalar2=None, op0=ALU.mult),
                 reads=[("cx", j), "cst"], writes=[("cz", i)])
            for k in (1, 0):
                P.op("dve", lambda e, j=j, i=i, k=k: e.scalar_tensor_tensor(out=v3(cz[:, i, :N], t), in0=cx[:, j, :, k:k + L],
                                                                           scalar=scw[:, pr, j, k:k + 1], in1=v3(cz[:, i, :N], t),
                                                                           op0=ALU.mult, op1=ALU.add),
                     reads=[("cx", j), ("cz", i), "cst"], writes=[("cz", i)])
            P.op("dve", lambda e, j=j, i=i: e.tensor_tensor(out=yb[:, 4 + j, :N], in0=cz[:, i, :N], in1=bgs[:, i, :N], op=ALU.mult),
                 reads=[("cz", i), ("bgs", i)], writes=[("y", 4 + j)])
        mt = O_MT.rearrange("p (i n) -> p i n", i=2)
        for j in range(4):
            b = next_bank()
            segs = []
            if t.kind == "sample":
                for q in range(NSS):
                    segs.append((q, 0, 64, q * 64, False))
            else:
                for bi, (c0, nb) in enumerate(t.vblocks):
                    segs.append((bi, 0, nb, c0, False))

            def fn(e, j=j, b=b, segs=segs):
                ins = None
                for (bi, po, ln, c0, smp) in segs:
                    rhs = wsTs[po:po + ln, pr * 4 + j, 0:ln] if smp else wsT[0:ln, pr * 4 + j, 0:ln]
                    ins = e.matmul(ps[b][:, c0:c0 + ln], vn[po:po + ln, bi, j * 128:(j + 1) * 128], rhs, start=True, stop=True)
                return ins
            P.op("pe", fn, reads=[("vn", bi) for bi in range(len(t.vblocks))] + ["wsT", "wsTs"], writes=[("ps", b)])
            i = j % 2
            if t.kind == "sample":
                groups = [(0, NSS, 64)]
            elif t.kind == "halo":
                groups = [(0, 1, 32), (32, 2, 128)]
            else:
                groups = [(0, 4, 128)]
            for gi, (c0, nblk, bl) in enumerate(groups):
                P.op("dve", lambda e, j=j, b=b, i=i, c0=c0, nblk=nblk, bl=bl: e.tensor_tensor(
                    out=mt[:, i, c0:c0 + nblk * bl].rearrange("p (a n) -> p a n", a=nblk),
                    in0=ps[b][:, c0:c0 + nblk * bl].rearrange("p (a n) -> p a n", a=nblk),
                    in1=sgubias[:, pr * 4 + j, 0:bl].unsqueeze(1).broadcast_to([128, nblk, bl]), op=ALU.add),
                    reads=[("ps", b), "sgubias"], writes=[("mt", i)] if gi == 0 else [("mt", i)])
            P.op("dve", lambda e, j=j, i=i: e.tensor_tensor(out=yb[:, j, :N], in0=mt[:, i, :N], in1=u[:, j, :N], op=ALU.mult),
                 reads=[("mt", i), ("u", j)], writes=[("y", j)])
        P.op("act", lambda e: e.activation(out=nsc[:, pr, :, seqsl(t), :], in_=cx[:, :, :, L:L + 2], func=AF.Identity),
             reads=[("cx", j) for j in range(4)], writes=["nsc"])
        prebuild_some(64)
        out_proj(t, "w_out_odd", pr, korder=(4, 5, 6, 7, 0, 1, 2, 3))

    def conv_ffn(t, l):
        N, S, L = t.N, t.S, t.L
        wup = ffn_w_up[l]
        wdn = ffn_w_down[l]
        gB = F_G.rearrange("p (c n) -> p c n", c=22)
        sbuf_ = F_S.rearrange("p (i n) -> p i n", i=2)
        ptb = F_PT.rearrange("p (i n) -> p i n", i=2)
        trim = (t.kind == "halo" and l == 3)
        d0 = dict(banks=[] if trim else [next_bank() for _ in range(4)])
        LAG = 2

        def down0_mini(u):
            banks = d0["banks"]
            if u == 0:
                reserved_banks.update(banks)
            kk, nk = 2 * u, 2
            s = load_piece([("ffn_w_down", l, kk * 128, nk * 128, 0, 512, 0, 0)])
            for k in range(nk):
                kg = kk + k

                def fn(e, s=s, k=k, kg=kg, banks=banks):
                    ins = None
                    for mm in range(4):
                        ins = e.matmul(ps[banks[mm]][:, :N], ring[s][:, k, mm * 128:(mm + 1) * 128], gB[:, kg, :N],
                                       start=(kg == 0), stop=(kg == 21))
                    return ins
                P.op("pe", fn, reads=[("g", kg), ("slot", s)], writes=[("ps", bq) for bq in banks] if kg == 0 else [("psacc", 0)])
            if u == NU - 1:
                for mm in range(4):
                    P.lastw[("ps", banks[mm])] = ("op", "pe", len(P.ops["pe"]) - 1)
        for ui in range(NU):
            s = load_piece([("ffn_w_up", l, 0, 1024, 256 * ui, 256, 0, 0), ("ffn_w_up", l, 0, 1024, DFF + 256 * ui, 256, 0, 256)])
            ub = ui % 2
            up = F_UP[:, ub * 2056: ub * 2056 + 4 * S * (2 + L)].rearrange("p (c s w) -> p c s w", c=4, s=S)
            hist_src = stffn[:, l, 4 * ui:4 * ui + 4] if t.kind == "sample" else nffn[:, l, 4 * ui:4 * ui + 4, 0:1, :]
            P.op("act", lambda e, up=up, hist_src=hist_src: e.activation(out=up[:, :, :, 0:2], in_=hist_src, func=AF.Identity),
                 reads=["rstd", "stffn", "nffn", "youtdma"], writes=[("up", ub, q) for q in range(4)])
            ubanks = proj_kouter(s, [0, 1, 2, 3], t) if ui == 0 else None
            for pp in range(2):
                cbuf = F_C[:, pp * 1024:(pp + 1) * 1024].rearrange("p (i n) -> p i n", i=2)
                banks = []
                for half in range(2):
                    q = half * 2 + pp
                    b = ubanks[q] if ui == 0 else proj_group(s, q, t)
                    banks.append(b)
                    ch = 4 * ui + q
                    P.op("act", lambda e, up=up, q=q, b=b: e.activation(out=up[:, q, :, 2:2 + L], in_=pv(b, t), func=AF.Identity),
                         reads=[("ps", b)], writes=[("up", ub, q)])
                    if trim:
                        continue
                    P.op("act", lambda e, cbuf=cbuf, half=half, b=b, ch=ch: e.activation(
                        out=cbuf[:, half, :N], in_=ps[b][:, :N], func=AF.Identity, bias=fcb[:, l, ch:ch + 1], scale=fcw[:, l, ch, 2:3]),
                        reads=[("ps", b), "cst"], writes=[("c", pp, half)])
                    P.op("dve", lambda e, cbuf=cbuf, half=half, up=up, q=q, ch=ch: e.scalar_tensor_tensor(
                        out=v3(cbuf[:, half, :N], t), in0=up[:, q, :, 1:1 + L], scalar=fcw[:, l, ch, 1:2],
                        in1=v3(cbuf[:, half, :N], t), op0=ALU.mult, op1=ALU.add),
                        reads=[("up", ub, q), ("c", pp, half), "cst"], writes=[("c", pp, half)])
                    P.op("dve", lambda e, cbuf=cbuf, half=half, up=up, q=q, ch=ch: e.scalar_tensor_tensor(
                        out=v3(cbuf[:, half, :N], t), in0=up[:, q, :, 0:L], scalar=fcw[:, l, ch, 0:1],
                        in1=v3(cbuf[:, half, :N], t), op0=ALU.mult, op1=ALU.add),
                        reads=[("up", ub, q), ("c", pp, half), "cst"], writes=[("c", pp, half)])
                if trim:
                    continue
                P.op("act", lambda e, cbuf=cbuf, pp=pp: e.activation(out=sbuf_[:, pp, :N], in_=cbuf[:, 0, :N], func=AF.Silu),
                     reads=[("c", pp, 0)], writes=[("s", pp)])
                if ui == NU - 1 and pp == 1:
                    preload_sqrt()
                gch = 2 * ui + pp
                P.op("dve", lambda e, cbuf=cbuf, pp=pp, gch=gch: e.tensor_tensor(out=gB[:, gch, :N], in0=sbuf_[:, pp, :N], in1=cbuf[:, 1, :N], op=ALU.mult),
                     reads=[("s", pp), ("c", pp, 1)], writes=[("g", gch)])
            P.op("act", lambda e, up=up, ui=ui: e.activation(out=nffn[:, l, 4 * ui:4 * ui + 4, seqsl(t), :], in_=up[:, :, :, L:L + 2], func=AF.Identity),
                 reads=[("up", ub, q) for q in range(4)], writes=["nffn"])
            if ui >= LAG and not trim:
                down0_mini(ui - LAG)
        if trim:
            return
        h1 = dict(banks=None, last={})
        H1P = ((0, 8), (8, 8), (16, 6))

        def half1_piece(pi):
            if h1["banks"] is None:
                h1["banks"] = [next_bank() for _ in range(4)]
                reserved_banks.update(h1["banks"])
            kk, nk = H1P[pi]
            s = load_piece([("ffn_w_down", l, kk * 128, nk * 128, 512, 512, 0, 0)])
            for mm in range(4):
                b = h1["banks"][mm]

                def fn(e, mm=mm, b=b, s=s, kk=kk, nk=nk):
                    ins = None
                    for k in range(nk):
                        kg = kk + k
                        ins = e.matmul(ps[b][:, :N], ring[s][:, k, mm * 128:(mm + 1) * 128], gB[:, kg, :N],
                                       start=(kg == 0), stop=(kg == 21))
                    return ins
                P.op("pe", fn, reads=[("g", kg) for kg in range(kk, kk + nk)] + [("slot", s)],
                     writes=[("ps", b)] if pi == 0 else [("psacc1", mm)])
                h1["last"][mm] = len(P.ops["pe"]) - 1

        down0_mini(NU - 2)
        half1_piece(0)
        half1_piece(1)
        down0_mini(NU - 1)
        banks = d0["banks"]
        reserved_banks.difference_update(banks)
        for mm in range(4):
            b = banks[mm]
            P.op("dve", lambda e, m=mm, b=b: e.tensor_tensor(out=xb[:, m, :N], in0=ps[b][:, :N], in1=xb[:, m, :N], op=ALU.add),
                 reads=[("ps", b), ("x", mm)], writes=[("x", mm)])
        half1_piece(2)
        banks = h1["banks"]
        reserved_banks.difference_update(banks)
        for mm in range(4):
            P.lastw[("ps", banks[mm])] = ("op", "pe", h1["last"][mm])
        for mm in range(4):
            m = 4 + mm
            b = banks[mm]
            P.op("dve", lambda e, m=m, b=b: e.tensor_tensor(out=xb[:, m, :N], in0=ps[b][:, :N], in1=xb[:, m, :N], op=ALU.add),
                 reads=[("ps", b), ("x", m)], writes=[("x", m)])

    first = True
    active = [t for ti, t in enumerate(tiles) if dbg_tiles is None or ti in dbg_tiles]

    def plan_xload(t):
        for c in range(8):
            sem = dsem(f"xin{c}")

            def fnx(e, t=t, sem=sem, c=c):
                e.dma_start(out=xb[:, c, :t.N], in_=xT[:, c, t.col0:t.col0 + t.N]).then_inc(dmasems[sem], 16)
                return None
            P.op("sp", fnx, writes=[("x", c)], dma_sem=sem)

    for ai, t in enumerate(active):
        N = t.N
        if ai == 0:
            plan_xload(t)
        for l in range(dbg_layers):
            rmsnorm(t, lambda c, l=l: gmix[:, l, c:c + 1], extra_reads=SETUP if first else (),
                    h_extra=["youtdma"] if l == 0 else ())
            first = False
            if dbg_stage < 1:
                continue
            if l % 2 == 0:
                even_mixer(t, l)
            else:
                odd_mixer(t, l)
            if dbg_stage < 2:
                continue
            rmsnorm(t, lambda c, l=l: gffn[:, l, c:c + 1])
            conv_ffn(t, l)
        if t.kind != "halo":
            rmsnorm(t, lambda c: gfin[:, c:c + 1], final=True)
        if ai + 1 < len(active):
            plan_xload(active[ai + 1])
        if t.kind != "halo":
            yo = YOUT.rearrange("p (c n) -> p c n", c=8)
            sem = dsem("yout")

            def fny(e, t=t, sem=sem):
                e.dma_start(out=yT[:, :, t.outcol:t.outcol + t.N], in_=yo[:, :, :t.N]).then_inc(dmasems[sem], 16)
                return None
            ev = P.op("sp", fny, reads=[("yout", c) for c in range(8)], dma_sem=sem)
            P.lastw["youtdma"] = ev

    sem = dsem("fin")
    for dst, srcb, res in ((o_pool, npool, "npool"), (o_ccv, nccv, "nccv"), (o_sc, nsc, "nsc"), (o_ffn, nffn, "nffn")):
        def fno(e, dst=dst, srcb=srcb, sem=sem):
            e.dma_start(out=dst, in_=srcb[:].rearrange("p a c s k -> p (a c s k)")).then_inc(dmasems[sem], 16)
            return None
        P.op("sp", fno, reads=[res], dma_sem=sem)

    finals = [("sem", n, P.dmasem_count[n]) for n in ("fin", "yout", "ov") if n in P.dmasem_count]


    if record:
        es.close()
        return pieces
    P.resolve()

    with nc.Block() as block:
        @block.tensor
        def _(e):
            P.emit("pe", e, engsem, dmasems)

        @block.scalar
        def _(e):
            P.emit("act", e, engsem, dmasems)

        @block.vector
        def _(e):
            P.emit("dve", e, engsem, dmasems)

        @block.gpsimd
        def _(e):
            P.emit("pool", e, engsem, dmasems)

        @block.sync
        def _(e):
            P.emit("sp", e, engsem, dmasems)
            for (_, n, v) in finals:
                e.wait_ge(dmasems[n], v)
    es.close()
    return nc


_NC_CACHE = {}


def _fm(a):
    a = np.asarray(a)
    C = a.shape[-1]
    T_ = a.shape[-2]
    lead = a.shape[:-2]
    b = a.reshape(lead + (T_, C // 128, 128))
    nd = b.ndim
    perm = (nd - 1,) + tuple(range(len(lead))) + (nd - 2, nd - 3)
    return np.ascontiguousarray(b.transpose(perm))


def kernel(x_prompt, x_sample, state_pool, state_ccv, state_sconv, state_ffn_conv,
           norm_mix_g, norm_ffn_g, norm_final_g,
           w_in_even, pool_w, pool_scale, ccv_w, ccv_b, ccv_ln_g, ccv_ln_b, w_out_even,
           w_in_odd, sgu_ln_g, sgu_ln_b, sgu_ws, sgu_b, sconv_w, w_out_odd,
           ffn_w_up, ffn_conv_w, ffn_conv_b, ffn_w_down):
    f32 = np.float32
    A = lambda a: np.ascontiguousarray(np.asarray(a, dtype=f32))
    x_prompt = A(x_prompt); x_sample = A(x_sample)
    perm = _unit_perm()

    def vec_fm(v):
        v = A(v)
        lead = v.shape[:-1]
        b = v.reshape(lead + (v.shape[-1] // 128, 128))
        nd = b.ndim
        return np.ascontiguousarray(b.transpose((nd - 1,) + tuple(range(nd - 1))))

    cst = np.zeros((128, CSTN), f32)

    def put(name, arr):
        o, s = CST[name]
        cst[:, o:o + s] = arr.reshape(128, s)
    put("gmix", vec_fm(norm_mix_g))
    put("gffn", vec_fm(norm_ffn_g))
    put("gfin", vec_fm(norm_final_g))
    put("pscale", vec_fm(pool_scale))
    put("ccvw", vec_fm(ccv_w).transpose(0, 1, 3, 2))
    put("ccvb", vec_fm(ccv_b))
    put("ccvg", vec_fm(ccv_ln_g))
    put("ccvbt", vec_fm(ccv_ln_b))
    put("scw", vec_fm(sconv_w).transpose(0, 1, 3, 2))
    fw = vec_fm(ffn_conv_w)[:, :, :, perm]
    put("fcw", fw.transpose(0, 1, 3, 2))
    put("fcb", vec_fm(ffn_conv_b)[:, :, perm])

    ws = A(sgu_ws)
    wsT = np.ascontiguousarray(ws.transpose(3, 0, 1, 2)).reshape(128, 2 * 4 * 128)
    wsTs = np.ascontiguousarray(ws[:, :, :64, :64].transpose(3, 0, 1, 2))
    wsTs = np.concatenate([wsTs, wsTs], axis=0).reshape(128, 2 * 4 * 64)
    tril = (np.arange(128)[None, :] >= np.arange(128)[:, None]).astype(f32)
    trils = np.concatenate([tril[:64, :64], tril[:64, :64]], axis=0)
    poolw = np.ascontiguousarray(A(pool_w).transpose(2, 0, 1, 3)).reshape(128, 2 * 4 * 128)

    shared = dict(
        cst=cst, sgug=A(sgu_ln_g).reshape(-1), sgubt=A(sgu_ln_b).reshape(-1), sgubias=A(sgu_b).reshape(-1),
        wsT=wsT, wsTs=wsTs, tril=tril, ident=np.eye(128, dtype=f32), trils=np.ascontiguousarray(trils), poolw=poolw,
        w_in_even=A(w_in_even), w_out_even=A(w_out_even), w_in_odd=A(w_in_odd), w_out_odd=A(w_out_odd),
        ffn_w_up=A(ffn_w_up), ffn_w_down=A(ffn_w_down),
    )

    xpT = _fm(x_prompt[0])
    xsT = _fm(x_sample)
    sp_fm = _fm(A(state_pool))
    scv_fm = _fm(A(state_ccv))
    ssc_fm = _fm(A(state_sconv))
    sff_fm = _fm(A(state_ffn_conv))[:, :, :, perm, :]

    in_maps = []
    for c in range(NCORES):
        a = c * OWN
        xT = np.zeros((128, 8, TOK), f32)
        if c > 0:
            xT[:, :, 0:HALO] = xpT[:, :, a - HALO:a]
        xT[:, :, HALO:HALO + OWN] = xpT[:, :, a:a + OWN]
        xs = xsT[:, NSS * c:NSS * c + NSS]
        xT[:, :, HALO + OWN:] = xs.transpose(0, 2, 1, 3).reshape(128, 8, NSS * LS)
        mask = np.full((HALO,), 0.0 if c == 0 else 1.0, f32)
        invc = np.zeros((4, 16), f32)
        for g, w in enumerate((2, 4, 8, 16)):
            if c == 0:
                invc[g] = 1.0 / np.minimum(np.arange(16) + 1, w)
            else:
                invc[g] = 1.0 / w
        sl = slice(NSS * c, NSS * c + NSS)
        m = dict(shared)
        m.update(
            xT=xT, mask=mask, invc=invc.reshape(-1),
            st_pool=np.ascontiguousarray(sp_fm[:, :, sl].transpose(0, 1, 3, 2, 4)).reshape(128, -1),
            st_ccv=np.ascontiguousarray(scv_fm[:, :, sl].transpose(0, 1, 3, 2, 4)).reshape(128, -1),
            st_sc=np.ascontiguousarray(ssc_fm[:, :, sl].transpose(0, 1, 3, 2, 4)).reshape(128, -1),
            st_ffn=np.ascontiguousarray(sff_fm[:, :, sl].transpose(0, 1, 3, 2, 4)).reshape(128, -1),
        )
        in_maps.append(m)

    if "nc" not in _NC_CACHE:
        _NC_CACHE["nc"] = build_program()
    nc = _NC_CACHE["nc"]
    res = run_bass_kernel_spmd(nc, in_maps, core_ids=list(range(NCORES)))
    R = res.results

    def tm(a):
        return np.ascontiguousarray(a.transpose(2, 1, 0).reshape(a.shape[2], -1))

    y_prompt = np.zeros((1, SEQ, D), f32)
    y_sample = np.zeros((32, LS, D), f32)
    for c in range(NCORES):
        yT = R[c]["yT"]
        y_prompt[0, c * OWN:(c + 1) * OWN] = tm(yT[:, :, :OWN])
        ys = yT[:, :, OWN:].reshape(128, 8, NSS, LS)
        for q in range(NSS):
            y_sample[NSS * c + q] = tm(ys[:, :, q, :])

    def states(key, npair, C, H, permute=None):
        pp = np.zeros((npair, 1, H, C * 128), f32)
        ss = np.zeros((npair, 32, H, C * 128), f32)
        for c in range(NCORES):
            a = R[c][key].reshape(128, npair, C, 5, H)
            if permute is not None:
                b = np.zeros_like(a)
                b[:, :, permute] = a
                a = b
            full = a.transpose(1, 3, 4, 2, 0).reshape(npair, 5, H, C * 128)
            if c == NCORES - 1:
                pp[:, 0] = full[:, 0]
            ss[:, NSS * c:NSS * c + NSS] = full[:, 1:5]
        return pp, ss

    pool_p, pool_s = states("o_pool", 2, 4, 15)
    ccv_p, ccv_s = states("o_ccv", 2, 4, 30)
    sc_p, sc_s = states("o_sc", 2, 4, 2)
    ffn_p, ffn_s = states("o_ffn", 4, 44, 2, permute=perm)
    v_s = np.zeros((2, 32, LS, 512), f32)
    for c in range(NCORES):
        v_s[:, NSS * c:NSS * c + NSS] = R[c]["o_v"]
    return (y_prompt, y_sample, pool_p, pool_s, ccv_p, ccv_s, sc_p, sc_s, v_s, ffn_p, ffn_s)
```

```python
import numpy as np
import concourse.bass as bass
import concourse.mybir as mybir
from concourse.bass_utils import run_bass_kernel_spmd

F32 = mybir.dt.float32
BF16 = mybir.dt.bfloat16
ALU = mybir.AluOpType
AF = mybir.ActivationFunctionType

NCORES = 8
D = 1024
SEQ = 16384
OWN = SEQ // NCORES
HALO = 288
NSS = 4
LS = 64
TOK = HALO + OWN + NSS * LS
NOUT = OWN + NSS * LS
DFF = 2816
NU = 11
EPS = 1e-6
NSLOT = 6

CST = {}
_off = 0
for _n, _sz in [("gmix", 32), ("gffn", 32), ("gfin", 8), ("pscale", 8), ("ccvw", 2 * 4 * 31),
                ("ccvb", 8), ("ccvg", 8), ("ccvbt", 8), ("scw", 2 * 4 * 3), ("fcw", 4 * 44 * 3),
                ("fcb", 4 * 44)]:
    CST[_n] = (_off, _sz)
    _off += _sz
CSTN = _off


def _unit_perm():
    perm = []
    for i in range(NU):
        perm += [2 * i, 2 * i + 1, 22 + 2 * i, 22 + 2 * i + 1]
    return np.array(perm)


class Plan:
    ENG = ("pe", "act", "dve", "pool", "sp")

    def __init__(self):
        self.ops = {e: [] for e in self.ENG}
        self.lastw = {}
        self.readers = {}
        self.dmasem_count = {}

    def op(self, eng, fn, reads=(), writes=(), dma_sem=None):
        idx = len(self.ops[eng])
        deps = set()
        for r in reads:
            if r in self.lastw:
                deps.add(self.lastw[r])
        for w in writes:
            if w in self.lastw:
                deps.add(self.lastw[w])
            for ev in self.readers.get(w, ()):
                deps.add(ev)
        if dma_sem is not None:
            self.dmasem_count[dma_sem] = self.dmasem_count.get(dma_sem, 0) + 16
            ev = ("sem", dma_sem, self.dmasem_count[dma_sem])
        else:
            ev = ("op", eng, idx)
        fdeps = set()
        for d in deps:
            if d[0] == "op" and d[1] == eng:
                if eng == "pe":
                    continue
                if idx - d[2] >= 3:
                    continue
            fdeps.add(d)
        self.ops[eng].append(dict(fn=fn, deps=fdeps, flag=False, dma=dma_sem is not None))
        for r in reads:
            self.readers.setdefault(r, []).append(ev)
        for w in writes:
            self.lastw[w] = ev
            self.readers[w] = []
        return ev

    def resolve(self):
        for e in self.ENG:
            for o in self.ops[e]:
                for d in o["deps"]:
                    if d[0] == "op":
                        self.ops[d[1]][d[2]]["flag"] = True
        self.count = {}
        for e in self.ENG:
            c = 0
            lst = []
            for o in self.ops[e]:
                if o["flag"]:
                    c += 1
                lst.append(c)
            self.count[e] = lst

    def emit(self, eng, e, engsem, dmasems):
        known = {}
        for o in self.ops[eng]:
            need = {}
            for d in o["deps"]:
                if d[0] == "op":
                    k = ("e", d[1])
                    v = self.count[d[1]][d[2]]
                else:
                    k = ("d", d[1])
                    v = d[2]
                if need.get(k, 0) < v:
                    need[k] = v
            for k, v in need.items():
                if known.get(k, 0) < v:
                    sem = engsem[k[1]] if k[0] == "e" else dmasems[k[1]]
                    e.wait_ge(sem, v)
                    known[k] = v
            ins = o["fn"](e)
            if o["flag"]:
                ins.then_inc(engsem[eng], 1)


def build_program(dbg_tiles=None, dbg_layers=4, dbg_stage=99, record=False, pieces=None):
    if not record and pieces is None:
        pieces = build_program(dbg_tiles, dbg_layers, dbg_stage, record=True, pieces=[])
    nc = bass.Bass("TRN2", target_bir_lowering=False)
    P = Plan()

    def din(name, shape):
        return nc.dram_tensor(name, list(shape), F32, kind="ExternalInput").ap()

    def dout(name, shape):
        return nc.dram_tensor(name, list(shape), F32, kind="ExternalOutput").ap()

    xT = din("xT", [128, 8, TOK])
    cst_d = din("cst", [128, CSTN])
    mask_d = din("mask", [HALO])
    invc_d = din("invc", [4 * 16])
    sgug_d = din("sgug", [2 * 512])
    sgub_d = din("sgubt", [2 * 512])
    sgubias_d = din("sgubias", [2 * 4 * 128])
    wsT_d = din("wsT", [128, 2 * 4 * 128])
    wsTs_d = din("wsTs", [128, 2 * 4 * 64])
    tril_d = din("tril", [128, 128])
    trils_d = din("trils", [128, 64])
    poolw_d = din("poolw", [128, 2 * 4 * 128])
    ident_d = din("ident", [128, 128])
    stpool_d = din("st_pool", [128, 2 * 4 * NSS * 15])
    stccv_d = din("st_ccv", [128, 2 * 4 * NSS * 30])
    stsc_d = din("st_sc", [128, 2 * 4 * NSS * 2])
    stffn_d = din("st_ffn", [128, 4 * 44 * NSS * 2])
    w_in_even = din("w_in_even", [2, D, 1536])
    w_out_even = din("w_out_even", [2, D, D])
    w_in_odd = din("w_in_odd", [2, D, 2560])
    w_out_odd = din("w_out_odd", [2, D, D])
    ffn_w_up = din("ffn_w_up", [4, D, 2 * DFF])
    ffn_w_down = din("ffn_w_down", [4, DFF, D])

    yT = dout("yT", [128, 8, NOUT])
    o_pool = dout("o_pool", [128, 2 * 4 * 5 * 15])
    o_ccv = dout("o_ccv", [128, 2 * 4 * 5 * 30])
    o_sc = dout("o_sc", [128, 2 * 4 * 5 * 2])
    o_ffn = dout("o_ffn", [128, 4 * 44 * 5 * 2])
    o_v = dout("o_v", [2, NSS, LS, 512])

    from contextlib import ExitStack
    es = ExitStack()

    def sb(name, shape, dt=F32):
        return es.enter_context(nc.sbuf_tensor("sb_" + name, list(shape), dt))

    xb = sb("xb", [128, 8, 512])
    hb = sb("hb", [128, 8, 512], BF16)
    yb = sb("yb", [128, 8, 512], BF16)
    rstd = sb("rstd", [128, 512])
    sd = sb("sd", [128, 512])
    cst = sb("cst", [128, CSTN])
    maskb = sb("maskb", [128, HALO])
    invc = sb("invc", [128, 4, 16])
    sgug = sb("sgug", [128, 2, 512])
    sgubt = sb("sgubt", [128, 2, 512])
    sgubias = sb("sgubias", [128, 8, 128])
    wsT = sb("wsT", [128, 8, 128], BF16)
    wsTs = sb("wsTs", [128, 8, 64], BF16)
    poolw = sb("poolw", [128, 8, 128], BF16)
    ones = sb("ones", [128, 128], BF16)
    dmy = sb("dmy", [128, 8])
    ident = sb("ident", [128, 128])
    stpool = sb("stpool", [128, 2, 4, NSS, 15])
    stccv = sb("stccv", [128, 2, 4, NSS, 30])
    stsc = sb("stsc", [128, 2, 4, NSS, 2])
    stffn = sb("stffn", [128, 4, 44, NSS, 2])
    npool = sb("npool", [128, 2, 4, 5, 15])
    nccv = sb("nccv", [128, 2, 4, 5, 30])
    nsc = sb("nsc", [128, 2, 4, 5, 2])
    nffn = sb("nffn", [128, 4, 44, 5, 2])
    ring = [sb(f"ring{i}", [128, 8, 512], BF16) for i in range(NSLOT)]
    ARENA_F = 18240
    arena = sb("arena", [128, ARENA_F])
    ps = [es.enter_context(nc.psum_tensor(f"ps{i}", [128, 512], F32)) for i in range(8)]

    import os
    if os.environ.get("SBUF_FREE"):
        print("SBUF free bytes/partition:", nc.sbuf_bytes_remaining)
    engsem = {e: es.enter_context(nc.semaphore(f"s_{e}")) for e in Plan.ENG}
    dmasems = {}

    def dsem(name):
        if name not in dmasems:
            dmasems[name] = es.enter_context(nc.semaphore(f"d_{name}"))
        return name

    def carve(off, words):
        return arena[:, off:off + words]

    A_ZAB = carve(0, 4 * 527)
    A_ZBB = carve(2108, 4 * 542)
    A_T1 = carve(4276, 527)
    A_T2 = carve(4803, 527)
    A_D = carve(5330, 1024).bitcast(BF16)
    A_SIG = carve(6354, 1024)
    A_CB = carve(7378, 2048)
    A_MEAN = carve(9426, 512)
    A_MSQ = carve(9938, 512)
    A_LRS = carve(10450, 512)
    A_TT = carve(10962, 1024)
    A_ZBH = carve(13010, 1084).bitcast(BF16)
    A_DG = carve(14094, 4096).bitcast(BF16)
    O_U = carve(0, 2048)
    O_VN = carve(2048, 1024).bitcast(BF16)
    O_V32 = carve(3072, 2048)
    O_VT = carve(5120, 1024)
    O_CX = carve(6144, 4 * 514)
    O_XIN = carve(8200, 1024)
    O_BG = carve(9224, 1024)
    O_CZ = carve(10248, 1024)
    O_MT = carve(11272, 1024)
    O_BN = carve(12296, 64)
    F_UP = carve(0, 2 * 4 * 514)
    F_C = carve(4112, 4 * 512)
    F_S = carve(6160, 1024)
    F_G = carve(7184, 5632).bitcast(BF16)
    F_PT = carve(12816, 1024)
    SQB = carve(0, 2048).bitcast(BF16)
    A_LNB = carve(10962 + 1024, 1070)
    YOUT = carve(2048, 4096)

    E_CBB = carve(11986, 1024).bitcast(BF16)
    assert 11986 + 1024 <= ARENA_F
    E_SQB = A_TT.bitcast(BF16)

    cview = lambda name: cst[:, CST[name][0]:CST[name][0] + CST[name][1]]
    gmix = cview("gmix").rearrange("p (l c) -> p l c", l=4)
    gffn = cview("gffn").rearrange("p (l c) -> p l c", l=4)
    gfin = cview("gfin")
    pscale = cview("pscale").rearrange("p (a c) -> p a c", a=2)
    ccvw = cview("ccvw").rearrange("p (a c k) -> p a c k", a=2, c=4)
    ccvb = cview("ccvb").rearrange("p (a c) -> p a c", a=2)
    ccvg = cview("ccvg").rearrange("p (a c) -> p a c", a=2)
    ccvbt = cview("ccvbt").rearrange("p (a c) -> p a c", a=2)
    scw = cview("scw").rearrange("p (a c k) -> p a c k", a=2, c=4)
    fcw = cview("fcw").rearrange("p (l c k) -> p l c k", l=4, c=44)
    fcb = cview("fcb").rearrange("p (l c) -> p l c", l=4)

    bank_ctr = [0]
    dg_ctr = [0]
    pt_ctr = [0]

    reserved_banks = set()

    def next_bank():
        while True:
            b = bank_ctr[0] % 8
            bank_ctr[0] += 1
            if b not in reserved_banks:
                return b

    WTS = dict(w_in_even=w_in_even, w_out_even=w_out_even, w_in_odd=w_in_odd, w_out_odd=w_out_odd,
               ffn_w_up=ffn_w_up, ffn_w_down=ffn_w_down)
    piece_ctr = [0]
    emitted = [0]
    KPRE = NSLOT - 3

    def emit_load(j):
        s = j % NSLOT
        sem = dsem(f"w{s}")
        for i, (wn, idx, r0, nrows, c0, ncol, k0, co) in enumerate(pieces[j]):
            src = WTS[wn][idx][r0:r0 + nrows, c0:c0 + ncol].rearrange("(k p) n -> p k n", p=128)
            nk = nrows // 128

            def fn(e, src=src, s=s, k0=k0, nk=nk, co=co, ncol=ncol, sem=sem):
                e.dma_start(out=ring[s][:, k0:k0 + nk, co:co + ncol], in_=src).then_inc(dmasems[sem], 16)
                return None
            P.op("pool", fn, writes=[("slot", s)] if i == 0 else [], dma_sem=sem)
            if i > 0:
                P.lastw[("slot", s)] = ("sem", sem, P.dmasem_count[sem])

    def load_piece(descs):
        i = piece_ctr[0]
        piece_ctr[0] += 1
        if record:
            pieces.append(list(descs))
            return i % NSLOT
        assert pieces[i] == list(descs)
        while emitted[0] <= min(i + KPRE, len(pieces) - 1):
            emit_load(emitted[0])
            emitted[0] += 1
        return i % NSLOT

    def wcols(w2d, c0, ncol):
        return w2d.rearrange("(k p) n -> p k n", p=128)[:, :, c0:c0 + ncol]

    def mm_group(lhs, rhs, K, M, N, reads, bank=None, start=True, stop=True):
        b = next_bank() if bank is None else bank

        def fn(e):
            ins = None
            for k in range(K):
                ins = e.matmul(ps[b][:M, :N], lhs(k), rhs(k), start=(start and k == 0), stop=(stop and k == K - 1))
            return ins
        P.op("pe", fn, reads=reads, writes=[("ps", b)])
        return b

    def ld(dst, src, res):
        sem = dsem("c_" + res)

        def fn(e):
            e.dma_start(out=dst, in_=src).then_inc(dmasems[sem], 16)
            return None
        P.op("sp", fn, writes=[res], dma_sem=sem)

    ld(cst[:], cst_d, "cst")
    ld(ident[:], ident_d, "ident")
    ld(maskb[:], mask_d.partition_broadcast(128), "maskb")
    ld(invc[:], invc_d.partition_broadcast(128).rearrange("p (g k) -> p g k", g=4), "invc")
    ld(sgug[:], sgug_d.partition_broadcast(128).rearrange("p (a n) -> p a n", a=2), "sgug")
    ld(sgubt[:], sgub_d.partition_broadcast(128).rearrange("p (a n) -> p a n", a=2), "sgubt")
    ld(sgubias[:], sgubias_d.partition_broadcast(128).rearrange("p (a n) -> p a n", a=8), "sgubias")
    ld(stpool[:], stpool_d.rearrange("p (a c s k) -> p a c s k", a=2, c=4, s=NSS), "stpool")
    ld(stccv[:], stccv_d.rearrange("p (a c s k) -> p a c s k", a=2, c=4, s=NSS), "stccv")
    ld(stsc[:], stsc_d.rearrange("p (a c s k) -> p a c s k", a=2, c=4, s=NSS), "stsc")
    ld(stffn[:], stffn_d.rearrange("p (a c s k) -> p a c s k", a=4, c=44, s=NSS), "stffn")
    T_WS = carve(0, 1024).rearrange("p (a n) -> p a n", a=8)
    T_WSS = carve(1024, 512).rearrange("p (a n) -> p a n", a=8)
    T_TR = carve(1536, 128)
    T_TRS = carve(1664, 64)
    T_PW = carve(1728, 1024).rearrange("p (a n) -> p a n", a=8)
    ld(T_WS, wsT_d.rearrange("p (a n) -> p a n", a=8), "t_ws")
    ld(T_WSS, wsTs_d.rearrange("p (a n) -> p a n", a=8), "t_wss")
    ld(T_TR, tril_d, "t_tr")
    ld(T_TRS, trils_d, "t_trs")
    ld(T_PW, poolw_d.rearrange("p (a n) -> p a n", a=8), "t_pw")

    P.op("dve", lambda e: e.tensor_tensor(out=wsT[:], in0=T_WS, in1=T_TR.unsqueeze(1).broadcast_to([128, 8, 128]), op=ALU.mult),
         reads=["t_ws", "t_tr"], writes=["wsT", "setup"])
    P.op("dve", lambda e: e.tensor_tensor(out=wsTs[:], in0=T_WSS, in1=T_TRS.unsqueeze(1).broadcast_to([128, 8, 64]), op=ALU.mult),
         reads=["t_wss", "t_trs"], writes=["wsTs", "setup2"])
    P.op("act", lambda e: e.activation(out=poolw[:], in_=T_PW, func=AF.Identity), reads=["t_pw"], writes=["poolw", "setup3"])
    P.op("dve", lambda e: e.memset(ones[:], 1.0), writes=["ones"])
    P.op("dve", lambda e: e.memset(dmy[:], 1.0), writes=["dmy"])
    P.op("dve", lambda e: e.memset(npool[:], 0.0), writes=["npool"])
    P.op("dve", lambda e: e.memset(nccv[:], 0.0), writes=["nccv"])
    P.op("dve", lambda e: e.memset(nsc[:], 0.0), writes=["nsc"])
    P.op("dve", lambda e: e.memset(nffn[:], 0.0), writes=["nffn"])

    SETUP = ["setup", "setup2", "setup3"]

    class T:
        pass

    tiles = []
    t0 = T(); t0.col0 = 0; t0.N = HALO; t0.S = 1; t0.L = HALO; t0.kind = "halo"
    t0.vblocks = [(0, 32), (32, 128), (160, 128)]
    tiles.append(t0)
    for i in range(4):
        t = T(); t.col0 = HALO + 512 * i; t.N = 512; t.S = 1; t.L = 512; t.kind = "own"; t.outcol = 512 * i
        t.vblocks = [(128 * b, 128) for b in range(4)]
        t.first = (i == 0)
        tiles.append(t)
    ts_ = T(); ts_.col0 = HALO + OWN; ts_.N = NSS * LS; ts_.S = NSS; ts_.L = LS; ts_.kind = "sample"; ts_.outcol = OWN
    ts_.vblocks = [(64 * q, 64) for q in range(NSS)]
    tiles.append(ts_)

    def v4(flat, C, S, W):
        return flat[:, 0:C * S * W].rearrange("p (c s w) -> p c s w", c=C, s=S)

    def pv(b, t):
        return ps[b][:, 0:t.N].rearrange("p (s l) -> p s l", s=t.S)

    def v3(flat2d, t):
        return flat2d.rearrange("p (s l) -> p s l", s=t.S)

    def seqsl(t):
        return slice(0, 1) if t.kind != "sample" else slice(1, 5)

    def rmsnorm(t, gain, final=False, extra_reads=(), h_extra=()):
        N = t.N
        sq = SQB.rearrange("p (c n) -> p c n", c=8)
        yo = YOUT.rearrange("p (c n) -> p c n", c=8)
        for c in range(8):
            P.op("act", lambda e, c=c: e.activation(out=sq[:, c, :N], in_=xb[:, c, :N], func=AF.Square),
                 reads=[("x", c)] + list(extra_reads), writes=[("sq", c)])
            if final:
                P.op("act", lambda e, c=c: e.activation(out=yo[:, c, :N], in_=xb[:, c, :N], func=AF.Identity, scale=gain(c)),
                     reads=[("x", c), "cst"], writes=[("yout", c)])
        import os
        NS = int(os.environ.get("DBG_NORM", "9"))
        if NS < 1:
            return
        b = next_bank()
        for k in range(8):
            P.op("pe", lambda e, k=k, b=b: e.matmul(ps[b][:, :N], ones[:, :], sq[:, k, :N], start=(k == 0), stop=(k == 7)),
                 reads=[("sq", k), "ones"], writes=[("ps", b)] if k == 0 else [("psacc_n", 0)])
        P.lastw[("ps", b)] = ("op", "pe", len(P.ops["pe"]) - 1)
        if NS < 2:
            return
        P.op("act", lambda e: e.activation(out=sd[:, :N], in_=ps[b][:, :N], func=AF.Ln, bias=EPS, scale=1.0 / D),
             reads=[("ps", b)], writes=["sd"])
        if NS < 3:
            return
        P.op("act", lambda e: e.activation(out=rstd[:, :N], in_=sd[:, :N], func=AF.Exp, scale=-0.5), reads=["sd"], writes=["rstd"])
        if NS < 4:
            return
        if t.kind == "halo":
            P.op("dve", lambda e: e.tensor_tensor(out=rstd[:, :N], in0=rstd[:, :N], in1=maskb[:, :N], op=ALU.mult),
                 reads=["rstd", "maskb"], writes=["rstd"])
        if not final:
            for c in range(8):
                P.op("dve", lambda e, c=c: e.scalar_tensor_tensor(out=hb[:, c, :N], in0=xb[:, c, :N], scalar=gain(c),
                                                                in1=rstd[:, :N], op0=ALU.mult, op1=ALU.mult),
                     reads=[("x", c), "rstd", "cst"] + list(h_extra), writes=[("h", c)])
        else:
            for c in range(8):
                P.op("dve", lambda e, c=c: e.tensor_tensor(out=yo[:, c, :N], in0=yo[:, c, :N], in1=rstd[:, :N], op=ALU.mult),
                     reads=[("yout", c), "rstd"], writes=[("yout", c)])

    HALL = [("h", c) for c in range(8)]

    def proj_group(slot, m, t, K=8, src=None, srcres=None):
        src = hb if src is None else src
        N = t.N
        return mm_group(lambda k: ring[slot][:, k, m * 128:(m + 1) * 128], lambda k: src[:, k, :N], K, 128, N,
                        reads=(HALL if srcres is None else srcres) + [("slot", slot)])

    dgbuf_all = A_DG.rearrange("p (i n) -> p i n", i=64)
    prebuilt = [None]

    pending_builds = []

    def prebuild_start(prn):
        pending_builds[:] = [(prn, j, k) for j in (0, 1) for k in range(31)]

    def prebuild_some(n):
        for _ in range(n):
            if not pending_builds:
                return
            prn, j, k = pending_builds.pop(0)
            di = j * 32 + k
            P.op("act", lambda e, prn=prn, j=j, k=k, di=di: e.activation(out=dgbuf_all[:, di, :], in_=ident[:], func=AF.Identity,
                                                                       scale=ccvw[:, prn, j, k:k + 1]),
                 reads=["cst", "ident"], writes=[("dg", j, k)])
            if not pending_builds:
                prebuilt[0] = prn

    def preload_sqrt():
        P.op("act", lambda e: e.activation(out=dmy[:, 0:1], in_=dmy[:, 1:2], func=AF.Ln), reads=["dmy"], writes=["dmy2"])

    def proj_kouter(slot, ms, t):
        N = t.N
        banks = [next_bank() for _ in ms]
        for k in range(8):
            def fn(e, k=k):
                ins = None
                for bi_, m in enumerate(ms):
                    ins = e.matmul(ps[banks[bi_]][:, :N], ring[slot][:, k, m * 128:(m + 1) * 128], hb[:, k, :N],
                                   start=(k == 0), stop=(k == 7))
                return ins
            P.op("pe", fn, reads=[("h", k), ("slot", slot)], writes=[("ps", b) for b in banks] if k == 0 else [("psacc_k", slot)])
        for b in banks:
            P.lastw[("ps", b)] = ("op", "pe", len(P.ops["pe"]) - 1)
        return banks

    def out_proj(t, wname, widx, korder=(0, 1, 2, 3, 4, 5, 6, 7)):
        N = t.N
        for half in range(2):
            s = load_piece([(wname, widx, 0, 1024, half * 512, 512, 0, 0)])
            banks = [next_bank() for _ in range(4)]
            if half == 0:
                for ki, k in enumerate(korder):
                    def fn(e, k=k, ki=ki, s=s, banks=banks):
                        ins = None
                        for mm in range(4):
                            ins = e.matmul(ps[banks[mm]][:, :N], ring[s][:, k, mm * 128:(mm + 1) * 128], yb[:, k, :N],
                                           start=(ki == 0), stop=(ki == 7))
                        return ins
                    P.op("pe", fn, reads=[("y", k), ("slot", s)], writes=[("ps", b) for b in banks] if ki == 0 else [("psacc_o", half)])
                for b in banks:
                    P.lastw[("ps", b)] = ("op", "pe", len(P.ops["pe"]) - 1)
            else:
                for mm in range(4):
                    def fn(e, mm=mm, s=s, b=banks[mm]):
                        ins = None
                        for k in range(8):
                            ins = e.matmul(ps[b][:, :N], ring[s][:, k, mm * 128:(mm + 1) * 128], yb[:, k, :N],
                                           start=(k == 0), stop=(k == 7))
                        return ins
                    P.op("pe", fn, reads=[("y", k) for k in range(8)] + [("slot", s)], writes=[("ps", banks[mm])])
            for mm in range(4):
                m = half * 4 + mm
                b = banks[mm]
                P.op("dve", lambda e, m=m, b=b: e.tensor_tensor(out=xb[:, m, :N], in0=ps[b][:, :N], in1=xb[:, m, :N], op=ALU.add),
                     reads=[("ps", b), ("x", m)], writes=[("x", m)])

    def even_mixer(t, l):
        pr = l // 2
        N, S, L = t.N, t.S, t.L
        zab = v4(A_ZAB, 4, S, 15 + L)
        zbb = v4(A_ZBB, 4, S, 30 + L)
        win = w_in_even[pr]
        if t.kind == "sample":
            P.op("act", lambda e: e.activation(out=zab[:, :, :, 0:15], in_=stpool[:, pr], func=AF.Identity),
                 reads=["rstd", "stpool", "youtdma"], writes=[("zab", m) for m in range(4)])
            P.op("act", lambda e: e.activation(out=zbb[:, :, :, 0:30], in_=stccv[:, pr], func=AF.Identity),
                 reads=["rstd", "stccv", "youtdma"], writes=[("zbb", m) for m in range(4)])
        else:
            P.op("act", lambda e: e.activation(out=zab[:, :, :, 0:15], in_=npool[:, pr, :, 0:1, :], func=AF.Identity),
                 reads=["rstd", "npool", "youtdma"], writes=[("zab", m) for m in range(4)])
            P.op("act", lambda e: e.activation(out=zbb[:, :, :, 0:30], in_=nccv[:, pr, :, 0:1, :], func=AF.Identity),
                 reads=["rstd", "nccv", "youtdma"], writes=[("zbb", m) for m in range(4)])
        dg = A_DG.rearrange("p (i n) -> p i n", i=64)
        zbh = v4(A_ZBH, 4, S, 30 + L)

        def build_diags(j, ks):
            for k in ks:
                di = (j % 2) * 32 + k
                if k % 2 == 0:
                    P.op("act", lambda e, j=j, k=k, di=di: e.activation(out=dg[:, di, :], in_=ident[:], func=AF.Identity,
                                                                       scale=ccvw[:, pr, j, k:k + 1]),
                         reads=["cst", "ident", "rstd"], writes=[("dg", j % 2, k)])
                else:
                    P.op("dve", lambda e, j=j, k=k, di=di: e.tensor_scalar(out=dg[:, di, :], in0=ident[:], scalar1=ccvw[:, pr, j, k:k + 1],
                                                                         scalar2=None, op0=ALU.mult),
                         reads=["cst", "ident", "rstd"], writes=[("dg", j % 2, k)])
        s0 = load_piece([("w_in_even", pr, 0, 1024, 0, 512, 0, 0)])
        sa = load_piece([("w_in_even", pr, 0, 1024, 512, 512, 0, 0)])
        sg = load_piece([("w_in_even", pr, 0, 1024, 1024, 512, 0, 0)])
        zbanks = proj_kouter(s0, [0, 1, 2, 3], t)
        for m in range(4):
            b = zbanks[m]
            P.op("act", lambda e, m=m, b=b: e.activation(out=zab[:, m, :, 15:15 + L], in_=pv(b, t), func=AF.Identity),
                 reads=[("ps", b)], writes=[("zab", m)])
        sig = A_SIG.rearrange("p (i n) -> p i n", i=2)
        for m in range(4):
            ba = proj_group(sa, m, t)
            bg = proj_group(sg, m, t)
            i = m % 2
            P.op("act", lambda e, i=i, bg=bg: e.activation(out=sig[:, i, :N], in_=ps[bg][:, :N], func=AF.Sigmoid),
                 reads=[("ps", bg)], writes=[("sig", i)])
            if m == 3:
                preload_sqrt()
            P.op("dve", lambda e, m=m, i=i, ba=ba: e.tensor_tensor(out=zbb[:, m, :, 30:30 + L], in0=pv(ba, t), in1=v3(sig[:, i, :N], t), op=ALU.mult),
                 reads=[("ps", ba), ("sig", i)], writes=[("zbb", m)])
            P.op("act", lambda e, m=m: e.activation(out=zbh[:, m], in_=zbb[:, m], func=AF.Identity),
                 reads=[("zbb", m)], writes=[("zbh", m)])
            if prebuilt[0] != pr:
                build_diags(0, range(8 * m, min(31, 8 * m + 8)))
        if prebuilt[0] != pr:
            build_diags(1, range(31))
        prebuilt[0] = None
        P.op("act", lambda e: e.activation(out=npool[:, pr, :, seqsl(t), :], in_=zab[:, :, :, L:L + 15], func=AF.Identity),
             reads=[("zab", m) for m in range(4)], writes=["npool"])
        P.op("act", lambda e: e.activation(out=nccv[:, pr, :, seqsl(t), :], in_=zbb[:, :, :, L:L + 30], func=AF.Identity),
             reads=[("zbb", m) for m in range(4)], writes=["nccv"])
        W = 15 + L
        T1 = A_T1[:, 0:S * W].rearrange("p (s w) -> p s w", s=S)
        T2 = A_T2[:, 0:S * W].rearrange("p (s w) -> p s w", s=S)
        dB = A_D.rearrange("p (c n) -> p c n", c=4)
        for g in range(4):
            z = zab[:, g]
            cur = z
            curres = ("zab", g)
            bufs = [(T1, "T1"), (T2, "T2")]
            sh = 1
            for step in range(g + 1):
                dst, dres = bufs[step % 2]
                lo = 2 * sh - 1
                P.op("dve", lambda e, dst=dst, cur=cur, lo=lo, sh=sh: e.tensor_tensor(
                    out=dst[:, :, lo:W], in0=cur[:, :, lo:W], in1=cur[:, :, lo - sh:W - sh], op=ALU.add),
                    reads=[curres], writes=[dres])
                cur, curres = dst, dres
                sh *= 2
            wlen = 2 ** (g + 1)
            P.op("dve", lambda e, g=g, cur=cur, z=z, wlen=wlen: e.scalar_tensor_tensor(
                out=v3(dB[:, g, :N], t), in0=cur[:, :, 15:W], scalar=1.0 / wlen, in1=z[:, :, 15:W],
                op0=ALU.mult, op1=ALU.subtract),
                reads=[curres, ("zab", g)], writes=[("d", g)])
            if t.kind == "own" and t.first:
                other = bufs[(g + 1) % 2]
                P.op("dve", lambda e, g=g, cur=cur, other=other: e.tensor_tensor(
                    out=other[0][:, 0, 0:16], in0=cur[:, 0, 15:31], in1=invc[:, g, :], op=ALU.mult),
                    reads=[curres, "invc"], writes=[other[1]])
                P.op("dve", lambda e, g=g, z=z, other=other: e.tensor_tensor(
                    out=dB[:, g, 0:16], in0=other[0][:, 0, 0:16], in1=z[:, 0, 15:31], op=ALU.subtract),
                    reads=[other[1], ("zab", g)], writes=[("d", g)])
        cb = A_CB.rearrange("p (c n) -> p c n", c=4)
        dg = A_DG.rearrange("p (i n) -> p i n", i=64)
        cbb = E_CBB.rearrange("p (c n) -> p c n", c=4)
        sqb = SQB.rearrange("p (c n) -> p c n", c=8)
        b1 = next_bank()
        b2 = next_bank()
        reserved_banks.update((b1, b2))

        def ln_stats(j):
            P.op("pe", lambda e, j=j: e.matmul(ps[b1][:, :N], ones[:, :], cbb[:, j, :N], start=(j == 0), stop=(j == 3)),
                 reads=[("cbb", j), "ones"], writes=[("ps", b1)] if j == 0 else [("psacc_l", 1)])
            P.op("pe", lambda e, j=j: e.matmul(ps[b2][:, :N], ones[:, :], sqb[:, j, :N], start=(j == 0), stop=(j == 3)),
                 reads=[("lsq", j), "ones"], writes=[("ps", b2)] if j == 0 else [("psacc_l", 2)])
            if j == 3:
                P.lastw[("ps", b1)] = ("op", "pe", len(P.ops["pe"]) - 2)
                P.lastw[("ps", b2)] = ("op", "pe", len(P.ops["pe"]) - 1)
        for j in range(4):
            b = next_bank()
            for k in range(31):
                di = (j % 2) * 32 + k
                P.op("pe", lambda e, j=j, k=k, di=di, b=b: e.matmul(pv(b, t), dg[:, di, :], zbh[:, j, :, k:k + L],
                                                                  start=(k == 0), stop=(k == 30)),
                     reads=[("dg", j % 2, k), ("zbh", j)], writes=[("ps", b)] if k == 0 else [("psacc_c", j)])
            P.lastw[("ps", b)] = ("op", "pe", len(P.ops["pe"]) - 1)
            P.op("act", lambda e, j=j, b=b: e.activation(out=cb[:, j, :N], in_=ps[b][:, :N], func=AF.Identity, bias=ccvb[:, pr, j:j + 1]),
                 reads=[("ps", b), "cst"], writes=[("cb", j)])
            P.op("act", lambda e, j=j, b=b: e.activation(out=cbb[:, j, :N], in_=ps[b][:, :N], func=AF.Identity, bias=ccvb[:, pr, j:j + 1]),
                 reads=[("ps", b), "cst"], writes=[("cbb", j)])
            P.op("act", lambda e, j=j, b=b: e.activation(out=sqb[:, j, :N], in_=ps[b][:, :N], func=AF.Square, bias=ccvb[:, pr, j:j + 1]),
                 reads=[("ps", b), "cst"] + [("zab", m) for m in range(4)] + [("d", g) for g in range(4)] + ["T1", "T2", "npool"],
                 writes=[("lsq", j), ("zab", 0), ("zab", 1)])
            if j + 2 < 4:
                build_diags(j + 2, range(31))
            if j >= 1:
                ln_stats(j - 1)
        for g in range(4):
            b = mm_group(lambda k, g=g: poolw[:, pr * 4 + g, :], lambda k, g=g: dB[:, g, :N], 1, 128, N,
                         reads=[("d", g), "poolw"])
            P.op("act", lambda e, g=g, b=b: e.activation(out=yb[:, g, :N], in_=ps[b][:, :N], func=AF.Identity, scale=pscale[:, pr, g:g + 1]),
                 reads=[("ps", b), "cst"], writes=[("y", g)])
        ln_stats(3)
        reserved_banks.difference_update((b1, b2))
        P.op("act", lambda e: e.activation(out=A_MEAN[:, :N], in_=ps[b1][:, :N], func=AF.Identity, scale=1.0 / 512),
             reads=[("ps", b1)], writes=["mean"])
        P.op("act", lambda e: e.activation(out=A_MSQ[:, :N], in_=ps[b1][:, :N], func=AF.Square, scale=1.0 / 512),
             reads=[("ps", b1)], writes=["msq"])
        P.op("dve", lambda e: e.scalar_tensor_tensor(out=A_LRS[:, :N], in0=ps[b2][:, :N], scalar=1.0 / 512, in1=A_MSQ[:, :N],
                                                    op0=ALU.mult, op1=ALU.subtract),
             reads=[("ps", b2), "msq"], writes=["lrs"])
        P.op("act", lambda e: e.activation(out=A_MSQ[:, :N], in_=A_LRS[:, :N], func=AF.Ln, bias=EPS, scale=1.0),
             reads=["lrs"], writes=["msq"])
        P.op("act", lambda e: e.activation(out=A_LRS[:, :N], in_=A_MSQ[:, :N], func=AF.Exp, scale=-0.5), reads=["msq"], writes=["lrs"])
        tt = A_TT.rearrange("p (i n) -> p i n", i=2)
        for j in range(4):
            i = j % 2
            P.op("dve", lambda e, j=j, i=i: e.tensor_tensor(out=tt[:, i, :N], in0=cb[:, j, :N], in1=A_MEAN[:, :N], op=ALU.subtract),
                 reads=[("cb", j), "mean"], writes=[("tt", i)])
            P.op("dve", lambda e, j=j, i=i: e.tensor_tensor(out=tt[:, i, :N], in0=tt[:, i, :N], in1=A_LRS[:, :N], op=ALU.mult),
                 reads=[("tt", i), "lrs"], writes=[("tt", i)])
            P.op("act", lambda e, j=j, i=i: e.activation(out=yb[:, 4 + j, :N], in_=tt[:, i, :N], func=AF.Silu,
                                                       bias=ccvbt[:, pr, j:j + 1], scale=ccvg[:, pr, j:j + 1]),
                 reads=[("tt", i), "cst"], writes=[("y", 4 + j)])
        preload_sqrt()
        out_proj(t, "w_out_even", pr)

    def odd_mixer(t, l):
        pr = l // 2
        N, S, L = t.N, t.S, t.L
        win = w_in_odd[pr]
        cx = v4(O_CX, 4, S, 2 + L)
        hist_src = stsc[:, pr] if t.kind == "sample" else nsc[:, pr, :, 0:1, :]
        P.op("act", lambda e: e.activation(out=cx[:, :, :, 0:2], in_=hist_src, func=AF.Identity),
             reads=["rstd", "stsc", "nsc", "youtdma"], writes=[("cx", j) for j in range(4)])
        if not (t.kind == "sample" and l == 3):
            prebuild_start((pr + 1) % 2)
        su = load_piece([("w_in_odd", pr, 0, 1024, 0, 512, 0, 0)])
        sv = load_piece([("w_in_odd", pr, 0, 1024, 512, 512, 0, 0)])
        vn = O_VN.rearrange("p (b n) -> p b n", b=4)
        v32 = O_V32.rearrange("p (b n) -> p b n", b=4)
        vt = O_VT.rearrange("p (i n) -> p i n", i=2)
        bn = O_BN
        vbanks = [next_bank() for _ in t.vblocks]
        for k in range(8):
            def fnv(e, k=k):
                ins = None
                for bi, (c0, nb) in enumerate(t.vblocks):
                    ins = e.matmul(ps[vbanks[bi]][:nb, :], hb[:, k, c0:c0 + nb], ring[sv][:, k, :], start=(k == 0), stop=(k == 7))
                return ins
            P.op("pe", fnv, reads=[("h", k), ("slot", sv)], writes=[("ps", b) for b in vbanks] if k == 0 else [("psacc_v", 0)])
        for b in vbanks:
            P.lastw[("ps", b)] = ("op", "pe", len(P.ops["pe"]) - 1)
        for bi, (c0, nb) in enumerate(t.vblocks):
            b = vbanks[bi]
            i = bi % 2
            P.op("dve", lambda e, b=b, nb=nb: e.bn_stats(out=bn[:nb, 0:6], in_=ps[b][:nb, :]), reads=[("ps", b)], writes=["bn6"])
            P.op("dve", lambda e, nb=nb: e.bn_aggr(out=bn[:nb, 8:10], in_=bn[:nb, 0:6]), reads=["bn6"], writes=["bnmv"])
            P.op("act", lambda e, nb=nb: e.activation(out=bn[:nb, 10:11], in_=bn[:nb, 9:10], func=AF.Ln, bias=EPS, scale=1.0),
                 reads=["bnmv"], writes=["bnsd"])
            P.op("act", lambda e, nb=nb: e.activation(out=bn[:nb, 11:12], in_=bn[:nb, 10:11], func=AF.Exp, scale=-0.5),
                 reads=["bnsd"], writes=["bnrs"])
            P.op("dve", lambda e, b=b, nb=nb, i=i: e.tensor_scalar(out=vt[:nb, i, :], in0=ps[b][:nb, :], scalar1=bn[:nb, 8:9],
                                                                  scalar2=bn[:nb, 11:12], op0=ALU.subtract, op1=ALU.mult),
                 reads=[("ps", b), "bnmv", "bnrs"], writes=[("vt", i)])
            P.op("dve", lambda e, nb=nb, i=i: e.tensor_tensor(out=vt[:nb, i, :], in0=vt[:nb, i, :], in1=sgug[:nb, pr, :], op=ALU.mult),
                 reads=[("vt", i), "sgug"], writes=[("vt", i)])
            if t.kind == "sample":
                P.op("dve", lambda e, nb=nb, i=i, bi=bi: e.tensor_tensor(out=v32[:nb, bi, :], in0=vt[:nb, i, :], in1=sgubt[:nb, pr, :], op=ALU.add),
                     reads=[("vt", i), "sgubt"], writes=[("v32", bi)])
                P.op("act", lambda e, nb=nb, bi=bi: e.activation(out=vn[:nb, bi, :], in_=v32[:nb, bi, :], func=AF.Identity),
                     reads=[("v32", bi)], writes=[("vn", bi)])
                sem = dsem("ov")

                def fn(e, bi=bi, sem=sem, nb=nb):
                    e.dma_start(out=o_v[pr, bi], in_=v32[:nb, bi, :]).then_inc(dmasems[sem], 16)
                    return None
                P.op("sp", fn, reads=[("v32", bi)], dma_sem=sem)
            else:
                P.op("dve", lambda e, nb=nb, i=i, bi=bi: e.tensor_tensor(out=vn[:nb, bi, :], in0=vt[:nb, i, :], in1=sgubt[:nb, pr, :], op=ALU.add),
                     reads=[("vt", i), "sgubt"], writes=[("vn", bi)])
        u = O_U.rearrange("p (c n) -> p c n", c=4)
        for m in range(4):
            b = proj_group(su, m, t)
            P.op("act", lambda e, m=m, b=b: e.activation(out=u[:, m, :N], in_=ps[b][:, :N], func=AF.Identity),
                 reads=[("ps", b)], writes=[("u", m)])
            prebuild_some(4)
        sb_ = load_piece([("w_in_odd", pr, 0, 1024, 1024, 512, 0, 0)])
        sc_ = load_piece([("w_in_odd", pr, 0, 1024, 1536, 512, 0, 0)])
        sx_ = load_piece([("w_in_odd", pr, 0, 1024, 2048, 512, 0, 0)])
        xin = O_XIN.rearrange("p (i n) -> p i n", i=2)
        bgs = O_BG.rearrange("p (i n) -> p i n", i=2)
        cz = O_CZ.rearrange("p (i n) -> p i n", i=2)
        for j in range(4):
            i = j % 2
            bx = proj_group(sx_, j, t)
            bc = proj_group(sc_, j, t)
            bb = proj_group(sb_, j, t)
            P.op("act", lambda e, i=i, bx=bx: e.activation(out=xin[:, i, :N], in_=ps[bx][:, :N], func=AF.Identity),
                 reads=[("ps", bx)], writes=[("xin", i)])
            P.op("dve", lambda e, j=j, i=i, bc=bc: e.tensor_tensor(out=cx[:, j, :, 2:2 + L], in0=pv(bc, t), in1=v3(xin[:, i, :N], t), op=ALU.mult),
                 reads=[("ps", bc), ("xin", i)], writes=[("cx", j)])
            P.op("act", lambda e, i=i, bb=bb: e.activation(out=bgs[:, i, :N], in_=ps[bb][:, :N], func=AF.Identity),
                 reads=[("ps", bb)], writes=[("bgs", i)])
            prebuild_some(11)
            P.op("dve", lambda e, j=j, i=i: e.tensor_scalar(out=v3(cz[:, i, :N], t), in0=cx[:, j, :, 2:2 + L], scalar1=scw[:, pr, j, 2:3],
                                                          scalar2=None, op0=ALU.mult),
                 reads=[("cx", j), "cst"], writes=[("cz", i)])
            for k in (1, 0):
                P.op("dve", lambda e, j=j, i=i, k=k: e.scalar_tensor_tensor(out=v3(cz[:, i, :N], t), in0=cx[:, j, :, k:k + L],
                                                                           scalar=scw[:, pr, j, k:k + 1], in1=v3(cz[:, i, :N], t),
                                                                           op0=ALU.mult, op1=ALU.add),
                     reads=[("cx", j), ("cz", i), "cst"], writes=[("cz", i)])
            P.op("dve", lambda e, j=j, i=i: e.tensor_tensor(out=yb[:, 4 + j, :N], in0=cz[:, i, :N], in1=bgs[:, i, :N], op=ALU.mult),
                 reads=[("cz", i), ("bgs", i)], writes=[("y", 4 + j)])
        mt = O_MT.rearrange("p (i n) -> p i n", i=2)
        for j in range(4):
            b = next_bank()
            segs = []
            if t.kind == "sample":
                for q in range(NSS):
                    segs.append((q, 0, 64, q * 64, False))
            else:
                for bi, (c0, nb) in enumerate(t.vblocks):
                    segs.append((bi, 0, nb, c0, False))

            def fn(e, j=j, b=b, segs=segs):
                ins = None
                for (bi, po, ln, c0, smp) in segs:
                    rhs = wsTs[po:po + ln, pr * 4 + j, 0:ln] if smp else wsT[0:ln, pr * 4 + j, 0:ln]
                    ins = e.matmul(ps[b][:, c0:c0 + ln], vn[po:po + ln, bi, j * 128:(j + 1) * 128], rhs, start=True, stop=True)
                return ins
            P.op("pe", fn, reads=[("vn", bi) for bi in range(len(t.vblocks))] + ["wsT", "wsTs"], writes=[("ps", b)])
            i = j % 2
            if t.kind == "sample":
                groups = [(0, NSS, 64)]
            elif t.kind == "halo":
                groups = [(0, 1, 32), (32, 2, 128)]
            else:
                groups = [(0, 4, 128)]
            for gi, (c0, nblk, bl) in enumerate(groups):
                P.op("dve", lambda e, j=j, b=b, i=i, c0=c0, nblk=nblk, bl=bl: e.tensor_tensor(
                    out=mt[:, i, c0:c0 + nblk * bl].rearrange("p (a n) -> p a n", a=nblk),
                    in0=ps[b][:, c0:c0 + nblk * bl].rearrange("p (a n) -> p a n", a=nblk),
                    in1=sgubias[:, pr * 4 + j, 0:bl].unsqueeze(1).broadcast_to([128, nblk, bl]), op=ALU.add),
                    reads=[("ps", b), "sgubias"], writes=[("mt", i)] if gi == 0 else [("mt", i)])
            P.op("dve", lambda e, j=j, i=i: e.tensor_tensor(out=yb[:, j, :N], in0=mt[:, i, :N], in1=u[:, j, :N], op=ALU.mult),
                 reads=[("mt", i), ("u", j)], writes=[("y", j)])
        P.op("act", lambda e: e.activation(out=nsc[:, pr, :, seqsl(t), :], in_=cx[:, :, :, L:L + 2], func=AF.Identity),
             reads=[("cx", j) for j in range(4)], writes=["nsc"])
        prebuild_some(64)
        out_proj(t, "w_out_odd", pr, korder=(4, 5, 6, 7, 0, 1, 2, 3))

    def conv_ffn(t, l):
        N, S, L = t.N, t.S, t.L
        wup = ffn_w_up[l]
        wdn = ffn_w_down[l]
        gB = F_G.rearrange("p (c n) -> p c n", c=22)
        sbuf_ = F_S.rearrange("p (i n) -> p i n", i=2)
        ptb = F_PT.rearrange("p (i n) -> p i n", i=2)
        trim = (t.kind == "halo" and l == 3)
        d0 = dict(banks=[] if trim else [next_bank() for _ in range(4)])
        LAG = 2

        def down0_mini(u):
            banks = d0["banks"]
            kk, nk = 2 * u, 2
            s = load_piece([("ffn_w_down", l, kk * 128, nk * 128, 0, 512, 0, 0)])
            for k in range(nk):
                kg = kk + k

                def fn(e, s=s, k=k, kg=kg, banks=banks):
                    ins = None
                    for mm in range(4):
                        ins = e.matmul(ps[banks[mm]][:, :N], ring[s][:, k, mm * 128:(mm + 1) * 128], gB[:, kg, :N],
                                       start=(kg == 0), stop=(kg == 21))
                    return ins
                P.op("pe", fn, reads=[("g", kg), ("slot", s)], writes=[("ps", bq) for bq in banks] if kg == 0 else [("psacc", 0)])
            if u == NU - 1:
                for mm in range(4):
                    P.lastw[("ps", banks[mm])] = ("op", "pe", len(P.ops["pe"]) - 1)
        for ui in range(NU):
            s = load_piece([("ffn_w_up", l, 0, 1024, 256 * ui, 256, 0, 0), ("ffn_w_up", l, 0, 1024, DFF + 256 * ui, 256, 0, 256)])
            ub = ui % 2
            up = F_UP[:, ub * 2056: ub * 2056 + 4 * S * (2 + L)].rearrange("p (c s w) -> p c s w", c=4, s=S)
            hist_src = stffn[:, l, 4 * ui:4 * ui + 4] if t.kind == "sample" else nffn[:, l, 4 * ui:4 * ui + 4, 0:1, :]
            P.op("act", lambda e, up=up, hist_src=hist_src: e.activation(out=up[:, :, :, 0:2], in_=hist_src, func=AF.Identity),
                 reads=["rstd", "stffn", "nffn", "youtdma"], writes=[("up", ub, q) for q in range(4)])
            ubanks = proj_kouter(s, [0, 1, 2, 3], t) if ui == 0 else None
            for pp in range(2):
                cbuf = F_C[:, pp * 1024:(pp + 1) * 1024].rearrange("p (i n) -> p i n", i=2)
                banks = []
                for half in range(2):
                    q = half * 2 + pp
                    b = ubanks[q] if ui == 0 else proj_group(s, q, t)
                    banks.append(b)
                    ch = 4 * ui + q
                    P.op("act", lambda e, up=up, q=q, b=b: e.activation(out=up[:, q, :, 2:2 + L], in_=pv(b, t), func=AF.Identity),
                         reads=[("ps", b)], writes=[("up", ub, q)])
                    if trim:
                        continue
                    P.op("act", lambda e, cbuf=cbuf, half=half, b=b, ch=ch: e.activation(
                        out=cbuf[:, half, :N], in_=ps[b][:, :N], func=AF.Identity, bias=fcb[:, l, ch:ch + 1], scale=fcw[:, l, ch, 2:3]),
                        reads=[("ps", b), "cst"], writes=[("c", pp, half)])
                    P.op("dve", lambda e, cbuf=cbuf, half=half, up=up, q=q, ch=ch: e.scalar_tensor_tensor(
                        out=v3(cbuf[:, half, :N], t), in0=up[:, q, :, 1:1 + L], scalar=fcw[:, l, ch, 1:2],
                        in1=v3(cbuf[:, half, :N], t), op0=ALU.mult, op1=ALU.add),
                        reads=[("up", ub, q), ("c", pp, half), "cst"], writes=[("c", pp, half)])
                    P.op("dve", lambda e, cbuf=cbuf, half=half, up=up, q=q, ch=ch: e.scalar_tensor_tensor(
                        out=v3(cbuf[:, half, :N], t), in0=up[:, q, :, 0:L], scalar=fcw[:, l, ch, 0:1],
                        in1=v3(cbuf[:, half, :N], t), op0=ALU.mult, op1=ALU.add),
                        reads=[("up", ub, q), ("c", pp, half), "cst"], writes=[("c", pp, half)])
                if trim:
                    continue
                P.op("act", lambda e, cbuf=cbuf, pp=pp: e.activation(out=sbuf_[:, pp, :N], in_=cbuf[:, 0, :N], func=AF.Silu),
                     reads=[("c", pp, 0)], writes=[("s", pp)])
                if ui == NU - 1 and pp == 1:
                    preload_sqrt()
                gch = 2 * ui + pp
                P.op("dve", lambda e, cbuf=cbuf, pp=pp, gch=gch: e.tensor_tensor(out=gB[:, gch, :N], in0=sbuf_[:, pp, :N], in1=cbuf[:, 1, :N], op=ALU.mult),
                     reads=[("s", pp), ("c", pp, 1)], writes=[("g", gch)])
            P.op("act", lambda e, up=up, ui=ui: e.activation(out=nffn[:, l, 4 * ui:4 * ui + 4, seqsl(t), :], in_=up[:, :, :, L:L + 2], func=AF.Identity),
                 reads=[("up", ub, q) for q in range(4)], writes=["nffn"])
            if ui == 1 and not trim:
                reserved_banks.update(d0["banks"])
            if ui >= LAG and not trim:
                down0_mini(ui - LAG)
        if trim:
            return
        h1 = dict(banks=None, last={})
        H1P = ((0, 8), (8, 8), (16, 6))

        def half1_piece(pi):
            if h1["banks"] is None:
                h1["banks"] = [next_bank() for _ in range(4)]
                reserved_banks.update(h1["banks"])
            kk, nk = H1P[pi]
            s = load_piece([("ffn_w_down", l, kk * 128, nk * 128, 512, 512, 0, 0)])
            for mm in range(4):
                b = h1["banks"][mm]

                def fn(e, mm=mm, b=b, s=s, kk=kk, nk=nk):
                    ins = None
                    for k in range(nk):
                        kg = kk + k
                        ins = e.matmul(ps[b][:, :N], ring[s][:, k, mm * 128:(mm + 1) * 128], gB[:, kg, :N],
                                       start=(kg == 0), stop=(kg == 21))
                    return ins
                P.op("pe", fn, reads=[("g", kg) for kg in range(kk, kk + nk)] + [("slot", s)],
                     writes=[("ps", b)] if pi == 0 else [("psacc1", mm)])
                h1["last"][mm] = len(P.ops["pe"]) - 1

        down0_mini(NU - 2)
        half1_piece(0)
        half1_piece(1)
        down0_mini(NU - 1)
        banks = d0["banks"]
        reserved_banks.difference_update(banks)
        for mm in range(4):
            b = banks[mm]
            P.op("dve", lambda e, m=mm, b=b: e.tensor_tensor(out=xb[:, m, :N], in0=ps[b][:, :N], in1=xb[:, m, :N], op=ALU.add),
                 reads=[("ps", b), ("x", mm)], writes=[("x", mm)])
        half1_piece(2)
        banks = h1["banks"]
        reserved_banks.difference_update(banks)
        for mm in range(4):
            P.lastw[("ps", banks[mm])] = ("op", "pe", h1["last"][mm])
        for mm in range(4):
            m = 4 + mm
            b = banks[mm]
            P.op("dve", lambda e, m=m, b=b: e.tensor_tensor(out=xb[:, m, :N], in0=ps[b][:, :N], in1=xb[:, m, :N], op=ALU.add),
                 reads=[("ps", b), ("x", m)], writes=[("x", m)])

    first = True
    active = [t for ti, t in enumerate(tiles) if dbg_tiles is None or ti in dbg_tiles]

    def plan_xload(t):
        for c in range(8):
            sem = dsem(f"xin{c}")

            def fnx(e, t=t, sem=sem, c=c):
                e.dma_start(out=xb[:, c, :t.N], in_=xT[:, c, t.col0:t.col0 + t.N]).then_inc(dmasems[sem], 16)
                return None
            P.op("sp", fnx, writes=[("x", c)], dma_sem=sem)

    for ai, t in enumerate(active):
        N = t.N
        if ai == 0:
            plan_xload(t)
        for l in range(dbg_layers):
            rmsnorm(t, lambda c, l=l: gmix[:, l, c:c + 1], extra_reads=SETUP if first else (),
                    h_extra=["youtdma"] if l == 0 else ())
            first = False
            if dbg_stage < 1:
                continue
            if l % 2 == 0:
                even_mixer(t, l)
            else:
                odd_mixer(t, l)
            if dbg_stage < 2:
                continue
            rmsnorm(t, lambda c, l=l: gffn[:, l, c:c + 1])
            conv_ffn(t, l)
        if t.kind != "halo":
            rmsnorm(t, lambda c: gfin[:, c:c + 1], final=True)
        if ai + 1 < len(active):
            plan_xload(active[ai + 1])
        if t.kind != "halo":
            yo = YOUT.rearrange("p (c n) -> p c n", c=8)
            sem = dsem("yout")

            def fny(e, t=t, sem=sem):
                e.dma_start(out=yT[:, :, t.outcol:t.outcol + t.N], in_=yo[:, :, :t.N]).then_inc(dmasems[sem], 16)
                return None
            ev = P.op("sp", fny, reads=[("yout", c) for c in range(8)], dma_sem=sem)
            P.lastw["youtdma"] = ev

    sem = dsem("fin")
    for dst, srcb, res in ((o_pool, npool, "npool"), (o_ccv, nccv, "nccv"), (o_sc, nsc, "nsc"), (o_ffn, nffn, "nffn")):
        def fno(e, dst=dst, srcb=srcb, sem=sem):
            e.dma_start(out=dst, in_=srcb[:].rearrange("p a c s k -> p (a c s k)")).then_inc(dmasems[sem], 16)
            return None
        P.op("sp", fno, reads=[res], dma_sem=sem)

    finals = [("sem", n, P.dmasem_count[n]) for n in ("fin", "yout", "ov") if n in P.dmasem_count]


    if record:
        es.close()
        return pieces
    P.resolve()

    with nc.Block() as block:
        @block.tensor
        def _(e):
            P.emit("pe", e, engsem, dmasems)

        @block.scalar
        def _(e):
            P.emit("act", e, engsem, dmasems)

        @block.vector
        def _(e):
            P.emit("dve", e, engsem, dmasems)

        @block.gpsimd
        def _(e):
            P.emit("pool", e, engsem, dmasems)

        @block.sync
        def _(e):
            P.emit("sp", e, engsem, dmasems)
            for (_, n, v) in finals:
                e.wait_ge(dmasems[n], v)
    es.close()
    return nc


_NC_CACHE = {}


def _fm(a):
    a = np.asarray(a)
    C = a.shape[-1]
    T_ = a.shape[-2]
    lead = a.shape[:-2]
    b = a.reshape(lead + (T_, C // 128, 128))
    nd = b.ndim
    perm = (nd - 1,) + tuple(range(len(lead))) + (nd - 2, nd - 3)
    return np.ascontiguousarray(b.transpose(perm))


def kernel(x_prompt, x_sample, state_pool, state_ccv, state_sconv, state_ffn_conv,
           norm_mix_g, norm_ffn_g, norm_final_g,
           w_in_even, pool_w, pool_scale, ccv_w, ccv_b, ccv_ln_g, ccv_ln_b, w_out_even,
           w_in_odd, sgu_ln_g, sgu_ln_b, sgu_ws, sgu_b, sconv_w, w_out_odd,
           ffn_w_up, ffn_conv_w, ffn_conv_b, ffn_w_down):
    f32 = np.float32
    A = lambda a: np.ascontiguousarray(np.asarray(a, dtype=f32))
    x_prompt = A(x_prompt); x_sample = A(x_sample)
    perm = _unit_perm()

    def vec_fm(v):
        v = A(v)
        lead = v.shape[:-1]
        b = v.reshape(lead + (v.shape[-1] // 128, 128))
        nd = b.ndim
        return np.ascontiguousarray(b.transpose((nd - 1,) + tuple(range(nd - 1))))

    cst = np.zeros((128, CSTN), f32)

    def put(name, arr):
        o, s = CST[name]
        cst[:, o:o + s] = arr.reshape(128, s)
    put("gmix", vec_fm(norm_mix_g))
    put("gffn", vec_fm(norm_ffn_g))
    put("gfin", vec_fm(norm_final_g))
    put("pscale", vec_fm(pool_scale))
    put("ccvw", vec_fm(ccv_w).transpose(0, 1, 3, 2))
    put("ccvb", vec_fm(ccv_b))
    put("ccvg", vec_fm(ccv_ln_g))
    put("ccvbt", vec_fm(ccv_ln_b))
    put("scw", vec_fm(sconv_w).transpose(0, 1, 3, 2))
    fw = vec_fm(ffn_conv_w)[:, :, :, perm]
    put("fcw", fw.transpose(0, 1, 3, 2))
    put("fcb", vec_fm(ffn_conv_b)[:, :, perm])

    ws = A(sgu_ws)
    wsT = np.ascontiguousarray(ws.transpose(3, 0, 1, 2)).reshape(128, 2 * 4 * 128)
    wsTs = np.ascontiguousarray(ws[:, :, :64, :64].transpose(3, 0, 1, 2))
    wsTs = np.concatenate([wsTs, wsTs], axis=0).reshape(128, 2 * 4 * 64)
    tril = (np.arange(128)[None, :] >= np.arange(128)[:, None]).astype(f32)
    trils = np.concatenate([tril[:64, :64], tril[:64, :64]], axis=0)
    poolw = np.ascontiguousarray(A(pool_w).transpose(2, 0, 1, 3)).reshape(128, 2 * 4 * 128)

    shared = dict(
        cst=cst, sgug=A(sgu_ln_g).reshape(-1), sgubt=A(sgu_ln_b).reshape(-1), sgubias=A(sgu_b).reshape(-1),
        wsT=wsT, wsTs=wsTs, tril=tril, ident=np.eye(128, dtype=f32), trils=np.ascontiguousarray(trils), poolw=poolw,
        w_in_even=A(w_in_even), w_out_even=A(w_out_even), w_in_odd=A(w_in_odd), w_out_odd=A(w_out_odd),
        ffn_w_up=A(ffn_w_up), ffn_w_down=A(ffn_w_down),
    )

    xpT = _fm(x_prompt[0])
    xsT = _fm(x_sample)
    sp_fm = _fm(A(state_pool))
    scv_fm = _fm(A(state_ccv))
    ssc_fm = _fm(A(state_sconv))
    sff_fm = _fm(A(state_ffn_conv))[:, :, :, perm, :]

    in_maps = []
    for c in range(NCORES):
        a = c * OWN
        xT = np.zeros((128, 8, TOK), f32)
        if c > 0:
            xT[:, :, 0:HALO] = xpT[:, :, a - HALO:a]
        xT[:, :, HALO:HALO + OWN] = xpT[:, :, a:a + OWN]
        xs = xsT[:, NSS * c:NSS * c + NSS]
        xT[:, :, HALO + OWN:] = xs.transpose(0, 2, 1, 3).reshape(128, 8, NSS * LS)
        mask = np.full((HALO,), 0.0 if c == 0 else 1.0, f32)
        invc = np.zeros((4, 16), f32)
        for g, w in enumerate((2, 4, 8, 16)):
            if c == 0:
                invc[g] = 1.0 / np.minimum(np.arange(16) + 1, w)
            else:
                invc[g] = 1.0 / w
        sl = slice(NSS * c, NSS * c + NSS)
        m = dict(shared)
        m.update(
            xT=xT, mask=mask, invc=invc.reshape(-1),
            st_pool=np.ascontiguousarray(sp_fm[:, :, sl].transpose(0, 1, 3, 2, 4)).reshape(128, -1),
            st_ccv=np.ascontiguousarray(scv_fm[:, :, sl].transpose(0, 1, 3, 2, 4)).reshape(128, -1),
            st_sc=np.ascontiguousarray(ssc_fm[:, :, sl].transpose(0, 1, 3, 2, 4)).reshape(128, -1),
            st_ffn=np.ascontiguousarray(sff_fm[:, :, sl].transpose(0, 1, 3, 2, 4)).reshape(128, -1),
        )
        in_maps.append(m)

    if "nc" not in _NC_CACHE:
        _NC_CACHE["nc"] = build_program()
    nc = _NC_CACHE["nc"]
    res = run_bass_kernel_spmd(nc, in_maps, core_ids=list(range(NCORES)))
    R = res.results

    def tm(a):
        return np.ascontiguousarray(a.transpose(2, 1, 0).reshape(a.shape[2], -1))

    y_prompt = np.zeros((1, SEQ, D), f32)
    y_sample = np.zeros((32, LS, D), f32)
    for c in range(NCORES):
        yT = R[c]["yT"]
        y_prompt[0, c * OWN:(c + 1) * OWN] = tm(yT[:, :, :OWN])
        ys = yT[:, :, OWN:].reshape(128, 8, NSS, LS)
        for q in range(NSS):
            y_sample[NSS * c + q] = tm(ys[:, :, q, :])

    def states(key, npair, C, H, permute=None):
        pp = np.zeros((npair, 1, H, C * 128), f32)
        ss = np.zeros((npair, 32, H, C * 128), f32)
        for c in range(NCORES):
            a = R[c][key].reshape(128, npair, C, 5, H)
            if permute is not None:
                b = np.zeros_like(a)
                b[:, :, permute] = a
                a = b
            full = a.transpose(1, 3, 4, 2, 0).reshape(npair, 5, H, C * 128)
            if c == NCORES - 1:
                pp[:, 0] = full[:, 0]
            ss[:, NSS * c:NSS * c + NSS] = full[:, 1:5]
        return pp, ss

    pool_p, pool_s = states("o_pool", 2, 4, 15)
    ccv_p, ccv_s = states("o_ccv", 2, 4, 30)
    sc_p, sc_s = states("o_sc", 2, 4, 2)
    ffn_p, ffn_s = states("o_ffn", 4, 44, 2, permute=perm)
    v_s = np.zeros((2, 32, LS, 512), f32)
    for c in range(NCORES):
        v_s[:, NSS * c:NSS * c + NSS] = R[c]["o_v"]
    return (y_prompt, y_sample, pool_p, pool_s, ccv_p, ccv_s, sc_p, sc_s, v_s, ffn_p, ffn_s)
```
